# Optimizing a Trainium2 kernel written in Bass

```python
import math
import jax, jax.numpy as jnp
from jax import lax
import numpy as np

D_MODEL = 2048
BATCH = 1
SEQ = 16384
DEPTH = 4

GRID_W = 64
CTX_LEN = 256
N_GROUPS = 4
GROUP_WIDTH = D_MODEL // N_GROUPS
D_MIX = N_GROUPS * GROUP_WIDTH
HEAD_DIM = 64
QBLOCK = 128
A_DH = HEAD_DIM
A_HEADS = GROUP_WIDTH // (2 * A_DH)
B_GDIM = 128
B_GROUPS = GROUP_WIDTH // B_GDIM
CHUNK = 128
C_DH = HEAD_DIM
C_HEADS = GROUP_WIDTH // C_DH
C_KV_HEADS = C_HEADS // 4
WINDOW = 128
D_DH = HEAD_DIM
D_HEADS = GROUP_WIDTH // D_DH
NA_ROWS = 8
NA_COLS = 16
D_FF = 256 * math.ceil(8 * D_MODEL / 3 / 256)
N_MOD = 9
ROPE_THETA = 10000.0
LN_EPS = 1e-6
NEG_INF = -1e30
DEEPNORM_ALPHA = (2 * DEPTH) ** 0.25
DEEPNORM_BETA = (8 * DEPTH) ** -0.25
PROJ_SPLITS = (GROUP_WIDTH, GROUP_WIDTH, GROUP_WIDTH,
               GROUP_WIDTH, GROUP_WIDTH,
               GROUP_WIDTH, C_KV_HEADS * C_DH, C_KV_HEADS * C_DH,
               GROUP_WIDTH, GROUP_WIDTH, GROUP_WIDTH)
D_PROJ = sum(PROJ_SPLITS)

kernel_name = 'hybrid_parallel_group_flow_block'


def layer_norm(t, g, b):
    tf = t.astype(jnp.float32)
    tc = tf - jnp.mean(tf, axis=-1, keepdims=True)
    var = jnp.mean(tc * tc, axis=-1, keepdims=True)
    return (tc * lax.rsqrt(var + LN_EPS) * g.astype(jnp.float32) + b.astype(jnp.float32)).astype(t.dtype)


def rms_norm(t):
    tf = t.astype(jnp.float32)
    return (tf * lax.rsqrt(jnp.mean(tf * tf, axis=-1, keepdims=True) + LN_EPS)).astype(t.dtype)


def modulate(t, shift, scale):
    return t * (1.0 + scale) + shift


def swiglu(h, w_in, w_out):
    gate, up = jnp.split(h @ w_in, 2, axis=-1)
    return (jax.nn.silu(gate) * up) @ w_out


def axial_rope_tables(n_tokens, head_dim):
    t = jnp.arange(n_tokens, dtype=jnp.int32)
    row = (t // GRID_W).astype(jnp.float32)
    col = (t % GRID_W).astype(jnp.float32)
    n_freq = head_dim // 4
    inv = ROPE_THETA ** (-jnp.arange(n_freq, dtype=jnp.float32) / n_freq)
    ang = jnp.concatenate([row[:, None] * inv, col[:, None] * inv], axis=-1)
    return jnp.cos(ang), jnp.sin(ang)


def apply_rope(t, cos, sin):
    half = t.shape[-1] // 2
    shape = (1, cos.shape[0]) + (1,) * (t.ndim - 3) + (half,)
    cs, sn = cos.reshape(shape), sin.reshape(shape)
    t1 = t[..., :half].astype(jnp.float32)
    t2 = t[..., half:].astype(jnp.float32)
    return jnp.concatenate([t1 * cs - t2 * sn, t1 * sn + t2 * cs], axis=-1).astype(t.dtype)


def diff_attention(q, k, v, lam):
    s = jnp.einsum('bqhcd,bkhcd->bhcqk', q, k).astype(jnp.float32) * (A_DH ** -0.5)
    p = jax.nn.softmax(s, axis=-1)
    a = p[:, :, 0] - lam * p[:, :, 1]
    return jnp.einsum('bhqk,bkhe->bqhe', a.astype(v.dtype), v)


def diff_attention_latent(q, k, v, kc, vc, lam):
    Bn, N = q.shape[:2]
    nb = N // QBLOCK
    k_all = jnp.concatenate([kc, k], axis=1)
    v_all = jnp.concatenate([vc, v], axis=1)
    qb = jnp.moveaxis(q.reshape((Bn, nb, QBLOCK) + q.shape[2:]), 1, 0)
    ob = lax.map(lambda qi: diff_attention(qi, k_all, v_all, lam), qb)
    return jnp.moveaxis(ob, 0, 1).reshape((Bn, N) + v.shape[2:])


def spatial_gating(u, v, gn_g, gn_b, w_s, b_s):
    Bn, N = u.shape[:2]
    u = jax.nn.gelu(u)
    v = layer_norm(jax.nn.gelu(v), gn_g, gn_b)
    vch = v.reshape(Bn, N // CHUNK, CHUNK, B_GROUPS, B_GDIM)
    mixed = jnp.einsum('gpq,bnqgc->bnpgc', w_s, vch) + jnp.swapaxes(b_s, 0, 1)[None, None, :, :, None]
    return u * mixed.reshape(Bn, N, GROUP_WIDTH)


def sink_softmax(sink, s):
    grp = C_HEADS // C_KV_HEADS
    sb = jnp.broadcast_to(sink.astype(jnp.float32).reshape(C_KV_HEADS, grp, 1, 1), s.shape[:-1] + (1,))
    return jax.nn.softmax(jnp.concatenate([sb, s], axis=-1), axis=-1)[..., 1:]


def window_attention_latent(q, k, v, kc, vc, sink):
    Bn, N = q.shape[:2]
    L = kc.shape[1]
    nb = N // QBLOCK
    grp = C_HEADS // C_KV_HEADS
    qb = q.reshape(Bn, nb, QBLOCK, C_KV_HEADS, grp, C_DH)

    def band(t):
        tb = t.reshape(Bn, nb, QBLOCK, C_KV_HEADS, C_DH)
        tp = jnp.pad(tb, ((0, 0), (1, 1), (0, 0), (0, 0), (0, 0)))
        return jnp.concatenate([tp[:, :-2], tp[:, 1:-1], tp[:, 2:]], axis=2)

    kb, vb = band(k), band(v)
    scale = C_DH ** -0.5
    s_loc = jnp.einsum('bnqkgd,bnskd->bnkgqs', qb, kb).astype(jnp.float32) * scale
    s_ctx = jnp.einsum('bnqkgd,bskd->bnkgqs', qb, kc).astype(jnp.float32) * scale
    blk = jnp.arange(nb)[:, None, None]
    qpos = blk * QBLOCK + jnp.arange(QBLOCK)[None, :, None]
    kpos = (blk - 1) * QBLOCK + jnp.arange(3 * QBLOCK)[None, None, :]
    valid = (jnp.abs(kpos - qpos) <= WINDOW) & (kpos >= 0) & (kpos < N)
    s_loc = jnp.where(valid[None, :, None, None], s_loc, NEG_INF)
    p = sink_softmax(sink, jnp.concatenate([s_ctx, s_loc], axis=-1))
    p_ctx = p[..., :L].astype(v.dtype)
    p_loc = p[..., L:].astype(v.dtype)
    o = (jnp.einsum('bnkgqs,bskd->bnqkgd', p_ctx, vc)
         + jnp.einsum('bnkgqs,bnskd->bnqkgd', p_loc, vb))
    return o.reshape(Bn, N, C_HEADS * C_DH)


def sink_attention_ctx(qc, kc, vc, sink):
    Bn, L = qc.shape[:2]
    qg = qc.reshape(Bn, L, C_KV_HEADS, C_HEADS // C_KV_HEADS, C_DH)
    s = jnp.einsum('bqkgd,bskd->bkgqs', qg, kc).astype(jnp.float32) * (C_DH ** -0.5)
    p = sink_softmax(sink, s).astype(vc.dtype)
    return jnp.einsum('bkgqs,bskd->bqkgd', p, vc).reshape(Bn, L, C_HEADS * C_DH)


def neighbourhood_attention_latent(q, k, v, kc, vc, rpb):
    Bn, N = q.shape[:2]
    L = kc.shape[1]
    R = N // GRID_W
    kr = min(NA_ROWS, R)
    qg = q.reshape(Bn, R, GRID_W, D_HEADS, D_DH)
    kg = k.reshape(Bn, R, GRID_W, D_HEADS, D_DH)
    vg = v.reshape(Bn, R, GRID_W, D_HEADS, D_DH)
    r = jnp.arange(R)
    rs = jnp.clip(r - kr // 2, 0, R - kr)
    rows = rs[:, None] + jnp.arange(kr)[None, :]
    kn = kg[:, rows]
    vn = vg[:, rows]
    scale = D_DH ** -0.5
    s_nb = jnp.einsum('brqhd,brjkhd->brhqjk', qg, kn).astype(jnp.float32) * scale
    cq = jnp.arange(GRID_W)
    cs = jnp.clip(cq - NA_COLS // 2, 0, GRID_W - NA_COLS)
    col_ok = (cq[None, :] >= cs[:, None]) & (cq[None, :] < cs[:, None] + NA_COLS)
    dr = rows - r[:, None] + (NA_ROWS - 1)
    dc = jnp.clip(cq[None, :] - cq[:, None], -(NA_COLS - 1), NA_COLS - 1) + (NA_COLS - 1)
    bias = rpb[:, dr[:, None, :, None], dc[None, :, None, :]]
    bias = jnp.moveaxis(bias, 0, 1).astype(jnp.float32)
    s_nb = jnp.where(col_ok[:, None, :], s_nb + bias[None], NEG_INF)
    s_nb = s_nb.reshape(Bn, R, D_HEADS, GRID_W, kr * GRID_W)
    s_ctx = jnp.einsum('brqhd,bshd->brhqs', qg, kc).astype(jnp.float32) * scale
    p = jax.nn.softmax(jnp.concatenate([s_ctx, s_nb], axis=-1), axis=-1)
    p_ctx = p[..., :L].astype(v.dtype)
    p_nb = p[..., L:].reshape(Bn, R, D_HEADS, GRID_W, kr, GRID_W).astype(v.dtype)
    o = (jnp.einsum('brhqs,bshd->brqhd', p_ctx, vc)
         + jnp.einsum('brhqjk,brjkhd->brqhd', p_nb, vn))
    return o.reshape(Bn, N, D_HEADS * D_DH)


def ctx_attention(qc, kc, vc):
    Bn, L = qc.shape[:2]
    s = jnp.einsum('bqhd,bshd->bhqs', qc, kc).astype(jnp.float32) * (D_DH ** -0.5)
    p = jax.nn.softmax(s, axis=-1).astype(vc.dtype)
    return jnp.einsum('bhqs,bshd->bqhd', p, vc).reshape(Bn, L, D_HEADS * D_DH)


def token_mixer(h, hc, w_in, w_out, lam_vecs, lam_init, gn_g, gn_b, w_s, b_s, sink, rpb, cos, sin, need_ctx):
    Bn, N, _ = h.shape
    L = hc.shape[1]
    cuts = np.cumsum(PROJ_SPLITS)[:-1].tolist()
    aq, ak, av, bu, bv, cq, ck, cv, dq, dk, dv = jnp.split(h @ w_in, cuts, axis=-1)
    aqc, akc, avc, buc, bvc, cqc, ckc, cvc, dqc, dkc, dvc = jnp.split(hc @ w_in, cuts, axis=-1)

    lv = lam_vecs.astype(jnp.float32)
    lam = jnp.exp(jnp.sum(lv[0] * lv[1])) - jnp.exp(jnp.sum(lv[2] * lv[3])) + lam_init
    qa = apply_rope(aq.reshape(Bn, N, A_HEADS, 2, A_DH), cos, sin)
    ka = apply_rope(ak.reshape(Bn, N, A_HEADS, 2, A_DH), cos, sin)
    va = av.reshape(Bn, N, A_HEADS, 2 * A_DH)
    kac = akc.reshape(Bn, L, A_HEADS, 2, A_DH)
    vac = avc.reshape(Bn, L, A_HEADS, 2 * A_DH)
    ya = diff_attention_latent(qa, ka, va, kac, vac, lam)
    ya = (rms_norm(ya) * (1.0 - lam_init)).reshape(Bn, N, GROUP_WIDTH)

    yb = spatial_gating(bu, bv, gn_g, gn_b, w_s, b_s)

    qcq = apply_rope(cq.reshape(Bn, N, C_HEADS, C_DH), cos, sin)
    kcq = apply_rope(ck.reshape(Bn, N, C_KV_HEADS, C_DH), cos, sin)
    vcq = cv.reshape(Bn, N, C_KV_HEADS, C_DH)
    kcc = ckc.reshape(Bn, L, C_KV_HEADS, C_DH)
    vcc = cvc.reshape(Bn, L, C_KV_HEADS, C_DH)
    yc = window_attention_latent(qcq, kcq, vcq, kcc, vcc, sink)

    kdc = dkc.reshape(Bn, L, D_HEADS, D_DH)
    vdc = dvc.reshape(Bn, L, D_HEADS, D_DH)
    yd = neighbourhood_attention_latent(dq.reshape(Bn, N, D_HEADS, D_DH), dk.reshape(Bn, N, D_HEADS, D_DH),
                                        dv.reshape(Bn, N, D_HEADS, D_DH), kdc, vdc, rpb)

    y = jnp.concatenate([ya, yb, yc, yd], axis=-1) @ w_out
    if not need_ctx:
        return y, None

    ya_c = diff_attention(aqc.reshape(Bn, L, A_HEADS, 2, A_DH), kac, vac, lam)
    ya_c = (rms_norm(ya_c) * (1.0 - lam_init)).reshape(Bn, L, GROUP_WIDTH)
    yb_c = spatial_gating(buc, bvc, gn_g, gn_b, w_s, b_s)
    yc_c = sink_attention_ctx(cqc.reshape(Bn, L, C_HEADS, C_DH), kcc, vcc, sink)
    yd_c = ctx_attention(dqc.reshape(Bn, L, D_HEADS, D_DH), kdc, vdc)
    y_ctx = jnp.concatenate([ya_c, yb_c, yc_c, yd_c], axis=-1) @ w_out
    return y, y_ctx


def setup_inputs(seed: int = 0) -> dict:
    key = jax.random.key(seed)
    ks = jax.random.split(key, 24)

    def nrm(k, shape, scale):
        return jax.random.normal(k, shape, jnp.float32) * scale

    return {
        'x': nrm(ks[0], (BATCH, SEQ, D_MODEL), 1.0),
        'c': nrm(ks[1], (BATCH, D_MODEL), 1.0),
        'ctx': nrm(ks[2], (BATCH, CTX_LEN, D_MODEL), 1.0),
        'c_ctx': nrm(ks[3], (D_MODEL,), 1.0),
        'w_mod': nrm(ks[4], (DEPTH, D_MODEL, N_MOD * D_MODEL), D_MODEL ** -0.5),
        'b_mod': nrm(ks[5], (DEPTH, N_MOD * D_MODEL), 0.02),
        'ln_g': 1.0 + nrm(ks[6], (DEPTH, 3, D_MODEL), 0.02),
        'ln_b': nrm(ks[7], (DEPTH, 3, D_MODEL), 0.02),
        'ffn1_w_in': nrm(ks[8], (DEPTH, D_MODEL, 2 * D_FF), D_MODEL ** -0.5),
        'ffn1_w_out': nrm(ks[9], (DEPTH, D_FF, D_MODEL), D_FF ** -0.5 * DEEPNORM_BETA),
        'ffn2_w_in': nrm(ks[10], (DEPTH, D_MODEL, 2 * D_FF), D_MODEL ** -0.5),
        'ffn2_w_out': nrm(ks[11], (DEPTH, D_FF, D_MODEL), D_FF ** -0.5 * DEEPNORM_BETA),
        'mix_w_in': nrm(ks[12], (DEPTH, D_MODEL, D_PROJ), D_MODEL ** -0.5),
        'mix_w_out': nrm(ks[13], (DEPTH, D_MIX, D_MODEL), D_MIX ** -0.5 * DEEPNORM_BETA),
        'a_lambda': nrm(ks[14], (DEPTH, 4, A_DH), 0.1),
        'b_norm_g': 1.0 + nrm(ks[15], (DEPTH, GROUP_WIDTH), 0.02),
        'b_norm_b': nrm(ks[16], (DEPTH, GROUP_WIDTH), 0.02),
        'b_spatial_w': nrm(ks[17], (DEPTH, B_GROUPS, CHUNK, CHUNK), CHUNK ** -0.5),
        'b_spatial_b': 1.0 + nrm(ks[18], (DEPTH, B_GROUPS, CHUNK), 0.02),
        'c_sink': nrm(ks[19], (DEPTH, C_HEADS), 0.5),
        'd_rpb': nrm(ks[20], (DEPTH, D_HEADS, 2 * NA_ROWS - 1, 2 * NA_COLS - 1), 0.1),
    }


def reference(x, c, ctx, c_ctx, w_mod, b_mod, ln_g, ln_b, ffn1_w_in, ffn1_w_out, ffn2_w_in, ffn2_w_out,
              mix_w_in, mix_w_out, a_lambda, b_norm_g, b_norm_b, b_spatial_w, b_spatial_b, c_sink, d_rpb):
    N = x.shape[1]
    cos, sin = axial_rope_tables(N, HEAD_DIM)
    xc = ctx
    for l in range(DEPTH):
        last = l == DEPTH - 1
        lam_init = 0.8 - 0.6 * math.exp(-0.3 * l)
        mx = jnp.split((jax.nn.silu(c) @ w_mod[l] + b_mod[l])[:, None, :], N_MOD, axis=-1)
        mc = jnp.split((jax.nn.silu(c_ctx) @ w_mod[l] + b_mod[l])[None, None, :], N_MOD, axis=-1)

        x = layer_norm(DEEPNORM_ALPHA * x + 0.5 * mx[2] * swiglu(modulate(x, mx[0], mx[1]), ffn1_w_in[l], ffn1_w_out[l]),
                       ln_g[l, 0], ln_b[l, 0])
        xc = layer_norm(DEEPNORM_ALPHA * xc + 0.5 * mc[2] * swiglu(modulate(xc, mc[0], mc[1]), ffn1_w_in[l], ffn1_w_out[l]),
                        ln_g[l, 0], ln_b[l, 0])

        y, y_ctx = token_mixer(modulate(x, mx[3], mx[4]), modulate(xc, mc[3], mc[4]), mix_w_in[l], mix_w_out[l],
                               a_lambda[l], lam_init, b_norm_g[l], b_norm_b[l], b_spatial_w[l], b_spatial_b[l],
                               c_sink[l], d_rpb[l], cos, sin, not last)
        x = layer_norm(DEEPNORM_ALPHA * x + mx[5] * y, ln_g[l, 1], ln_b[l, 1])

        x = layer_norm(DEEPNORM_ALPHA * x + 0.5 * mx[8] * swiglu(modulate(x, mx[6], mx[7]), ffn2_w_in[l], ffn2_w_out[l]),
                       ln_g[l, 2], ln_b[l, 2])
        if not last:
            xc = layer_norm(DEEPNORM_ALPHA * xc + mc[5] * y_ctx, ln_g[l, 1], ln_b[l, 1])
            xc = layer_norm(DEEPNORM_ALPHA * xc + 0.5 * mc[8] * swiglu(modulate(xc, mc[6], mc[7]), ffn2_w_in[l], ffn2_w_out[l]),
                            ln_g[l, 2], ln_b[l, 2])
    return x
```

```python
import numpy as np
import concourse.bass as bass
import concourse.mybir as mybir
from concourse.bass_utils import run_bass_kernel_spmd
from contextlib import ExitStack

F32 = mybir.dt.float32
BF16 = mybir.dt.bfloat16
AF = mybir.ActivationFunctionType
ALU = mybir.AluOpType


class Buf:
    def __init__(self, name, t=None):
        self.name = name
        self.t = t
        self.w = {}
        self.r = {}
        self.dsem = None
        self.dcnt = 0


class KB:
    def __init__(self, nc, es):
        self.nc, self.es = nc, es
        self.e = {"pe": nc.tensor, "act": nc.scalar, "dve": nc.vector, "pool": nc.gpsimd, "sp": nc.sync}
        self.sem = {k: es.enter_context(nc.semaphore("s_" + k)) for k in ("pe", "act", "dve", "pool")}
        self.cnt = {k: 0 for k in self.sem}
        self.seen = {q: {} for q in self.e}
        self.finals = {}
        self.dbufs = {}
        self.pend = []
        self.pe_pend = []
        self.attach = False
        self.nbuf = 0

    def sbuf(self, name, shape, dtype, es=None):
        t = (es or self.es).enter_context(self.nc.sbuf_tensor(name, shape, dtype))
        return Buf(name, t)

    def psum(self, name, shape, dtype=F32):
        t = self.es.enter_context(self.nc.psum_tensor(name, shape, dtype))
        return Buf(name, t)

    def _wait(self, q, key, sem, val):
        if key == "pe" and q == "pe":
            return
        if key in self.dbufs:
            val = self.dbufs[key].dcnt
        if self.seen[q].get(key, 0) >= val:
            return
        self.pend.append((q, sem, val))
        self.seen[q][key] = val

    def _flush(self, q, ins=None):
        pend, self.pend = self.pend, []
        if ins is not None and pend and self.attach:
            last = pend.pop()
        else:
            last = None
        for (qq, sem, val) in pend:
            self.e[qq].wait_ge(sem, val)
        return last

    def _pre(self, q, reads, writes):
        for b in reads:
            for k, (s, v) in b.w.items():
                self._wait(q, k, s, v)
        for w in writes:
            b, part = (w if isinstance(w, tuple) else (w, False))
            if part and not b.r:
                continue
            for k, (s, v) in b.r.items():
                self._wait(q, k, s, v)
            for k, (s, v) in b.w.items():
                self._wait(q, k, s, v)

    def _post(self, key, sem, val, reads, writes):
        for w in writes:
            b, part = (w if isinstance(w, tuple) else (w, False))
            if (not part) or b.r:
                b.w = {}
                b.r = {}
            b.w[key] = (sem, val)
        for b in reads:
            if any((w[0] if isinstance(w, tuple) else w) is b for w in writes):
                continue
            b.r[key] = (sem, val)

    def op(self, q, fn, reads=(), writes=()):
        self._pre(q, reads, writes)
        self._flush(q)
        ins = fn(self.e[q])
        self.cnt[q] += 1
        ins.then_inc(self.sem[q], 1)
        self._post(q, self.sem[q], self.cnt[q], reads, writes)
        return ins

    def mm_group(self, out_ap, pairs, reads, out_buf, part=False):
        wr = [(out_buf, part)]
        self._pre("pe", reads, wr)
        self._flush("pe")
        n = len(pairs)
        ins = None
        for i, (l, r) in enumerate(pairs):
            ins = self.nc.tensor.matmul(out_ap, lhsT=l, rhs=r, start=(i == 0), stop=(i == n - 1))
        self.cnt["pe"] += 1
        ins.then_inc(self.sem["pe"], 1)
        self._post("pe", self.sem["pe"], self.cnt["pe"], reads, wr)

    def mm(self, out_ap, lhsT, rhs, start, stop, reads, out_buf, inc=True):
        wr = [(out_buf, True)]
        self._pre("pe", reads, wr)
        self._flush("pe")
        ins = self.nc.tensor.matmul(out_ap, lhsT=lhsT, rhs=rhs, start=start, stop=stop)
        self.pe_pend.append((reads, wr))
        if inc:
            self.cnt["pe"] += 1
            ins.then_inc(self.sem["pe"], 1)
            for (r, w) in self.pe_pend:
                self._post("pe", self.sem["pe"], self.cnt["pe"], r, w)
            self.pe_pend = []
        return ins

    def barrier(self):
        assert not self.pe_pend
        for q in self.e:
            for k in self.sem:
                if self.cnt[k]:
                    self._wait(q, k, self.sem[k], self.cnt[k])
            for k, b in self.dbufs.items():
                self._wait(q, k, b.dsem, b.dcnt)
            self._flush(q)

    def dma(self, q, out_ap, in_ap, sb, reads=(), writes=(), final=False):
        self._pre(q, reads, writes)
        self._flush(q)
        if sb.dsem is None:
            self.nbuf += 1
            sb.dsem = self.es.enter_context(self.nc.semaphore("d%d_%s" % (self.nbuf, sb.name)))
        ins = self.e[q].dma_start(out=out_ap, in_=in_ap)
        sb.dcnt += 16
        ins.then_inc(sb.dsem, 16)
        key = "d_" + sb.name
        self.dbufs[key] = sb
        self._post(key, sb.dsem, sb.dcnt, reads, writes)
        if final:
            self.finals[key] = (sb.dsem, sb.dcnt)
        return ins

    def finish(self):
        for k, (s, v) in self.finals.items():
            self._wait("sp", k, s, v)
        self._flush("sp")


D = 2048
DFF = 5632
NT_TILES = 18
NTOK = NT_TILES * 128
ALPHA = 8 ** 0.25
LN_EPS = 1e-6
GROUPS = [(0, 4), (4, 4), (8, 4), (12, 4), (16, 2)]


def ln_epilogue(kb, xt, xb, stats, mv, rs, gbc, bbc, gb):
    nc = kb.nc
    for q4 in range(4):
        kb.op("dve", lambda e, q4=q4: e.bn_stats(out=stats.t[:, q4, :], in_=xt[:, q4 * 512:(q4 + 1) * 512]),
              reads=[xb], writes=[(stats, True)])
    kb.op("dve", lambda e: e.bn_aggr(out=mv.t[:], in_=stats.t[:].rearrange("p a b -> p (a b)")), reads=[stats], writes=[mv])
    kb.op("act", lambda e: e.activation(out=rs.t[:, 0:1], in_=mv.t[:, 1:2], func=AF.Sqrt, bias=LN_EPS, scale=1.0),
          reads=[mv], writes=[rs])
    kb.op("dve", lambda e: e.reciprocal(out=rs.t[:, 0:1], in_=rs.t[:, 0:1]), reads=[rs], writes=[rs])
    kb.op("dve", lambda e: e.tensor_scalar(out=rs.t[:, 1:2], in0=mv.t[:, 0:1], scalar1=rs.t[:, 0:1], scalar2=-1.0,
                                            op0=ALU.mult, op1=ALU.mult), reads=[mv, rs], writes=[rs])
    kb.op("act", lambda e: e.activation(out=xt, in_=xt, func=AF.Identity, bias=rs.t[:, 1:2], scale=rs.t[:, 0:1]),
          reads=[xb, rs], writes=[xb])
    kb.op("pool", lambda e: e.tensor_tensor(out=xt, in0=xt, in1=gbc, op=ALU.mult), reads=[xb, gb], writes=[xb])
    kb.op("dve", lambda e: e.tensor_tensor(out=xt, in0=xt, in1=bbc, op=ALU.add), reads=[xb, gb], writes=[xb])


def build_ffn(phases=("tr", "in", "out", "ln"), groups=GROUPS):
    nc = bass.Bass("TRN2", target_bir_lowering=False)
    xin = nc.dram_tensor("xin", [NTOK, D], F32, kind="ExternalInput").ap()
    modc = nc.dram_tensor("modc", [128, 96], F32, kind="ExternalInput").ap()
    lnp = nc.dram_tensor("lnp", [2, D], F32, kind="ExternalInput").ap()
    ident = nc.dram_tensor("ident", [128, 128], F32, kind="ExternalInput").ap()
    w_in = nc.dram_tensor("w_in", [D, 2 * DFF], F32, kind="ExternalInput").ap()
    w_out = nc.dram_tensor("w_out", [DFF, D], F32, kind="ExternalInput").ap()
    xout = nc.dram_tensor("xout", [NTOK, D], F32, kind="ExternalOutput").ap()
    with ExitStack() as es:
        kb = KB(nc, es)
        mod = kb.sbuf("mod", [128, 2, 3, 16], F32)
        idb = kb.sbuf("ident_sb", [128, 128], F32)
        gbc = kb.sbuf("gbc", [128, 2, D], F32)
        xg = kb.sbuf("xg", [128, 4, D], F32)
        xgb = [Buf("xg%d" % i) for i in range(4)]
        hT = kb.sbuf("hT", [128, 16, 512], BF16)
        hid = kb.sbuf("hid", [128, 44, 512], BF16)
        wg = [kb.sbuf("wg%d" % i, [128, 16, 256], BF16) for i in range(2)]
        wu = [kb.sbuf("wu%d" % i, [128, 16, 256], BF16) for i in range(2)]
        wo = [kb.sbuf("wo%d" % i, [128, 44, 256], BF16) for i in range(2)]
        sg = [kb.sbuf("sg%d" % i, [128, 512], F32) for i in range(2)]
        yT = [kb.sbuf("yT%d" % i, [128, 512], F32) for i in range(2)]
        stats = [kb.sbuf("stats%d" % i, [128, 4, 6], F32) for i in range(2)]
        mv = [kb.sbuf("mv%d" % i, [128, 2], F32) for i in range(2)]
        rs = [kb.sbuf("rs%d" % i, [128, 2], F32) for i in range(2)]
        tpb = [kb.psum("tp%d" % i, [128, 512]) for i in range(2)]
        gps = [kb.psum("gp%d" % i, [128, 512]) for i in range(2)]
        ups = [kb.psum("up%d" % i, [128, 512]) for i in range(2)]
        yps = [kb.psum("yp%d" % i, [128, 512]) for i in range(2)]

        kb.dma("sp", mod.t[:].rearrange("p a b c -> p (a b c)"), modc, mod, writes=[mod])
        kb.dma("sp", idb.t[:], ident, idb, writes=[idb])
        kb.dma("sp", gbc.t[:, 0, :], lnp[0:1, :].partition_broadcast(128), gbc, writes=[(gbc, True)])
        kb.dma("sp", gbc.t[:, 1, :], lnp[1:2, :].partition_broadcast(128), gbc, writes=[(gbc, True)])
        kb.op("dve", lambda e: e.tensor_scalar(out=mod.t[:, :, 1, :], in0=mod.t[:, :, 1, :], scalar1=1.0, scalar2=None,
                                                op0=ALU.add), reads=[mod], writes=[mod])
        kb.op("dve", lambda e: e.tensor_scalar(out=mod.t[:, :, 2, :], in0=mod.t[:, :, 2, :], scalar1=0.5, scalar2=None,
                                                op0=ALU.mult), reads=[mod], writes=[mod])
        rot = {"tp": 0, "ev": 0}

        def next_tp():
            rot["tp"] ^= 1
            return tpb[rot["tp"]]

        def ev_eng():
            rot["ev"] ^= 1
            return "dve" if rot["ev"] else "act"

        for (t0, ntl) in groups:
            ntok = ntl * 128
            s = 1 if t0 >= 16 else 0
            kb.dma("sp", xg.t[:, 0:ntl, :], xin[t0 * 128:(t0 + ntl) * 128, :].rearrange("(t p) c -> p t c", p=128),
                   xg, writes=xgb[:ntl])
            for ti in range(ntl if "tr" in phases else 0):
                for kq in range(4):
                    tp = next_tp()
                    for k4 in range(4):
                        kc = kq * 4 + k4
                        kb.op("pe", lambda e, tp=tp, k4=k4, kc=kc, ti=ti: e.transpose(
                            out=tp.t[:, k4 * 128:(k4 + 1) * 128], in_=xg.t[:, ti, kc * 128:(kc + 1) * 128], identity=idb.t[:]),
                            reads=[xgb[ti], idb], writes=[(tp, True)])
                    eng = ev_eng()
                    for k4 in range(4):
                        kc = kq * 4 + k4
                        if eng == "dve":
                            kb.op("dve", lambda e, tp=tp, k4=k4, kc=kc, ti=ti: e.tensor_scalar(
                                out=hT.t[:, kc, ti * 128:(ti + 1) * 128], in0=tp.t[:, k4 * 128:(k4 + 1) * 128],
                                scalar1=mod.t[:, s, 1, kc:kc + 1], scalar2=mod.t[:, s, 0, kc:kc + 1],
                                op0=ALU.mult, op1=ALU.add), reads=[tp, mod], writes=[(hT, True)])
                        else:
                            kb.op("act", lambda e, tp=tp, k4=k4, kc=kc, ti=ti: e.activation(
                                out=hT.t[:, kc, ti * 128:(ti + 1) * 128], in_=tp.t[:, k4 * 128:(k4 + 1) * 128],
                                func=AF.Identity, bias=mod.t[:, s, 0, kc:kc + 1], scale=mod.t[:, s, 1, kc:kc + 1]),
                                reads=[tp, mod], writes=[(hT, True)])
            for jb in range(22 if "in" in phases else 0):
                wgb, wub = wg[jb % 2], wu[jb % 2]
                kb.dma("pool", wgb.t[:], w_in[:, jb * 256:(jb + 1) * 256].rearrange("(kc p) c -> p kc c", p=128),
                       wgb, writes=[wgb])
                kb.dma("pool", wub.t[:], w_in[:, DFF + jb * 256:DFF + (jb + 1) * 256].rearrange("(kc p) c -> p kc c", p=128),
                       wub, writes=[wub])
                for jj in range(2):
                    j = jb * 2 + jj
                    gp, up, sgb = gps[j % 2], ups[j % 2], sg[j % 2]
                    kb.mm_group(gp.t[:, :ntok], [(wgb.t[:, kc, jj * 128:(jj + 1) * 128], hT.t[:, kc, :ntok]) for kc in range(16)],
                                reads=[wgb, hT], out_buf=gp)
                    kb.mm_group(up.t[:, :ntok], [(wub.t[:, kc, jj * 128:(jj + 1) * 128], hT.t[:, kc, :ntok]) for kc in range(16)],
                                reads=[wub, hT], out_buf=up)
                    kb.op("act", lambda e, gp=gp, sgb=sgb: e.activation(out=sgb.t[:, :ntok], in_=gp.t[:, :ntok], func=AF.Silu),
                          reads=[gp], writes=[sgb])
                    kb.op("dve", lambda e, up=up, sgb=sgb, j=j: e.tensor_tensor(out=hid.t[:, j, :ntok], in0=sgb.t[:, :ntok],
                                                                              in1=up.t[:, :ntok], op=ALU.mult),
                          reads=[sgb, up], writes=[(hid, True)])
            for ob in range(8 if "out" in phases else 0):
                wob = wo[ob % 2]
                kb.dma("pool", wob.t[:], w_out[:, ob * 256:(ob + 1) * 256].rearrange("(j p) c -> p j c", p=128),
                       wob, writes=[wob])
                for oo in range(2):
                    oc = ob * 2 + oo
                    yp, yTb = yps[oc % 2], yT[oc % 2]
                    kb.mm_group(yp.t[:, :ntok], [(wob.t[:, j, oo * 128:(oo + 1) * 128], hid.t[:, j, :ntok]) for j in range(44)],
                                reads=[wob, hid], out_buf=yp)
                    kb.op("act", lambda e, yp=yp, yTb=yTb, oc=oc: e.activation(
                        out=yTb.t[:, :ntok], in_=yp.t[:, :ntok], func=AF.Identity, scale=mod.t[:, s, 2, oc:oc + 1]),
                        reads=[yp, mod], writes=[yTb])
                    tp = next_tp()
                    for ti in range(ntl):
                        kb.op("pe", lambda e, tp=tp, ti=ti, yTb=yTb: e.transpose(
                            out=tp.t[:, ti * 128:(ti + 1) * 128], in_=yTb.t[:, ti * 128:(ti + 1) * 128], identity=idb.t[:]),
                            reads=[yTb, idb], writes=[(tp, True)])
                    kb.op("dve", lambda e, tp=tp, oc=oc: e.scalar_tensor_tensor(
                        out=xg.t[:, 0:ntl, oc * 128:(oc + 1) * 128], in0=xg.t[:, 0:ntl, oc * 128:(oc + 1) * 128], scalar=ALPHA,
                        in1=tp.t[:, :ntok].rearrange("p (t c) -> p t c", c=128), op0=ALU.mult, op1=ALU.add),
                        reads=[tp] + xgb[:ntl], writes=[(b, True) for b in xgb[:ntl]])
            for ti in range(ntl):
                if "ln" in phases:
                        ln_epilogue(kb, xg.t[:, ti, :], xgb[ti], stats[ti % 2], mv[ti % 2], rs[ti % 2],
                                gbc.t[:, 0, :], gbc.t[:, 1, :], gbc)
                kb.dma("sp", xout[(t0 + ti) * 128:(t0 + ti + 1) * 128, :], xg.t[:, ti, :], xg, reads=[xgb[ti]], final=True)
        kb.finish()
    return nc


NFM = 26
FM_QA, FM_KA, FM_QC, FM_KC, FM_QD, FM_KD, FM_YB = 0, 4, 8, 12, 14, 18, 22
NTM = 1152
WFM_COLS = 40 * 128
WTM_COLS = 1664
GELU_C = 0.7978845608028654


def load_x_transpose_mod(kb, xin, t0, ntl, s, xg, xgb, hT, idb, mod, tpb, rot):
    kb.dma("sp", xg.t[:, 0:ntl, :], xin[t0 * 128:(t0 + ntl) * 128, :].rearrange("(t p) c -> p t c", p=128),
           xg, writes=xgb[:ntl])
    for ti in range(ntl):
        for kq in range(4):
            rot["tp"] ^= 1
            tp = tpb[rot["tp"]]
            for k4 in range(4):
                kc = kq * 4 + k4
                kb.op("pe", lambda e, tp=tp, k4=k4, kc=kc, ti=ti: e.transpose(
                    out=tp.t[:, k4 * 128:(k4 + 1) * 128], in_=xg.t[:, ti, kc * 128:(kc + 1) * 128], identity=idb.t[:]),
                    reads=[xgb[ti], idb], writes=[(tp, True)])
            rot["ev"] ^= 1
            for k4 in range(4):
                kc = kq * 4 + k4
                if rot["ev"]:
                    kb.op("dve", lambda e, tp=tp, k4=k4, kc=kc, ti=ti: e.tensor_scalar(
                        out=hT.t[:, kc, ti * 128:(ti + 1) * 128], in0=tp.t[:, k4 * 128:(k4 + 1) * 128],
                        scalar1=mod.t[:, s, 1, kc:kc + 1], scalar2=mod.t[:, s, 0, kc:kc + 1],
                        op0=ALU.mult, op1=ALU.add), reads=[tp, mod], writes=[(hT, True)])
                else:
                    kb.op("act", lambda e, tp=tp, k4=k4, kc=kc, ti=ti: e.activation(
                        out=hT.t[:, kc, ti * 128:(ti + 1) * 128], in_=tp.t[:, k4 * 128:(k4 + 1) * 128],
                        func=AF.Identity, bias=mod.t[:, s, 0, kc:kc + 1], scale=mod.t[:, s, 1, kc:kc + 1]),
                        reads=[tp, mod], writes=[(hT, True)])


def build_proj(groups=GROUPS):
    nc = bass.Bass("TRN2", target_bir_lowering=False)
    xin = nc.dram_tensor("xin", [NTOK, D], F32, kind="ExternalInput").ap()
    modc = nc.dram_tensor("modc", [128, 96], F32, kind="ExternalInput").ap()
    ident = nc.dram_tensor("ident", [128, 128], F32, kind="ExternalInput").ap()
    w_fm = nc.dram_tensor("w_fm", [D, WFM_COLS], F32, kind="ExternalInput").ap()
    w_tm = nc.dram_tensor("w_tm", [D, WTM_COLS], F32, kind="ExternalInput").ap()
    rope = nc.dram_tensor("rope", [2, 128, NTOK], F32, kind="ExternalInput").ap()
    gnp = nc.dram_tensor("gnp", [2, 512], F32, kind="ExternalInput").ap()
    wsT = nc.dram_tensor("wsT", [128, 512], F32, kind="ExternalInput").ap()
    bsr = nc.dram_tensor("bsr", [1, 512], F32, kind="ExternalInput").ap()
    fm = nc.dram_tensor("fm", [NFM, 128, NTOK], BF16, kind="ExternalOutput").ap()
    tm = nc.dram_tensor("tm", [NTOK, NTM], BF16, kind="ExternalOutput").ap()
    with ExitStack() as es:
        kb = KB(nc, es)
        mod = kb.sbuf("mod", [128, 2, 3, 16], F32)
        idb = kb.sbuf("ident_sb", [128, 128], F32)
        rp = kb.sbuf("rope_sb", [128, 2, NTOK], F32)
        gn = kb.sbuf("gn", [128, 2, 512], F32)
        ws = kb.sbuf("ws", [128, 512], BF16)
        bsb = kb.sbuf("bsb", [128, 512], F32)
        xg = kb.sbuf("xg", [128, 4, D], F32)
        xgb = [Buf("xg%d" % i) for i in range(4)]
        hT = kb.sbuf("hT", [128, 16, 512], BF16)
        uT = kb.sbuf("uT", [128, 4, 512], BF16)
        wb = [kb.sbuf("wb%d" % i, [128, 16, 256], BF16) for i in range(2)]
        wt = [kb.sbuf("wt%d" % i, [128, 16, 512], BF16) for i in range(2)]
        t1 = [kb.sbuf("t1_%d" % i, [128, 512], F32) for i in range(2)]
        t2 = [kb.sbuf("t2_%d" % i, [128, 512], F32) for i in range(2)]
        stg = [kb.sbuf("stg%d" % i, [128, 512], BF16) for i in range(3)]
        vv = [kb.sbuf("vv%d" % i, [128, 512], F32) for i in range(2)]
        vt = [kb.sbuf("vt%d" % i, [128, 512], BF16) for i in range(2)]
        ybs = [kb.sbuf("ybs%d" % i, [128, 512], F32) for i in range(2)]
        stats = [kb.sbuf("stats%d" % i, [128, 6], F32) for i in range(2)]
        mv = [kb.sbuf("mv%d" % i, [128, 2], F32) for i in range(2)]
        rs = [kb.sbuf("rs%d" % i, [128, 2], F32) for i in range(2)]
        tpb = [kb.psum("tp%d" % i, [128, 512]) for i in range(2)]
        pp = [kb.psum("pp%d" % i, [128, 512]) for i in range(4)]
        pt = [kb.psum("pt%d" % i, [128, 512]) for i in range(2)]

        kb.dma("sp", mod.t[:].rearrange("p a b c -> p (a b c)"), modc, mod, writes=[mod])
        kb.dma("sp", idb.t[:], ident, idb, writes=[idb])
        kb.dma("sp", rp.t[:, 0, :], rope[0], rp, writes=[(rp, True)])
        kb.dma("sp", rp.t[:, 1, :], rope[1], rp, writes=[(rp, True)])
        kb.dma("sp", gn.t[:, 0, :], gnp[0:1, :].partition_broadcast(128), gn, writes=[(gn, True)])
        kb.dma("sp", gn.t[:, 1, :], gnp[1:2, :].partition_broadcast(128), gn, writes=[(gn, True)])
        kb.dma("sp", bsb.t[:], bsr[0:1, :].partition_broadcast(128), bsb, writes=[bsb])
        kb.dma("pool", ws.t[:], wsT, ws, writes=[ws])
        kb.op("dve", lambda e: e.tensor_scalar(out=mod.t[:, :, 1, :], in0=mod.t[:, :, 1, :], scalar1=1.0, scalar2=None,
                                                op0=ALU.add), reads=[mod], writes=[mod])
        rot = {"tp": 0, "ev": 0, "stg": 0, "wt": 0}

        def next_stg():
            rot["stg"] = (rot["stg"] + 1) % 3
            return stg[rot["stg"]]

        for (t0, ntl) in groups:
            ntok = ntl * 128
            s = 1 if t0 >= 16 else 0
            tk = slice(t0 * 128, t0 * 128 + ntok)
            load_x_transpose_mod(kb, xin, t0, ntl, s, xg, xgb, hT, idb, mod, tpb, rot)
            for blk in range(20):
                wbb = wb[blk % 2]
                kb.dma("pool", wbb.t[:], w_fm[:, blk * 256:(blk + 1) * 256].rearrange("(kc p) c -> p kc c", p=128),
                       wbb, writes=[wbb])
                pa, pb = pp[(blk % 2) * 2], pp[(blk % 2) * 2 + 1]
                for jj, pq in ((0, pa), (1, pb)):
                    kb.mm_group(pq.t[:, :ntok], [(wbb.t[:, kc, jj * 128:(jj + 1) * 128], hT.t[:, kc, :ntok]) for kc in range(16)],
                                reads=[wbb, hT], out_buf=pq)
                if blk < 14:
                    oc = (FM_QA + blk) if blk < 4 else (FM_KA + blk - 4) if blk < 8 else (FM_QC + blk - 8) if blk < 12 \
                        else (FM_KC + blk - 12)
                    a1, a2, sg_ = t1[blk % 2], t2[blk % 2], next_stg()
                    kb.op("dve", lambda e, pa=pa, a1=a1: e.tensor_tensor(out=a1.t[:, :ntok], in0=pa.t[:, :ntok], in1=rp.t[:, 0, tk],
                                                                           op=ALU.mult), reads=[pa, rp], writes=[a1])
                    kb.op("dve", lambda e, pb=pb, a2=a2: e.tensor_tensor(out=a2.t[:, :ntok], in0=pb.t[:, :ntok], in1=rp.t[:, 1, tk],
                                                                           op=ALU.mult), reads=[pb, rp], writes=[a2])
                    kb.op("pool", lambda e, a1=a1, a2=a2, sg_=sg_: e.tensor_tensor(out=sg_.t[:, :ntok], in0=a1.t[:, :ntok],
                                                                                      in1=a2.t[:, :ntok], op=ALU.add),
                          reads=[a1, a2], writes=[sg_])
                    kb.dma("sp", fm[oc, :, tk], sg_.t[:, :ntok], sg_, reads=[sg_], final=True)
                elif blk < 18:
                    for jj, pq in ((0, pa), (1, pb)):
                        ci = (blk - 14) * 2 + jj
                        oc = (FM_QD + ci) if ci < 4 else (FM_KD + ci - 4)
                        sg_ = next_stg()
                        kb.op("act", lambda e, pq=pq, sg_=sg_: e.activation(out=sg_.t[:, :ntok], in_=pq.t[:, :ntok], func=AF.Identity),
                              reads=[pq], writes=[sg_])
                        kb.dma("sp", fm[oc, :, tk], sg_.t[:, :ntok], sg_, reads=[sg_], final=True)
                else:
                    for jj, pq in ((0, pa), (1, pb)):
                        ci = (blk - 18) * 2 + jj
                        kb.op("act", lambda e, pq=pq, ci=ci: e.activation(out=uT.t[:, ci, :ntok], in_=pq.t[:, :ntok],
                                                                            func=AF.Gelu_apprx_tanh), reads=[pq], writes=[(uT, True)])
            for bi, (c0, ncol) in enumerate(((0, 512), (512, 512), (1024, 512), (1536, 128))):
                rot["wt"] ^= 1
                wtb = wt[rot["wt"]]
                kb.dma("pool", wtb.t[:, :, :ncol], w_tm[:, c0:c0 + ncol].rearrange("(kc p) c -> p kc c", p=128),
                       wtb, writes=[wtb])
                for ti in range(ntl):
                    ptb = pt[ti % 2]
                    tks = slice((t0 + ti) * 128, (t0 + ti + 1) * 128)
                    kb.mm_group(ptb.t[:, :ncol], [(hT.t[:, kc, ti * 128:(ti + 1) * 128], wtb.t[:, kc, :ncol]) for kc in range(16)],
                                reads=[wtb, hT], out_buf=ptb)
                    if bi == 0:
                        v_, vb_, st_, mv_, rs_, yb_ = vv[ti % 2], vt[ti % 2], stats[ti % 2], mv[ti % 2], rs[ti % 2], ybs[ti % 2]
                        kb.op("act", lambda e, ptb=ptb, v_=v_: e.activation(out=v_.t[:], in_=ptb.t[:], func=AF.Gelu_apprx_tanh),
                              reads=[ptb], writes=[v_])
                        kb.op("dve", lambda e, v_=v_, st_=st_: e.bn_stats(out=st_.t[:], in_=v_.t[:]), reads=[v_], writes=[st_])
                        kb.op("dve", lambda e, st_=st_, mv_=mv_: e.bn_aggr(out=mv_.t[:], in_=st_.t[:]), reads=[st_], writes=[mv_])
                        kb.op("act", lambda e, mv_=mv_, rs_=rs_: e.activation(out=rs_.t[:, 0:1], in_=mv_.t[:, 1:2], func=AF.Sqrt,
                                                                                bias=LN_EPS, scale=1.0), reads=[mv_], writes=[rs_])
                        kb.op("dve", lambda e, rs_=rs_: e.reciprocal(out=rs_.t[:, 0:1], in_=rs_.t[:, 0:1]), reads=[rs_], writes=[rs_])
                        kb.op("dve", lambda e, rs_=rs_, mv_=mv_: e.tensor_scalar(out=rs_.t[:, 1:2], in0=mv_.t[:, 0:1],
                                                                                   scalar1=rs_.t[:, 0:1], scalar2=-1.0,
                                                                                   op0=ALU.mult, op1=ALU.mult),
                              reads=[mv_, rs_], writes=[rs_])
                        kb.op("act", lambda e, v_=v_, rs_=rs_: e.activation(out=v_.t[:], in_=v_.t[:], func=AF.Identity,
                                                                              bias=rs_.t[:, 1:2], scale=rs_.t[:, 0:1]),
                              reads=[v_, rs_], writes=[v_])
                        kb.op("pool", lambda e, v_=v_: e.tensor_tensor(out=v_.t[:], in0=v_.t[:], in1=gn.t[:, 0, :], op=ALU.mult),
                              reads=[v_, gn], writes=[v_])
                        kb.op("dve", lambda e, v_=v_, vb_=vb_: e.tensor_tensor(out=vb_.t[:], in0=v_.t[:], in1=gn.t[:, 1, :], op=ALU.add),
                              reads=[v_, gn], writes=[vb_])
                        rot["tp"] ^= 1
                        mp = tpb[rot["tp"]]
                        for g in range(4):
                            kb.mm_group(mp.t[:, g * 128:(g + 1) * 128], [(vb_.t[:, g * 128:(g + 1) * 128], ws.t[:, g * 128:(g + 1) * 128])],
                                        reads=[vb_, ws], out_buf=mp, part=True)
                        kb.op("dve", lambda e, mp=mp, yb_=yb_: e.tensor_tensor(out=yb_.t[:], in0=mp.t[:], in1=bsb.t[:], op=ALU.add),
                              reads=[mp, bsb], writes=[yb_])
                        sg_ = next_stg()
                        kb.op("pool", lambda e, yb_=yb_, sg_=sg_, ti=ti: e.tensor_tensor(
                            out=sg_.t[:].rearrange("p (g t) -> p g t", g=4), in0=yb_.t[:].rearrange("p (g t) -> p g t", g=4),
                            in1=uT.t[:, :, ti * 128:(ti + 1) * 128], op=ALU.mult), reads=[yb_, uT], writes=[sg_])
                        kb.dma("sp", fm[FM_YB:FM_YB + 4, :, tks].rearrange("g p t -> p g t"),
                               sg_.t[:].rearrange("p (g t) -> p g t", g=4), sg_, reads=[sg_], final=True)
                    else:
                        oc0 = {1: 0, 2: 640, 3: 512}[bi]
                        sg_ = next_stg()
                        if ti % 2:
                            kb.op("act", lambda e, ptb=ptb, sg_=sg_: e.activation(out=sg_.t[:, :ncol], in_=ptb.t[:, :ncol], func=AF.Identity),
                                  reads=[ptb], writes=[sg_])
                        else:
                            kb.op("dve", lambda e, ptb=ptb, sg_=sg_: e.tensor_copy(out=sg_.t[:, :ncol], in_=ptb.t[:, :ncol]),
                                  reads=[ptb], writes=[sg_])
                        kb.dma("sp", tm[tks, oc0:oc0 + ncol], sg_.t[:, :ncol], sg_, reads=[sg_], final=True)
        kb.finish()
    return nc


PROJ_CUTS = np.cumsum([512, 512, 512, 512, 512, 512, 128, 128, 512, 512, 512])[:-1].tolist()


def swap_cols(w):
    k, n = w.shape
    return np.ascontiguousarray(w.reshape(k, n // 64, 2, 32)[:, :, ::-1, :]).reshape(k, n)


def build_proj_weights(w):
    aq, ak, av, bu, bv, cq, ck, cv, dq, dk, dv = np.split(w, PROJ_CUTS, axis=1)
    chunks = []

    def pairs(m):
        ms = swap_cols(m)
        for i in range(m.shape[1] // 128):
            chunks.append(m[:, i * 128:(i + 1) * 128])
            chunks.append(ms[:, i * 128:(i + 1) * 128])

    pairs(aq)
    pairs(ak)
    pairs(cq)
    pairs(np.concatenate([ck[:, :64], ck[:, :64], ck[:, 64:], ck[:, 64:]], 1))
    for m in (dq, dk, bu):
        for i in range(4):
            chunks.append(m[:, i * 128:(i + 1) * 128])
    w_fm = np.ascontiguousarray(np.concatenate(chunks, 1))
    w_tm = np.ascontiguousarray(np.concatenate([bv, av, dv, cv], 1))
    return w_fm, w_tm


def rope_tables(core):
    t = np.arange(core * 2048, (core + 1) * 2048)
    row = (t // 64).astype(np.float32)
    col = (t % 64).astype(np.float32)
    inv = (np.float32(10000.0) ** (-np.arange(16, dtype=np.float32) / np.float32(16))).astype(np.float32)
    ang = np.concatenate([row[:, None] * inv, col[:, None] * inv], -1).astype(np.float32)
    cos, sin = np.cos(ang).astype(np.float32), np.sin(ang).astype(np.float32)
    tab = np.zeros((2, 128, NTOK), np.float32)
    tab[0, :, 2048:] = 1.0
    for p in range(128):
        d = p % 64
        j = d % 32
        tab[0, p, :2048] = cos[:, j]
        tab[1, p, :2048] = -sin[:, j] if d < 32 else sin[:, j]
    return tab


def mod_cols(m_lat, m_ctx, idx):
    out = np.zeros((128, 2, 3, 16), np.float32)
    for s, m in enumerate((m_lat, m_ctx)):
        mm = m.reshape(9, 16, 128)
        for wi, i in enumerate(idx):
            if i is not None:
                out[:, s, wi, :] = mm[i].T
    return out.reshape(128, 96)


ORDER = (0, 2, 1, 3)
NKA = 256 + 16384
NEXT = 22 * 128
D_OFFS = {0: (-2, -1, 0, 1, 2, 3), 1: (-2, -1, 0, 1, 2), 2: (-2, -1, 0, 1, 2), 3: (-2, -1, 0, 1, 2), 4: (-3, -2, -1, 0, 1, 2)}


def d_class(T):
    return 0 if T == 0 else 1 if T == 1 else 3 if T == 14 else 4 if T == 15 else 2


def build_attn(do=("A", "C", "D", "O"), qgroups=GROUPS, dbg=False, stage=9, ctiles=18):
    nc = bass.Bass("TRN2", target_bir_lowering=False)
    xin = nc.dram_tensor("xin", [NTOK, D], F32, kind="ExternalInput").ap()
    fm = nc.dram_tensor("fm", [NFM, 128, NTOK], BF16, kind="ExternalInput").ap()
    ka = nc.dram_tensor("ka", [4, 128, NKA], BF16, kind="ExternalInput").ap()
    va = nc.dram_tensor("va", [NKA, 512], BF16, kind="ExternalInput").ap()
    kc = nc.dram_tensor("kc", [2, 128, NEXT], BF16, kind="ExternalInput").ap()
    vc = nc.dram_tensor("vc", [NEXT, 128], BF16, kind="ExternalInput").ap()
    kd = nc.dram_tensor("kd", [4, 128, NEXT], BF16, kind="ExternalInput").ap()
    vd = nc.dram_tensor("vd", [NEXT, 512], BF16, kind="ExternalInput").ap()
    cmask = nc.dram_tensor("cmask", [4, 128, 512], F32, kind="ExternalInput").ap()
    dbias = nc.dram_tensor("dbias", [5, 7, 2, 128, 512], F32, kind="ExternalInput").ap()
    alam = nc.dram_tensor("alam", [1, 258], F32, kind="ExternalInput").ap()
    sink = nc.dram_tensor("sink", [1, 8], F32, kind="ExternalInput").ap()
    ident = nc.dram_tensor("ident", [128, 128], F32, kind="ExternalInput").ap()
    gate = nc.dram_tensor("gate", [2, D], F32, kind="ExternalInput").ap()
    lnp = nc.dram_tensor("lnp", [2, D], F32, kind="ExternalInput").ap()
    w_ab = nc.dram_tensor("w_ab", [1024, D], F32, kind="ExternalInput").ap()
    w_cd = nc.dram_tensor("w_cd", [1024, D], F32, kind="ExternalInput").ap()
    xout = nc.dram_tensor("xout", [NTOK, D], F32, kind="ExternalOutput").ap()
    dk_ = {"kind": "ExternalOutput"} if dbg else {}
    catA = nc.dram_tensor("catA", [4, 128, NTOK], BF16, **dk_).ap()
    catC = nc.dram_tensor("catC", [8, 64, NTOK], BF16, **dk_).ap()
    catD = nc.dram_tensor("catD", [8, 64, NTOK], BF16, **dk_).ap()
    with ExitStack() as es:
        kb = KB(nc, es)
        catAb, catCb, catDb = Buf("catA"), Buf("catC"), Buf("catD")
        ps = [kb.psum("ps%d" % i, [128, 512]) for i in range(8)]
        ones = kb.sbuf("ones", [128, 128], BF16)
        onesf = kb.sbuf("onesf", [128, 128], F32)
        lam = kb.sbuf("lam", [128, 264], F32)
        esk = kb.sbuf("esk", [128, 8], F32)
        kb.op("pool", lambda e: e.memset(ones.t[:], 1.0), writes=[ones])
        kb.op("pool", lambda e: e.memset(onesf.t[:], 1.0), writes=[onesf])
        kb.dma("sp", lam.t[:, 0:258], alam[0:1, :].partition_broadcast(128), lam, writes=[lam])
        kb.dma("sp", esk.t[:], sink[0:1, :].partition_broadcast(128), esk, writes=[esk])
        lsc = kb.sbuf("lsc", [128, 128], F32)
        kb.op("dve", lambda e: e.tensor_tensor(out=lsc.t[:, 0:64], in0=lam.t[:, 0:64], in1=lam.t[:, 64:128], op=ALU.mult),
              reads=[lam], writes=[lsc])
        kb.op("dve", lambda e: e.tensor_tensor(out=lsc.t[:, 64:128], in0=lam.t[:, 128:192], in1=lam.t[:, 192:256], op=ALU.mult),
              reads=[lam, lsc], writes=[lsc])
        kb.op("dve", lambda e: e.reduce_sum(out=lam.t[:, 258:260], in_=lsc.t[:].rearrange("p (a b) -> p a b", a=2),
                                            axis=mybir.AxisListType.X), reads=[lsc, lam], writes=[lam])
        kb.op("act", lambda e: e.activation(out=lam.t[:, 258:260], in_=lam.t[:, 258:260], func=AF.Exp), reads=[lam], writes=[lam])
        kb.op("act", lambda e: e.activation(out=esk.t[:], in_=esk.t[:], func=AF.Exp), reads=[esk], writes=[esk])
        kb.op("dve", lambda e: e.tensor_tensor(out=lam.t[:, 260:261], in0=lam.t[:, 259:260], in1=lam.t[:, 258:259], op=ALU.subtract),
              reads=[lam], writes=[lam])
        kb.op("dve", lambda e: e.tensor_tensor(out=lam.t[:, 260:261], in0=lam.t[:, 260:261], in1=lam.t[:, 256:257], op=ALU.subtract),
              reads=[lam], writes=[lam])
        NLAM, OML = lam.t[:, 260:261], lam.t[:, 257:258]

        if "A" in do:
          with ExitStack() as pes:
            qa = kb.sbuf("qa", [128, 512], BF16, pes)
            kblk = [kb.sbuf("kblk%d" % i, [128, 2048], BF16, pes) for i in range(2)]
            vblk = [kb.sbuf("vblk%d" % i, [128, 16, 128], BF16, pes) for i in range(2)]
            pt_ = [[kb.sbuf("p%d_%d" % (c, i), [128, 512], BF16, pes) for i in range(2)] for c in range(2)]
            r0 = kb.sbuf("r0", [128, 512], F32, pes)
            r1 = kb.sbuf("r1", [128, 512], F32, pes)
            a0 = kb.sbuf("a0", [128, 512], F32, pes)
            a1 = kb.sbuf("a1", [128, 512], F32, pes)
            sq = kb.sbuf("sq", [128, 512], F32, pes)
            yst = [kb.sbuf("yst%d" % i, [128, 512], BF16, pes) for i in range(2)]
            O0, O1, Z0, Z1 = ps[4], ps[5], ps[6], ps[7]
            nblk = 0
            nst = 0
            for h in range(4):
                for (t0, ntl) in qgroups:
                    nq = ntl * 128
                    tk = slice(t0 * 128, t0 * 128 + nq)
                    isctx = t0 >= 16
                    kb.dma("sp", qa.t[:, :nq], fm[FM_QA + h, :, tk], qa, writes=[qa])
                    blocks = [(0, 256)] + ([] if isctx else [(256 + i * 2048, 2048) for i in range(8)])
                    nkt_total = sum(b[1] // 128 for b in blocks)
                    kt_i = 0
                    for (k0, ksz) in blocks:
                        kbb, vbb = kblk[nblk % 2], vblk[nblk % 2]
                        nblk += 1
                        nkt = ksz // 128
                        kb.dma("sp", kbb.t[:, :ksz], ka[h, :, k0:k0 + ksz], kbb, writes=[kbb])
                        kb.dma("sp", vbb.t[:, :nkt, :], va[k0:k0 + ksz, h * 128:(h + 1) * 128].rearrange("(t p) c -> p t c", p=128),
                               vbb, writes=[vbb])
                        for kt in range(nkt):
                            par = kt_i % 2
                            S0, S1 = ps[par * 2], ps[par * 2 + 1]
                            P0, P1 = pt_[0][par], pt_[1][par]
                            kb.mm(S0.t[:, :nq], kbb.t[0:64, kt * 128:(kt + 1) * 128], qa.t[0:64, :nq], True, True, [kbb, qa], S0)
                            kb.mm(S1.t[:, :nq], kbb.t[64:128, kt * 128:(kt + 1) * 128], qa.t[64:128, :nq], True, True, [kbb, qa], S1)
                            kb.op("act", lambda e, S0=S0, P0=P0: e.activation(out=P0.t[:, :nq], in_=S0.t[:, :nq], func=AF.Exp, scale=0.125),
                                  reads=[S0], writes=[P0])
                            kb.op("act", lambda e, S1=S1, P1=P1: e.activation(out=P1.t[:, :nq], in_=S1.t[:, :nq], func=AF.Exp, scale=0.125),
                                  reads=[S1], writes=[P1])
                            first, last = kt_i == 0, kt_i == nkt_total - 1
                            kb.mm(O0.t[:, :nq], vbb.t[:, kt, :], P0.t[:, :nq], first, last, [vbb, P0], O0, inc=False)
                            kb.mm(Z0.t[:, :nq], ones.t[:], P0.t[:, :nq], first, last, [ones, P0], Z0, inc=False)
                            kb.mm(O1.t[:, :nq], vbb.t[:, kt, :], P1.t[:, :nq], first, last, [vbb, P1], O1, inc=False)
                            kb.mm(Z1.t[:, :nq], ones.t[:], P1.t[:, :nq], first, last, [ones, P1], Z1, inc=True)
                            kt_i += 1
                    kb.op("dve", lambda e: e.reciprocal(out=r0.t[:, :nq], in_=Z0.t[:, :nq]), reads=[Z0], writes=[r0])
                    kb.op("dve", lambda e: e.reciprocal(out=r1.t[:, :nq], in_=Z1.t[:, :nq]), reads=[Z1], writes=[r1])
                    kb.op("dve", lambda e: e.tensor_tensor(out=a0.t[:, :nq], in0=O0.t[:, :nq], in1=r0.t[:, :nq], op=ALU.mult),
                          reads=[O0, r0], writes=[a0])
                    kb.op("dve", lambda e: e.tensor_tensor(out=a1.t[:, :nq], in0=O1.t[:, :nq], in1=r1.t[:, :nq], op=ALU.mult),
                          reads=[O1, r1], writes=[a1])
                    kb.op("dve", lambda e: e.scalar_tensor_tensor(out=a0.t[:, :nq], in0=a1.t[:, :nq], scalar=NLAM, in1=a0.t[:, :nq],
                                                                  op0=ALU.mult, op1=ALU.add), reads=[a0, a1, lam], writes=[a0])
                    kb.op("pool", lambda e: e.tensor_tensor(out=sq.t[:, :nq], in0=a0.t[:, :nq], in1=a0.t[:, :nq], op=ALU.mult),
                          reads=[a0], writes=[sq])
                    MS = ps[0]
                    kb.mm(MS.t[:, :nq], onesf.t[:], sq.t[:, :nq], True, True, [onesf, sq], MS)
                    kb.op("act", lambda e, MS=MS: e.activation(out=r0.t[:, :nq], in_=MS.t[:, :nq], func=AF.Sqrt, bias=LN_EPS, scale=1.0 / 128),
                          reads=[MS], writes=[r0])
                    kb.op("dve", lambda e: e.reciprocal(out=r0.t[:, :nq], in_=r0.t[:, :nq]), reads=[r0], writes=[r0])
                    ys = yst[nst % 2]
                    nst += 1
                    kb.op("dve", lambda e, ys=ys: e.scalar_tensor_tensor(out=ys.t[:, :nq], in0=a0.t[:, :nq], scalar=OML, in1=r0.t[:, :nq],
                                                                         op0=ALU.mult, op1=ALU.mult), reads=[a0, r0, lam], writes=[ys])
                    kb.dma("sp", catA[h, :, tk], ys.t[:, :nq], ys, reads=[ys], writes=[(catAb, True)])
            kb.barrier()

        if "C" in do or "D" in do:
          with ExitStack() as pes:
            qc = kb.sbuf("qc", [128, 4, NTOK], BF16, pes)
            kcx = kb.sbuf("kcx", [128, 2, NEXT], BF16, pes)
            vcx = kb.sbuf("vcx", [128, 44, 65], BF16, pes)
            qd = kb.sbuf("qd", [128, 4, NTOK], BF16, pes)
            kdx = kb.sbuf("kdx", [128, 4, NEXT], BF16, pes)
            vdx = kb.sbuf("vdx", [128, 176, 65], BF16, pes)
            vstage = kb.sbuf("vstage", [128, 22 * 512], BF16, pes)
            cmk = kb.sbuf("cmk", [128, 4, 512], BF16, pes)
            db = [kb.sbuf("db%d" % i, [128, 512], F32, pes) for i in range(2)]
            tS = [kb.sbuf("tS%d" % i, [128, 512], F32, pes) for i in range(2)]
            pcd = [kb.sbuf("pcd%d" % i, [128, 512], BF16, pes) for i in range(2)]
            zr = kb.sbuf("zr", [128, 512], F32, pes)
            bcs = kb.sbuf("bcs", [128, 512], F32, pes)
            stgc = [kb.sbuf("stgc%d" % i, [128, 512], BF16, pes) for i in range(2)]
            kb.dma("sp", qc.t[:], fm[FM_QC:FM_QC + 4].rearrange("c p t -> p c t"), qc, writes=[qc])
            kb.dma("sp", qd.t[:], fm[FM_QD:FM_QD + 4].rearrange("c p t -> p c t"), qd, writes=[qd])
            kb.dma("sp", kcx.t[:], kc.rearrange("c p t -> p c t"), kcx, writes=[kcx])
            kb.dma("sp", kdx.t[:], kd.rearrange("c p t -> p c t"), kdx, writes=[kdx])
            kb.dma("pool", cmk.t[:], cmask.rearrange("m p c -> p m c"), cmk, writes=[cmk])
            kb.op("pool", lambda e: e.memset(vcx.t[:], 1.0), writes=[vcx])
            kb.op("pool", lambda e: e.memset(vdx.t[:], 1.0), writes=[vdx])
            kb.dma("sp", vstage.t[:, 0:22 * 128].rearrange("p (t c) -> p t c", c=128), vc.rearrange("(t p) c -> p t c", p=128),
                   vstage, writes=[vstage])
            kb.op("dve", lambda e: e.tensor_copy(out=vcx.t[:, :, 0:64], in_=vstage.t[:, 0:22 * 128].rearrange("p (a d) -> p a d", d=64)),
                  reads=[vstage], writes=[vcx])
            kb.dma("sp", vstage.t[:].rearrange("p (t c) -> p t c", c=512), vd.rearrange("(t p) c -> p t c", p=128),
                   vstage, writes=[vstage])
            kb.op("dve", lambda e: e.tensor_copy(out=vdx.t[:, :, 0:64], in_=vstage.t[:].rearrange("p (a d) -> p a d", d=64)),
                  reads=[vstage], writes=[vdx])
            cnt = {"s": 0, "o": 0, "p": 0, "b": 0, "g": 0, "db": 0}

            def finalize_cd(O, sink_cols, dst, dst_buf):
                if sink_cols is not None:
                    for sl, g in enumerate(ORDER):
                        kb.op("dve", lambda e, g=g, sl=sl: e.tensor_scalar(
                            out=zr.t[64:65, sl * 128:(sl + 1) * 128], in0=O.t[64:65, sl * 128:(sl + 1) * 128],
                            scalar1=esk.t[64:65, sink_cols + g:sink_cols + g + 1], scalar2=None, op0=ALU.add),
                            reads=[O, esk], writes=[(zr, True)])
                else:
                    kb.op("dve", lambda e: e.tensor_copy(out=zr.t[64:65, :], in_=O.t[64:65, :]), reads=[O], writes=[zr])
                kb.op("dve", lambda e: e.reciprocal(out=zr.t[64:65, :], in_=zr.t[64:65, :]), reads=[zr], writes=[zr])
                BC = ps[6 + cnt["b"] % 2]
                cnt["b"] += 1
                kb.mm(BC.t[0:64, :], onesf.t[64:65, 0:64], zr.t[64:65, :], True, True, [onesf, zr], BC)
                kb.op("act", lambda e: e.activation(out=bcs.t[0:64, :], in_=BC.t[0:64, :], func=AF.Identity), reads=[BC], writes=[bcs])
                sg_ = stgc[cnt["g"] % 2]
                cnt["g"] += 1
                kb.op("dve", lambda e: e.tensor_tensor(out=sg_.t[0:64, :].rearrange("p (a b t) -> p b a t", a=2, b=2),
                                                        in0=O.t[0:64, :].rearrange("p (b a t) -> p b a t", a=2, b=2),
                                                        in1=bcs.t[0:64, :].rearrange("p (b a t) -> p b a t", a=2, b=2), op=ALU.mult),
                      reads=[O, bcs], writes=[sg_])
                kb.dma("sp", dst.rearrange("g p t -> p g t"), sg_.t[0:64, :].rearrange("p (g t) -> p g t", g=4), sg_,
                       reads=[sg_], writes=[(dst_buf, True)])

            for T in range(ctiles if "C" in do else 0):
                isctx = T >= 16
                qtk = slice(T * 128, (T + 1) * 128)
                keys = [(0, None), (1, None)]
                if not isctx:
                    keys += [(4 + T - 1, 0 if T == 0 else 1), (4 + T, None), (4 + T + 1, 3 if T == 15 else 2)]
                for kv in range(2):
                    OC = ps[4 + cnt["o"] % 2]
                    cnt["o"] += 1
                    for ki, (e_, mk) in enumerate(keys):
                        Sa, Sb = ps[(cnt["s"] % 2) * 2], ps[(cnt["s"] % 2) * 2 + 1]
                        cnt["s"] += 1
                        for sl, g in enumerate(ORDER):
                            j = kv * 4 + g
                            ch, hf = j // 2, j % 2
                            Sx = Sa if hf == 0 else Sb
                            kb.mm(Sx.t[:, (sl % 2) * 128:(sl % 2 + 1) * 128], kcx.t[hf * 64:(hf + 1) * 64, kv, e_ * 128:(e_ + 1) * 128],
                                  qc.t[hf * 64:(hf + 1) * 64, ch, qtk], True, True, [kcx, qc], Sx, inc=(sl % 2 == 1))
                        P = pcd[cnt["p"] % 2]
                        cnt["p"] += 1
                        kb.op("act", lambda e, Sa=Sa, P=P: e.activation(out=P.t[:, 0:256], in_=Sa.t[:, 0:256], func=AF.Exp, scale=0.125),
                              reads=[Sa], writes=[(P, True)])
                        kb.op("act", lambda e, Sb=Sb, P=P: e.activation(out=P.t[:, 256:512], in_=Sb.t[:, 0:256], func=AF.Exp, scale=0.125),
                              reads=[Sb], writes=[(P, True)])
                        if mk is not None and stage >= 2:
                            kb.op("dve", lambda e, P=P, mk=mk: e.tensor_tensor(out=P.t[:], in0=P.t[:], in1=cmk.t[:, mk, :], op=ALU.mult),
                                  reads=[P, cmk], writes=[P])
                        if stage >= 3:
                            kb.mm(OC.t[0:65, :], vcx.t[:, e_ * 2 + kv, :], P.t[:], ki == 0, ki == len(keys) - 1, [vcx, P], OC)
                    if stage >= 4:
                        finalize_cd(OC, kv * 4, catC[kv * 4:(kv + 1) * 4, :, qtk], catCb)

            for T in range(18 if "D" in do else 0):
                isctx = T >= 16
                qtk = slice(T * 128, (T + 1) * 128)
                keys = [(0, None), (1, None)]
                if not isctx:
                    cls = d_class(T)
                    keys += [(4 + T + o, (cls, o + 3)) for o in D_OFFS[cls]]
                for hg in range(2):
                    OD = ps[4 + cnt["o"] % 2]
                    cnt["o"] += 1
                    for ki, (e_, bi) in enumerate(keys):
                        Sa, Sb = ps[(cnt["s"] % 2) * 2], ps[(cnt["s"] % 2) * 2 + 1]
                        cnt["s"] += 1
                        for sl, i in enumerate(ORDER):
                            h = hg * 4 + i
                            ch, hf = h // 2, h % 2
                            Sx = Sa if hf == 0 else Sb
                            kb.mm(Sx.t[:, (sl % 2) * 128:(sl % 2 + 1) * 128], kdx.t[hf * 64:(hf + 1) * 64, ch, e_ * 128:(e_ + 1) * 128],
                                  qd.t[hf * 64:(hf + 1) * 64, ch, qtk], True, True, [kdx, qd], Sx, inc=(sl % 2 == 1))
                        P = pcd[cnt["p"] % 2]
                        cnt["p"] += 1
                        if bi is None:
                            kb.op("act", lambda e, Sa=Sa, P=P: e.activation(out=P.t[:, 0:256], in_=Sa.t[:, 0:256], func=AF.Exp, scale=0.125),
                                  reads=[Sa], writes=[(P, True)])
                            kb.op("act", lambda e, Sb=Sb, P=P: e.activation(out=P.t[:, 256:512], in_=Sb.t[:, 0:256], func=AF.Exp, scale=0.125),
                                  reads=[Sb], writes=[(P, True)])
                        else:
                            dbb, tsb = db[cnt["db"] % 2], tS[cnt["db"] % 2]
                            cnt["db"] += 1
                            kb.dma("sp", dbb.t[:], dbias[bi[0], bi[1], hg], dbb, writes=[dbb])
                            kb.op("dve", lambda e, Sa=Sa, dbb=dbb, tsb=tsb: e.scalar_tensor_tensor(
                                out=tsb.t[:, 0:256], in0=Sa.t[:, 0:256], scalar=0.125, in1=dbb.t[:, 0:256], op0=ALU.mult, op1=ALU.add),
                                reads=[Sa, dbb], writes=[(tsb, True)])
                            kb.op("dve", lambda e, Sb=Sb, dbb=dbb, tsb=tsb: e.scalar_tensor_tensor(
                                out=tsb.t[:, 256:512], in0=Sb.t[:, 0:256], scalar=0.125, in1=dbb.t[:, 256:512], op0=ALU.mult, op1=ALU.add),
                                reads=[Sb, dbb], writes=[(tsb, True)])
                            kb.op("act", lambda e, tsb=tsb, P=P: e.activation(out=P.t[:], in_=tsb.t[:], func=AF.Exp),
                                  reads=[tsb], writes=[P])
                        for sl, i in enumerate(ORDER):
                            h = hg * 4 + i
                            kb.mm(OD.t[0:65, sl * 128:(sl + 1) * 128], vdx.t[:, e_ * 8 + h, :], P.t[:, sl * 128:(sl + 1) * 128],
                                  ki == 0 and sl == 0, ki == len(keys) - 1, [vdx, P], OD, inc=(sl == 3))
                    finalize_cd(OD, None, catD[hg * 4:(hg + 1) * 4, :, qtk], catDb)
            kb.barrier()

        if "O" in do:
          with ExitStack() as pes:
            wab = kb.sbuf("wab", [128, 8, D], BF16, pes)
            wcd = kb.sbuf("wcd", [64, 16, D], BF16, pes)
            gb2 = kb.sbuf("gb2", [128, 2, D], F32, pes)
            lbc = kb.sbuf("lbc", [128, 2, D], F32, pes)
            xg = kb.sbuf("xg", [128, 4, D], F32, pes)
            xgb = [Buf("xg%d" % i) for i in range(4)]
            cA = kb.sbuf("cA", [128, 4, 512], BF16, pes)
            cB = kb.sbuf("cB", [128, 4, 512], BF16, pes)
            cC = kb.sbuf("cC", [64, 8, 512], BF16, pes)
            cD = kb.sbuf("cD", [64, 8, 512], BF16, pes)
            tmp = [kb.sbuf("tmp%d" % i, [128, 512], F32, pes) for i in range(2)]
            stats = [kb.sbuf("stats%d" % i, [128, 4, 6], F32, pes) for i in range(2)]
            mv = [kb.sbuf("mv%d" % i, [128, 2], F32, pes) for i in range(2)]
            rs = [kb.sbuf("rs%d" % i, [128, 2], F32, pes) for i in range(2)]
            for c4 in range(2):
                kb.dma("pool", wab.t[:, c4 * 4:(c4 + 1) * 4, :], w_ab[c4 * 512:(c4 + 1) * 512, :].rearrange("(c p) n -> p c n", p=128),
                       wab, writes=[(wab, True)])
            for c4 in range(4):
                kb.dma("pool", wcd.t[:, c4 * 4:(c4 + 1) * 4, :], w_cd[c4 * 256:(c4 + 1) * 256, :].rearrange("(h p) n -> p h n", p=64),
                       wcd, writes=[(wcd, True)])
            for i in range(2):
                kb.dma("sp", gb2.t[:, i, :], gate[i:i + 1, :].partition_broadcast(128), gb2, writes=[(gb2, True)])
                kb.dma("sp", lbc.t[:, i, :], lnp[i:i + 1, :].partition_broadcast(128), lbc, writes=[(lbc, True)])
            ny = 0
            for (t0, ntl) in GROUPS:
                ntok = ntl * 128
                s = 1 if t0 >= 16 else 0
                tk = slice(t0 * 128, t0 * 128 + ntok)
                kb.dma("sp", xg.t[:, 0:ntl, :], xin[tk, :].rearrange("(t p) c -> p t c", p=128), xg, writes=xgb[:ntl])
                kb.dma("sp", cA.t[:, :, :ntok], catA[:, :, tk].rearrange("c p t -> p c t"), cA, reads=[catAb], writes=[cA])
                kb.dma("sp", cB.t[:, :, :ntok], fm[FM_YB:FM_YB + 4, :, tk].rearrange("c p t -> p c t"), cB, writes=[cB])
                kb.dma("sp", cC.t[:, :, :ntok], catC[:, :, tk].rearrange("h p t -> p h t"), cC, reads=[catCb], writes=[cC])
                kb.dma("sp", cD.t[:, :, :ntok], catD[:, :, tk].rearrange("h p t -> p h t"), cD, reads=[catDb], writes=[cD])
                for ti in range(ntl):
                    tt = slice(ti * 128, (ti + 1) * 128)
                    for n in range(4):
                        ncs = slice(n * 512, (n + 1) * 512)
                        Y = ps[ny % 4]
                        tm_ = tmp[ny % 2]
                        ny += 1
                        pairs = [(cA.t[:, c, tt], wab.t[:, c, ncs]) for c in range(4)]
                        pairs += [(cB.t[:, c, tt], wab.t[:, 4 + c, ncs]) for c in range(4)]
                        pairs += [(cC.t[0:64, h, tt], wcd.t[0:64, h, ncs]) for h in range(8)]
                        pairs += [(cD.t[0:64, h, tt], wcd.t[0:64, 8 + h, ncs]) for h in range(8)]
                        kb.mm_group(Y.t[:], pairs, reads=[cA, cB, cC, cD, wab, wcd], out_buf=Y)
                        kb.op("dve", lambda e, Y=Y, tm_=tm_, ncs=ncs: e.tensor_tensor(out=tm_.t[:], in0=Y.t[:], in1=gb2.t[:, s, ncs], op=ALU.mult),
                              reads=[Y, gb2], writes=[tm_])
                        kb.op("dve", lambda e, tm_=tm_, ti=ti, ncs=ncs: e.scalar_tensor_tensor(
                            out=xg.t[:, ti, ncs], in0=xg.t[:, ti, ncs], scalar=ALPHA, in1=tm_.t[:], op0=ALU.mult, op1=ALU.add),
                            reads=[tm_, xgb[ti]], writes=[(xgb[ti], True)])
                    ln_epilogue(kb, xg.t[:, ti, :], xgb[ti], stats[ti % 2], mv[ti % 2], rs[ti % 2], lbc.t[:, 0, :], lbc.t[:, 1, :], lbc)
                    kb.dma("sp", xout[(t0 + ti) * 128:(t0 + ti + 1) * 128, :], xg.t[:, ti, :], xg, reads=[xgb[ti]], final=True)
            kb.barrier()
        kb.finish()
    return nc


MCOLS = 2304


def build_mod():
    nc = bass.Bass("TRN2", target_bir_lowering=False)
    cc = nc.dram_tensor("cc", [128, 32], F32, kind="ExternalInput").ap()
    wm = nc.dram_tensor("wm", [4, D, MCOLS], F32, kind="ExternalInput").ap()
    bm = nc.dram_tensor("bm", [1, 4 * MCOLS], F32, kind="ExternalInput").ap()
    mo = nc.dram_tensor("mo", [2, 4 * MCOLS], F32, kind="ExternalOutput").ap()
    with ExitStack() as es:
        kb = KB(nc, es)
        c_sb = kb.sbuf("c_sb", [128, 16, 2], F32)
        bb = kb.sbuf("bb", [2, 4 * MCOLS], F32)
        ob = kb.sbuf("ob", [2, 4 * MCOLS], F32)
        wbuf = [kb.sbuf("wbuf%d" % i, [128, 16, 512], F32) for i in range(2)]
        pm = [kb.psum("pm%d" % i, [128, 512]) for i in range(2)]
        kb.dma("sp", c_sb.t[:].rearrange("p a b -> p (a b)"), cc, c_sb, writes=[c_sb])
        kb.dma("sp", bb.t[:], bm[0:1, :].partition_broadcast(2), bb, writes=[bb])
        kb.op("act", lambda e: e.activation(out=c_sb.t[:], in_=c_sb.t[:], func=AF.Silu), reads=[c_sb], writes=[c_sb])
        n = 0
        for l in range(4):
            for c0 in range(0, MCOLS, 512):
                ncol = min(512, MCOLS - c0)
                wb_, pb_ = wbuf[n % 2], pm[n % 2]
                n += 1
                for k4 in range(4):
                    kb.dma("sp", wb_.t[:, k4 * 4:(k4 + 1) * 4, :ncol],
                           wm[l, k4 * 512:(k4 + 1) * 512, c0:c0 + ncol].rearrange("(kc p) c -> p kc c", p=128),
                           wb_, writes=[(wb_, True)])
                kb.mm_group(pb_.t[0:2, :ncol], [(c_sb.t[:, kc, :], wb_.t[:, kc, :ncol]) for kc in range(16)],
                            reads=[c_sb, wb_], out_buf=pb_)
                o0 = l * MCOLS + c0
                kb.op("dve", lambda e, pb_=pb_, o0=o0, ncol=ncol: e.tensor_tensor(
                    out=ob.t[:, o0:o0 + ncol], in0=pb_.t[0:2, :ncol], in1=bb.t[:, o0:o0 + ncol], op=ALU.add),
                    reads=[pb_, bb], writes=[(ob, True)])
        kb.dma("sp", mo, ob.t[:], ob, reads=[ob], final=True)
        kb.finish()
    return nc


NCORES = 8
_PROGS = {}


def _prog(name):
    if name not in _PROGS:
        _PROGS[name] = {"M": build_mod, "F": build_ffn, "J": build_proj, "T": build_attn}[name]()
    return _PROGS[name]


def _run(name, in_maps):
    import time, os, sys
    t0 = time.time()
    res = run_bass_kernel_spmd(_prog(name), in_maps, core_ids=list(range(NCORES)))
    if os.environ.get("K_TIMING"):
        nb = sum(v.nbytes for m in in_maps for v in m.values())
        print("[launch %s] %.1fs  in=%.0fMB" % (name, time.time() - t0, nb / 1e6), file=sys.stderr, flush=True)
    return res.results


def d_bias_tables(rpb, core):
    out = np.full((5, 7, 2, 128, 512), -30000.0, np.float32)
    ii = np.arange(128)
    for cls, tloc in enumerate((0, 1, 2, 14, 15)):
        tg = core * 16 + tloc
        rq = (2 * tg + ii // 64)[None, :]
        cq = (ii % 64)[None, :]
        rs = np.clip(rq - 4, 0, 256 - 8)
        cs = np.clip(cq - 8, 0, 64 - 16)
        for o in D_OFFS[cls]:
            kg = tg + o
            if kg < 0 or kg > 127:
                continue
            rk = (2 * kg + ii // 64)[:, None]
            ck = (ii % 64)[:, None]
            valid = (rk >= rs) & (rk < rs + 8) & (ck >= cs) & (ck < cs + 16)
            dr = np.clip(rk - rq + 7, 0, 14)
            dc = np.clip(ck - cq, -15, 15) + 15
            for hg in range(2):
                for sl, i in enumerate(ORDER):
                    g = rpb[hg * 4 + i][dr, dc]
                    out[cls, o + 3, hg, :, sl * 128:(sl + 1) * 128] = np.where(valid, g, np.float32(-30000.0))
    return out


def c_masks(core):
    i = np.arange(128)[:, None]
    j = np.arange(128)[None, :]
    mprev = np.tile((i >= j).astype(np.float32), (1, 4))
    mnext = np.tile((i <= j).astype(np.float32), (1, 4))
    z = np.zeros_like(mprev)
    return np.stack([z if core == 0 else mprev, mprev, mnext, z if core == NCORES - 1 else mnext])


def _ext_fm(fms, i, c0, nch):
    own = fms[i][c0:c0 + nch]
    z = np.zeros((nch, 128, 256), own.dtype)
    prev = fms[i - 1][c0:c0 + nch][:, :, 1792:2048] if i > 0 else z
    nxt = fms[i + 1][c0:c0 + nch][:, :, 0:256] if i < NCORES - 1 else z
    return np.ascontiguousarray(np.concatenate([own[:, :, 2048:2304], prev, own[:, :, :2048], nxt], axis=2))


def _ext_tm(tms, i, c0, ncol):
    own = tms[i][:, c0:c0 + ncol]
    z = np.zeros((256, ncol), own.dtype)
    prev = tms[i - 1][1792:2048, c0:c0 + ncol] if i > 0 else z
    nxt = tms[i + 1][0:256, c0:c0 + ncol] if i < NCORES - 1 else z
    return np.ascontiguousarray(np.concatenate([own[2048:2304], prev, own[:2048], nxt], axis=0))


def kernel(x, c, ctx, c_ctx, w_mod, b_mod, ln_g, ln_b, ffn1_w_in, ffn1_w_out, ffn2_w_in, ffn2_w_out,
           mix_w_in, mix_w_out, a_lambda, b_norm_g, b_norm_b, b_spatial_w, b_spatial_b, c_sink, d_rpb,
           _nlayers=4, _dump=None):
    f32 = np.float32
    x = np.asarray(x, f32)[0]
    ctx = np.asarray(ctx, f32)[0]
    ident = np.eye(128, dtype=f32)
    cc = np.stack([np.asarray(c, f32)[0].reshape(16, 128).T, np.asarray(c_ctx, f32).reshape(16, 128).T], axis=2)
    cc = np.ascontiguousarray(cc.reshape(128, 32))
    w_mod = np.asarray(w_mod, f32)
    b_mod = np.asarray(b_mod, f32)
    ins = [{"cc": cc, "wm": np.ascontiguousarray(w_mod[:, :, i * MCOLS:(i + 1) * MCOLS]),
            "bm": np.ascontiguousarray(b_mod[:, i * MCOLS:(i + 1) * MCOLS]).reshape(1, 4 * MCOLS)} for i in range(NCORES)]
    mo = _run("M", ins)
    mods = np.concatenate([r["mo"].reshape(2, 4, MCOLS) for r in mo], axis=2)
    xs = [np.ascontiguousarray(np.concatenate([x[i * 2048:(i + 1) * 2048], ctx], 0)) for i in range(NCORES)]
    ropes = [rope_tables(i) for i in range(NCORES)]
    cms = [c_masks(i) for i in range(NCORES)]

    def ffn(xs, l, idx, lnk, w_in, w_out):
        modc = mod_cols(mods[0, l], mods[1, l], idx)
        lnp = np.ascontiguousarray(np.stack([ln_g[l, lnk], ln_b[l, lnk]]).astype(f32))
        w_in = np.ascontiguousarray(w_in, f32)
        w_out = np.ascontiguousarray(w_out, f32)
        r = _run("F", [{"xin": xs[i], "modc": modc, "lnp": lnp, "ident": ident, "w_in": w_in, "w_out": w_out}
                       for i in range(NCORES)])
        return [q["xout"] for q in r]

    if _dump is not None:
        _dump["mods"] = mods
    for l in range(_nlayers):
        xs = ffn(xs, l, (0, 1, 2), 0, ffn1_w_in[l], ffn1_w_out[l])
        if _dump is not None:
            _dump["x1_%d" % l] = xs
        w_fm, w_tm = build_proj_weights(np.asarray(mix_w_in[l], f32))
        modc = mod_cols(mods[0, l], mods[1, l], (3, 4, None))
        gnp = np.ascontiguousarray(np.stack([b_norm_g[l], b_norm_b[l]]).astype(f32))
        wsT = np.ascontiguousarray(np.transpose(np.asarray(b_spatial_w[l], f32), (2, 0, 1)).reshape(128, 512))
        bsr = np.ascontiguousarray(np.asarray(b_spatial_b[l], f32).reshape(1, 512))
        r = _run("J", [{"xin": xs[i], "modc": modc, "ident": ident, "w_fm": w_fm, "w_tm": w_tm, "rope": ropes[i],
                        "gnp": gnp, "wsT": wsT, "bsr": bsr} for i in range(NCORES)])
        fms = [q["fm"] for q in r]
        tms = [q["tm"] for q in r]
        ka = np.ascontiguousarray(np.concatenate([fms[0][FM_KA:FM_KA + 4][:, :, 2048:2304]] +
                                                 [fms[i][FM_KA:FM_KA + 4][:, :, :2048] for i in range(NCORES)], axis=2))
        va = np.ascontiguousarray(np.concatenate([tms[0][2048:2304, 0:512]] + [tms[i][:2048, 0:512] for i in range(NCORES)], axis=0))
        lam_init = 0.8 - 0.6 * np.exp(-0.3 * l)
        alam = np.concatenate([np.asarray(a_lambda[l], f32).reshape(256), np.array([lam_init, 1.0 - lam_init], f32)]).reshape(1, 258)
        alam = np.ascontiguousarray(alam.astype(f32))
        sink = np.ascontiguousarray(np.asarray(c_sink[l], f32).reshape(1, 8))
        m9 = mods[:, l].reshape(2, 9, D)
        gate = np.ascontiguousarray(m9[:, 5, :])
        lnp = np.ascontiguousarray(np.stack([ln_g[l, 1], ln_b[l, 1]]).astype(f32))
        wo = np.asarray(mix_w_out[l], f32)
        w_ab, w_cd = np.ascontiguousarray(wo[:1024]), np.ascontiguousarray(wo[1024:])
        rpb = np.asarray(d_rpb[l], f32)
        ins = []
        for i in range(NCORES):
            ins.append({"xin": xs[i], "fm": fms[i], "ka": ka, "va": va,
                        "kc": _ext_fm(fms, i, FM_KC, 2), "vc": _ext_tm(tms, i, 512, 128),
                        "kd": _ext_fm(fms, i, FM_KD, 4), "vd": _ext_tm(tms, i, 640, 512),
                        "cmask": cms[i], "dbias": d_bias_tables(rpb, i), "alam": alam, "sink": sink, "ident": ident,
                        "gate": gate, "lnp": lnp, "w_ab": w_ab, "w_cd": w_cd})
        r = _run("T", ins)
        xs = [q["xout"] for q in r]
        if _dump is not None:
            _dump["x2_%d" % l] = xs
            _dump["fm_%d" % l] = fms
        xs = ffn(xs, l, (6, 7, 8), 2, ffn2_w_in[l], ffn2_w_out[l])
    out = np.concatenate([xs[i][:2048] for i in range(NCORES)], axis=0)
    return np.ascontiguousarray(out[None].astype(f32))
```

```python
import os
import numpy as np
import concourse.bass as bass
import concourse.mybir as mybir
from concourse.bass_utils import run_bass_kernel_spmd
from contextlib import ExitStack

F32 = mybir.dt.float32
BF16 = mybir.dt.bfloat16
AF = mybir.ActivationFunctionType
ALU = mybir.AluOpType


class Buf:
    def __init__(self, name, t=None):
        self.name = name
        self.t = t
        self.w = {}
        self.r = {}
        self.dsem = None
        self.dcnt = 0


class KB:
    def __init__(self, nc, es):
        self.nc, self.es = nc, es
        self.e = {"pe": nc.tensor, "act": nc.scalar, "dve": nc.vector, "pool": nc.gpsimd, "sp": nc.sync}
        self.sem = {k: es.enter_context(nc.semaphore("s_" + k)) for k in ("pe", "act", "dve", "pool")}
        self.cnt = {k: 0 for k in self.sem}
        self.seen = {q: {} for q in self.e}
        self.finals = {}
        self.dbufs = {}
        self.pend = []
        self.pe_pend = []
        self.attach = False
        self.nbuf = 0

    def sbuf(self, name, shape, dtype, es=None):
        t = (es or self.es).enter_context(self.nc.sbuf_tensor(name, shape, dtype))
        return Buf(name, t)

    def psum(self, name, shape, dtype=F32):
        t = self.es.enter_context(self.nc.psum_tensor(name, shape, dtype))
        return Buf(name, t)

    def _wait(self, q, key, sem, val):
        if key == "pe" and q == "pe":
            return
        if key in self.dbufs:
            val = self.dbufs[key].dcnt
        if self.seen[q].get(key, 0) >= val:
            return
        self.pend.append((q, sem, val))
        self.seen[q][key] = val

    def _flush(self, q, ins=None):
        pend, self.pend = self.pend, []
        if ins is not None and pend and self.attach:
            last = pend.pop()
        else:
            last = None
        for (qq, sem, val) in pend:
            self.e[qq].wait_ge(sem, val)
        return last

    def _pre(self, q, reads, writes):
        for b in reads:
            for k, (s, v) in b.w.items():
                self._wait(q, k, s, v)
        for w in writes:
            b, part = (w if isinstance(w, tuple) else (w, False))
            if part and not b.r:
                continue
            for k, (s, v) in b.r.items():
                self._wait(q, k, s, v)
            for k, (s, v) in b.w.items():
                self._wait(q, k, s, v)

    def _post(self, key, sem, val, reads, writes):
        for w in writes:
            b, part = (w if isinstance(w, tuple) else (w, False))
            if (not part) or b.r:
                b.w = {}
                b.r = {}
            b.w[key] = (sem, val)
        for b in reads:
            if any((w[0] if isinstance(w, tuple) else w) is b for w in writes):
                continue
            b.r[key] = (sem, val)

    def op(self, q, fn, reads=(), writes=()):
        self._pre(q, reads, writes)
        self._flush(q)
        ins = fn(self.e[q])
        self.cnt[q] += 1
        ins.then_inc(self.sem[q], 1)
        self._post(q, self.sem[q], self.cnt[q], reads, writes)
        return ins

    def mm_group(self, out_ap, pairs, reads, out_buf, part=False):
        wr = [(out_buf, part)]
        self._pre("pe", reads, wr)
        self._flush("pe")
        n = len(pairs)
        ins = None
        for i, (l, r) in enumerate(pairs):
            ins = self.nc.tensor.matmul(out_ap, lhsT=l, rhs=r, start=(i == 0), stop=(i == n - 1))
        self.cnt["pe"] += 1
        ins.then_inc(self.sem["pe"], 1)
        self._post("pe", self.sem["pe"], self.cnt["pe"], reads, wr)

    def mm(self, out_ap, lhsT, rhs, start, stop, reads, out_buf, inc=True):
        wr = [(out_buf, True)]
        self._pre("pe", reads, wr)
        self._flush("pe")
        ins = self.nc.tensor.matmul(out_ap, lhsT=lhsT, rhs=rhs, start=start, stop=stop)
        self.pe_pend.append((reads, wr))
        if inc:
            self.cnt["pe"] += 1
            ins.then_inc(self.sem["pe"], 1)
            for (r, w) in self.pe_pend:
                self._post("pe", self.sem["pe"], self.cnt["pe"], r, w)
            self.pe_pend = []
        return ins

    def barrier(self):
        assert not self.pe_pend
        for q in self.e:
            for k in self.sem:
                if self.cnt[k]:
                    self._wait(q, k, self.sem[k], self.cnt[k])
            for k, b in self.dbufs.items():
                self._wait(q, k, b.dsem, b.dcnt)
            self._flush(q)

    def dma(self, q, out_ap, in_ap, sb, reads=(), writes=(), final=False):
        self._pre(q, reads, writes)
        self._flush(q)
        if sb.dsem is None:
            self.nbuf += 1
            sb.dsem = self.es.enter_context(self.nc.semaphore("d%d_%s" % (self.nbuf, sb.name)))
        ins = self.e[q].dma_start(out=out_ap, in_=in_ap)
        sb.dcnt += 16
        ins.then_inc(sb.dsem, 16)
        key = "d_" + sb.name
        self.dbufs[key] = sb
        self._post(key, sb.dsem, sb.dcnt, reads, writes)
        if final:
            self.finals[key] = (sb.dsem, sb.dcnt)
        return ins

    def finish(self):
        for k, (s, v) in self.finals.items():
            self._wait("sp", k, s, v)
        self._flush("sp")


D = 2048
DFF = 5632
NT_TILES = 18
NTOK = NT_TILES * 128
ALPHA = 8 ** 0.25
LN_EPS = 1e-6
GROUPS = [(0, 4), (4, 4), (8, 4), (12, 4), (16, 2)]


def ln_epilogue(kb, xt, xb, stats, mv, rs, gbc, bbc, gb):
    nc = kb.nc
    for q4 in range(4):
        kb.op("dve", lambda e, q4=q4: e.bn_stats(out=stats.t[:, q4, :], in_=xt[:, q4 * 512:(q4 + 1) * 512]),
              reads=[xb], writes=[(stats, True)])
    kb.op("dve", lambda e: e.bn_aggr(out=mv.t[:], in_=stats.t[:].rearrange("p a b -> p (a b)")), reads=[stats], writes=[mv])
    kb.op("act", lambda e: e.activation(out=rs.t[:, 0:1], in_=mv.t[:, 1:2], func=AF.Sqrt, bias=LN_EPS, scale=1.0),
          reads=[mv], writes=[rs])
    kb.op("dve", lambda e: e.reciprocal(out=rs.t[:, 0:1], in_=rs.t[:, 0:1]), reads=[rs], writes=[rs])
    kb.op("dve", lambda e: e.tensor_scalar(out=rs.t[:, 1:2], in0=mv.t[:, 0:1], scalar1=rs.t[:, 0:1], scalar2=-1.0,
                                            op0=ALU.mult, op1=ALU.mult), reads=[mv, rs], writes=[rs])
    kb.op("act", lambda e: e.activation(out=xt, in_=xt, func=AF.Identity, bias=rs.t[:, 1:2], scale=rs.t[:, 0:1]),
          reads=[xb, rs], writes=[xb])
    kb.op("pool", lambda e: e.tensor_tensor(out=xt, in0=xt, in1=gbc, op=ALU.mult), reads=[xb, gb], writes=[xb])
    kb.op("dve", lambda e: e.tensor_tensor(out=xt, in0=xt, in1=bbc, op=ALU.add), reads=[xb, gb], writes=[xb])


def build_ffn(phases=("tr", "in", "out", "ln"), groups=GROUPS):
    nc = bass.Bass("TRN2", target_bir_lowering=False)
    xin = nc.dram_tensor("xin", [NTOK, D], F32, kind="ExternalInput").ap()
    modc = nc.dram_tensor("modc", [128, 96], F32, kind="ExternalInput").ap()
    lnp = nc.dram_tensor("lnp", [2, D], F32, kind="ExternalInput").ap()
    ident = nc.dram_tensor("ident", [128, 128], F32, kind="ExternalInput").ap()
    w_in = nc.dram_tensor("w_in", [D, 2 * DFF], F32, kind="ExternalInput").ap()
    w_out = nc.dram_tensor("w_out", [DFF, D], F32, kind="ExternalInput").ap()
    xout = nc.dram_tensor("xout", [NTOK, D], F32, kind="ExternalOutput").ap()
    with ExitStack() as es:
        kb = KB(nc, es)
        mod = kb.sbuf("mod", [128, 2, 3, 16], F32)
        idb = kb.sbuf("ident_sb", [128, 128], F32)
        gbc = kb.sbuf("gbc", [128, 2, D], F32)
        xg = kb.sbuf("xg", [128, 4, D], F32)
        xgb = [Buf("xg%d" % i) for i in range(4)]
        hT = kb.sbuf("hT", [128, 16, 512], BF16)
        hid = kb.sbuf("hid", [128, 44, 512], BF16)
        wg = [kb.sbuf("wg%d" % i, [128, 16, 256], BF16) for i in range(2)]
        wu = [kb.sbuf("wu%d" % i, [128, 16, 256], BF16) for i in range(2)]
        wo = [kb.sbuf("wo%d" % i, [128, 44, 256], BF16) for i in range(2)]
        sg = [kb.sbuf("sg%d" % i, [128, 512], F32) for i in range(2)]
        yT = [kb.sbuf("yT%d" % i, [128, 512], F32) for i in range(2)]
        stats = [kb.sbuf("stats%d" % i, [128, 4, 6], F32) for i in range(2)]
        mv = [kb.sbuf("mv%d" % i, [128, 2], F32) for i in range(2)]
        rs = [kb.sbuf("rs%d" % i, [128, 2], F32) for i in range(2)]
        tpb = [kb.psum("tp%d" % i, [128, 512]) for i in range(2)]
        gps = [kb.psum("gp%d" % i, [128, 512]) for i in range(2)]
        ups = [kb.psum("up%d" % i, [128, 512]) for i in range(2)]
        yps = [kb.psum("yp%d" % i, [128, 512]) for i in range(2)]

        kb.dma("sp", mod.t[:].rearrange("p a b c -> p (a b c)"), modc, mod, writes=[mod])
        kb.dma("sp", idb.t[:], ident, idb, writes=[idb])
        kb.dma("sp", gbc.t[:, 0, :], lnp[0:1, :].partition_broadcast(128), gbc, writes=[(gbc, True)])
        kb.dma("sp", gbc.t[:, 1, :], lnp[1:2, :].partition_broadcast(128), gbc, writes=[(gbc, True)])
        kb.op("dve", lambda e: e.tensor_scalar(out=mod.t[:, :, 1, :], in0=mod.t[:, :, 1, :], scalar1=1.0, scalar2=None,
                                                op0=ALU.add), reads=[mod], writes=[mod])
        kb.op("dve", lambda e: e.tensor_scalar(out=mod.t[:, :, 2, :], in0=mod.t[:, :, 2, :], scalar1=0.5, scalar2=None,
                                                op0=ALU.mult), reads=[mod], writes=[mod])
        rot = {"tp": 0, "ev": 0}

        def next_tp():
            rot["tp"] ^= 1
            return tpb[rot["tp"]]

        def ev_eng():
            rot["ev"] ^= 1
            return "dve" if rot["ev"] else "act"

        xt = kb.sbuf("xt", [128, D], F32)

        def phase_a(t0, ntl):
            s = 1 if t0 >= 16 else 0
            for ti in range(ntl if "tr" in phases else 0):
                kb.dma("sp", xt.t[:], xin[(t0 + ti) * 128:(t0 + ti + 1) * 128, :], xt, writes=[xt])
                for kq in range(4):
                    tp = next_tp()
                    for k4 in range(4):
                        kc = kq * 4 + k4
                        kb.op("pe", lambda e, tp=tp, k4=k4, kc=kc: e.transpose(
                            out=tp.t[:, k4 * 128:(k4 + 1) * 128], in_=xt.t[:, kc * 128:(kc + 1) * 128], identity=idb.t[:]),
                            reads=[xt, idb], writes=[(tp, True)])
                    eng = ev_eng()
                    for k4 in range(4):
                        kc = kq * 4 + k4
                        if eng == "dve":
                            kb.op("dve", lambda e, tp=tp, k4=k4, kc=kc, ti=ti: e.tensor_scalar(
                                out=hT.t[:, kc, ti * 128:(ti + 1) * 128], in0=tp.t[:, k4 * 128:(k4 + 1) * 128],
                                scalar1=mod.t[:, s, 1, kc:kc + 1], scalar2=mod.t[:, s, 0, kc:kc + 1],
                                op0=ALU.mult, op1=ALU.add), reads=[tp, mod], writes=[(hT, True)])
                        else:
                            kb.op("act", lambda e, tp=tp, k4=k4, kc=kc, ti=ti: e.activation(
                                out=hT.t[:, kc, ti * 128:(ti + 1) * 128], in_=tp.t[:, k4 * 128:(k4 + 1) * 128],
                                func=AF.Identity, bias=mod.t[:, s, 0, kc:kc + 1], scale=mod.t[:, s, 1, kc:kc + 1]),
                                reads=[tp, mod], writes=[(hT, True)])

        def phase_b(t0, ntl):
            ntok = ntl * 128
            for jb in range(22 if "in" in phases else 0):
                wgb, wub = wg[jb % 2], wu[jb % 2]
                kb.dma("pool", wgb.t[:], w_in[:, jb * 256:(jb + 1) * 256].rearrange("(kc p) c -> p kc c", p=128),
                       wgb, writes=[wgb])
                kb.dma("pool", wub.t[:], w_in[:, DFF + jb * 256:DFF + (jb + 1) * 256].rearrange("(kc p) c -> p kc c", p=128),
                       wub, writes=[wub])
                for jj in range(2):
                    j = jb * 2 + jj
                    gp, up, sgb = gps[j % 2], ups[j % 2], sg[j % 2]
                    kb.mm_group(gp.t[:, :ntok], [(wgb.t[:, kc, jj * 128:(jj + 1) * 128], hT.t[:, kc, :ntok]) for kc in range(16)],
                                reads=[wgb, hT], out_buf=gp)
                    kb.mm_group(up.t[:, :ntok], [(wub.t[:, kc, jj * 128:(jj + 1) * 128], hT.t[:, kc, :ntok]) for kc in range(16)],
                                reads=[wub, hT], out_buf=up)
                    kb.op("act", lambda e, gp=gp, sgb=sgb: e.activation(out=sgb.t[:, :ntok], in_=gp.t[:, :ntok], func=AF.Silu),
                          reads=[gp], writes=[sgb])
                    kb.op("dve", lambda e, up=up, sgb=sgb, j=j: e.tensor_tensor(out=hid.t[:, j, :ntok], in0=sgb.t[:, :ntok],
                                                                              in1=up.t[:, :ntok], op=ALU.mult),
                          reads=[sgb, up], writes=[(hid, True)])

        def phase_c(t0, ntl):
            ntok = ntl * 128
            s = 1 if t0 >= 16 else 0
            kb.dma("sp", xg.t[:, 0:ntl, :], xin[t0 * 128:(t0 + ntl) * 128, :].rearrange("(t p) c -> p t c", p=128),
                   xg, writes=xgb[:ntl])
            for ob in range(8 if "out" in phases else 0):
                wob = wo[ob % 2]
                kb.dma("pool", wob.t[:], w_out[:, ob * 256:(ob + 1) * 256].rearrange("(j p) c -> p j c", p=128),
                       wob, writes=[wob])
                for oo in range(2):
                    oc = ob * 2 + oo
                    yp, yTb = yps[oc % 2], yT[oc % 2]
                    kb.mm_group(yp.t[:, :ntok], [(wob.t[:, j, oo * 128:(oo + 1) * 128], hid.t[:, j, :ntok]) for j in range(44)],
                                reads=[wob, hid], out_buf=yp)
                    kb.op("act", lambda e, yp=yp, yTb=yTb, oc=oc: e.activation(
                        out=yTb.t[:, :ntok], in_=yp.t[:, :ntok], func=AF.Identity, scale=mod.t[:, s, 2, oc:oc + 1]),
                        reads=[yp, mod], writes=[yTb])
                    tp = next_tp()
                    for ti in range(ntl):
                        kb.op("pe", lambda e, tp=tp, ti=ti, yTb=yTb: e.transpose(
                            out=tp.t[:, ti * 128:(ti + 1) * 128], in_=yTb.t[:, ti * 128:(ti + 1) * 128], identity=idb.t[:]),
                            reads=[yTb, idb], writes=[(tp, True)])
                    kb.op("dve", lambda e, tp=tp, oc=oc: e.scalar_tensor_tensor(
                        out=xg.t[:, 0:ntl, oc * 128:(oc + 1) * 128], in0=xg.t[:, 0:ntl, oc * 128:(oc + 1) * 128], scalar=ALPHA,
                        in1=tp.t[:, :ntok].rearrange("p (t c) -> p t c", c=128), op0=ALU.mult, op1=ALU.add),
                        reads=[tp] + xgb[:ntl], writes=[(b, True) for b in xgb[:ntl]])

        def phase_e(t0, ntl):
            for ti in range(ntl):
                if "ln" in phases:
                    ln_epilogue(kb, xg.t[:, ti, :], xgb[ti], stats[ti % 2], mv[ti % 2], rs[ti % 2],
                                gbc.t[:, 0, :], gbc.t[:, 1, :], gbc)
                kb.dma("sp", xout[(t0 + ti) * 128:(t0 + ti + 1) * 128, :], xg.t[:, ti, :], xg, reads=[xgb[ti]], final=True)

        ng = len(groups)
        phase_a(*groups[0])
        for gi in range(ng):
            phase_b(*groups[gi])
            phase_c(*groups[gi])
            if gi + 1 < ng:
                phase_a(*groups[gi + 1])
            phase_e(*groups[gi])
        kb.finish()
    return nc


NFM = 26
FM_QA, FM_KA, FM_QC, FM_KC, FM_QD, FM_KD, FM_YB = 0, 4, 8, 12, 14, 18, 22
NTM = 1152
WFM_COLS = 40 * 128
WTM_COLS = 1664
GELU_C = 0.7978845608028654


def load_x_transpose_mod(kb, xin, t0, ntl, s, xg, xgb, hT, idb, mod, tpb, rot):
    kb.dma("sp", xg.t[:, 0:ntl, :], xin[t0 * 128:(t0 + ntl) * 128, :].rearrange("(t p) c -> p t c", p=128),
           xg, writes=xgb[:ntl])
    for ti in range(ntl):
        for kq in range(4):
            rot["tp"] ^= 1
            tp = tpb[rot["tp"]]
            for k4 in range(4):
                kc = kq * 4 + k4
                kb.op("pe", lambda e, tp=tp, k4=k4, kc=kc, ti=ti: e.transpose(
                    out=tp.t[:, k4 * 128:(k4 + 1) * 128], in_=xg.t[:, ti, kc * 128:(kc + 1) * 128], identity=idb.t[:]),
                    reads=[xgb[ti], idb], writes=[(tp, True)])
            rot["ev"] ^= 1
            for k4 in range(4):
                kc = kq * 4 + k4
                if rot["ev"]:
                    kb.op("dve", lambda e, tp=tp, k4=k4, kc=kc, ti=ti: e.tensor_scalar(
                        out=hT.t[:, kc, ti * 128:(ti + 1) * 128], in0=tp.t[:, k4 * 128:(k4 + 1) * 128],
                        scalar1=mod.t[:, s, 1, kc:kc + 1], scalar2=mod.t[:, s, 0, kc:kc + 1],
                        op0=ALU.mult, op1=ALU.add), reads=[tp, mod], writes=[(hT, True)])
                else:
                    kb.op("act", lambda e, tp=tp, k4=k4, kc=kc, ti=ti: e.activation(
                        out=hT.t[:, kc, ti * 128:(ti + 1) * 128], in_=tp.t[:, k4 * 128:(k4 + 1) * 128],
                        func=AF.Identity, bias=mod.t[:, s, 0, kc:kc + 1], scale=mod.t[:, s, 1, kc:kc + 1]),
                        reads=[tp, mod], writes=[(hT, True)])


def build_proj(groups=GROUPS):
    nc = bass.Bass("TRN2", target_bir_lowering=False)
    xin = nc.dram_tensor("xin", [NTOK, D], F32, kind="ExternalInput").ap()
    modc = nc.dram_tensor("modc", [128, 96], F32, kind="ExternalInput").ap()
    ident = nc.dram_tensor("ident", [128, 128], F32, kind="ExternalInput").ap()
    w_fm = nc.dram_tensor("w_fm", [D, WFM_COLS], F32, kind="ExternalInput").ap()
    w_tm = nc.dram_tensor("w_tm", [D, WTM_COLS], F32, kind="ExternalInput").ap()
    rope = nc.dram_tensor("rope", [2, 128, NTOK], F32, kind="ExternalInput").ap()
    gnp = nc.dram_tensor("gnp", [2, 512], F32, kind="ExternalInput").ap()
    wsT = nc.dram_tensor("wsT", [128, 512], F32, kind="ExternalInput").ap()
    bsr = nc.dram_tensor("bsr", [1, 512], F32, kind="ExternalInput").ap()
    fm = nc.dram_tensor("fm", [NFM, 128, NTOK], BF16, kind="ExternalOutput").ap()
    tm = nc.dram_tensor("tm", [NTOK, NTM], BF16, kind="ExternalOutput").ap()
    with ExitStack() as es:
        kb = KB(nc, es)
        mod = kb.sbuf("mod", [128, 2, 3, 16], F32)
        idb = kb.sbuf("ident_sb", [128, 128], F32)
        rp = kb.sbuf("rope_sb", [128, 2, NTOK], F32)
        gn = kb.sbuf("gn", [128, 2, 512], F32)
        ws = kb.sbuf("ws", [128, 512], BF16)
        bsb = kb.sbuf("bsb", [128, 512], F32)
        xg = kb.sbuf("xg", [128, 4, D], F32)
        xgb = [Buf("xg%d" % i) for i in range(4)]
        hT = kb.sbuf("hT", [128, 16, 512], BF16)
        uT = kb.sbuf("uT", [128, 4, 512], BF16)
        wb = [kb.sbuf("wb%d" % i, [128, 16, 256], BF16) for i in range(2)]
        wt = [kb.sbuf("wt%d" % i, [128, 16, 512], BF16) for i in range(2)]
        t1 = [kb.sbuf("t1_%d" % i, [128, 512], F32) for i in range(2)]
        t2 = [kb.sbuf("t2_%d" % i, [128, 512], F32) for i in range(2)]
        stg = [kb.sbuf("stg%d" % i, [128, 512], BF16) for i in range(3)]
        vv = [kb.sbuf("vv%d" % i, [128, 512], F32) for i in range(2)]
        vt = [kb.sbuf("vt%d" % i, [128, 512], BF16) for i in range(2)]
        ybs = [kb.sbuf("ybs%d" % i, [128, 512], F32) for i in range(2)]
        stats = [kb.sbuf("stats%d" % i, [128, 6], F32) for i in range(2)]
        mv = [kb.sbuf("mv%d" % i, [128, 2], F32) for i in range(2)]
        rs = [kb.sbuf("rs%d" % i, [128, 2], F32) for i in range(2)]
        tpb = [kb.psum("tp%d" % i, [128, 512]) for i in range(2)]
        pp = [kb.psum("pp%d" % i, [128, 512]) for i in range(4)]
        pt = [kb.psum("pt%d" % i, [128, 512]) for i in range(2)]

        kb.dma("sp", mod.t[:].rearrange("p a b c -> p (a b c)"), modc, mod, writes=[mod])
        kb.dma("sp", idb.t[:], ident, idb, writes=[idb])
        kb.dma("sp", rp.t[:, 0, :], rope[0], rp, writes=[(rp, True)])
        kb.dma("sp", rp.t[:, 1, :], rope[1], rp, writes=[(rp, True)])
        kb.dma("sp", gn.t[:, 0, :], gnp[0:1, :].partition_broadcast(128), gn, writes=[(gn, True)])
        kb.dma("sp", gn.t[:, 1, :], gnp[1:2, :].partition_broadcast(128), gn, writes=[(gn, True)])
        kb.dma("sp", bsb.t[:], bsr[0:1, :].partition_broadcast(128), bsb, writes=[bsb])
        kb.dma("pool", ws.t[:], wsT, ws, writes=[ws])
        kb.op("dve", lambda e: e.tensor_scalar(out=mod.t[:, :, 1, :], in0=mod.t[:, :, 1, :], scalar1=1.0, scalar2=None,
                                                op0=ALU.add), reads=[mod], writes=[mod])
        rot = {"tp": 0, "ev": 0, "stg": 0, "wt": 0}

        def next_stg():
            rot["stg"] = (rot["stg"] + 1) % 3
            return stg[rot["stg"]]

        for (t0, ntl) in groups:
            ntok = ntl * 128
            s = 1 if t0 >= 16 else 0
            tk = slice(t0 * 128, t0 * 128 + ntok)
            load_x_transpose_mod(kb, xin, t0, ntl, s, xg, xgb, hT, idb, mod, tpb, rot)
            for blk in range(20):
                wbb = wb[blk % 2]
                kb.dma("pool", wbb.t[:], w_fm[:, blk * 256:(blk + 1) * 256].rearrange("(kc p) c -> p kc c", p=128),
                       wbb, writes=[wbb])
                pa, pb = pp[(blk % 2) * 2], pp[(blk % 2) * 2 + 1]
                for jj, pq in ((0, pa), (1, pb)):
                    kb.mm_group(pq.t[:, :ntok], [(wbb.t[:, kc, jj * 128:(jj + 1) * 128], hT.t[:, kc, :ntok]) for kc in range(16)],
                                reads=[wbb, hT], out_buf=pq)
                if blk < 14:
                    oc = (FM_QA + blk) if blk < 4 else (FM_KA + blk - 4) if blk < 8 else (FM_QC + blk - 8) if blk < 12 \
                        else (FM_KC + blk - 12)
                    a1, a2, sg_ = t1[blk % 2], t2[blk % 2], next_stg()
                    kb.op("dve", lambda e, pa=pa, a1=a1: e.tensor_tensor(out=a1.t[:, :ntok], in0=pa.t[:, :ntok], in1=rp.t[:, 0, tk],
                                                                           op=ALU.mult), reads=[pa, rp], writes=[a1])
                    kb.op("dve", lambda e, pb=pb, a2=a2: e.tensor_tensor(out=a2.t[:, :ntok], in0=pb.t[:, :ntok], in1=rp.t[:, 1, tk],
                                                                           op=ALU.mult), reads=[pb, rp], writes=[a2])
                    kb.op("pool", lambda e, a1=a1, a2=a2, sg_=sg_: e.tensor_tensor(out=sg_.t[:, :ntok], in0=a1.t[:, :ntok],
                                                                                      in1=a2.t[:, :ntok], op=ALU.add),
                          reads=[a1, a2], writes=[sg_])
                    kb.dma("sp", fm[oc, :, tk], sg_.t[:, :ntok], sg_, reads=[sg_], final=True)
                elif blk < 18:
                    for jj, pq in ((0, pa), (1, pb)):
                        ci = (blk - 14) * 2 + jj
                        oc = (FM_QD + ci) if ci < 4 else (FM_KD + ci - 4)
                        sg_ = next_stg()
                        kb.op("act", lambda e, pq=pq, sg_=sg_: e.activation(out=sg_.t[:, :ntok], in_=pq.t[:, :ntok], func=AF.Identity),
                              reads=[pq], writes=[sg_])
                        kb.dma("sp", fm[oc, :, tk], sg_.t[:, :ntok], sg_, reads=[sg_], final=True)
                else:
                    for jj, pq in ((0, pa), (1, pb)):
                        ci = (blk - 18) * 2 + jj
                        kb.op("act", lambda e, pq=pq, ci=ci: e.activation(out=uT.t[:, ci, :ntok], in_=pq.t[:, :ntok],
                                                                            func=AF.Gelu_apprx_tanh), reads=[pq], writes=[(uT, True)])
            for bi, (c0, ncol) in enumerate(((0, 512), (512, 512), (1024, 512), (1536, 128))):
                rot["wt"] ^= 1
                wtb = wt[rot["wt"]]
                kb.dma("pool", wtb.t[:, :, :ncol], w_tm[:, c0:c0 + ncol].rearrange("(kc p) c -> p kc c", p=128),
                       wtb, writes=[wtb])
                for ti in range(ntl):
                    ptb = pt[ti % 2]
                    tks = slice((t0 + ti) * 128, (t0 + ti + 1) * 128)
                    kb.mm_group(ptb.t[:, :ncol], [(hT.t[:, kc, ti * 128:(ti + 1) * 128], wtb.t[:, kc, :ncol]) for kc in range(16)],
                                reads=[wtb, hT], out_buf=ptb)
                    if bi == 0:
                        v_, vb_, st_, mv_, rs_, yb_ = vv[ti % 2], vt[ti % 2], stats[ti % 2], mv[ti % 2], rs[ti % 2], ybs[ti % 2]
                        kb.op("act", lambda e, ptb=ptb, v_=v_: e.activation(out=v_.t[:], in_=ptb.t[:], func=AF.Gelu_apprx_tanh),
                              reads=[ptb], writes=[v_])
                        kb.op("dve", lambda e, v_=v_, st_=st_: e.bn_stats(out=st_.t[:], in_=v_.t[:]), reads=[v_], writes=[st_])
                        kb.op("dve", lambda e, st_=st_, mv_=mv_: e.bn_aggr(out=mv_.t[:], in_=st_.t[:]), reads=[st_], writes=[mv_])
                        kb.op("act", lambda e, mv_=mv_, rs_=rs_: e.activation(out=rs_.t[:, 0:1], in_=mv_.t[:, 1:2], func=AF.Sqrt,
                                                                                bias=LN_EPS, scale=1.0), reads=[mv_], writes=[rs_])
                        kb.op("dve", lambda e, rs_=rs_: e.reciprocal(out=rs_.t[:, 0:1], in_=rs_.t[:, 0:1]), reads=[rs_], writes=[rs_])
                        kb.op("dve", lambda e, rs_=rs_, mv_=mv_: e.tensor_scalar(out=rs_.t[:, 1:2], in0=mv_.t[:, 0:1],
                                                                                   scalar1=rs_.t[:, 0:1], scalar2=-1.0,
                                                                                   op0=ALU.mult, op1=ALU.mult),
                              reads=[mv_, rs_], writes=[rs_])
                        kb.op("act", lambda e, v_=v_, rs_=rs_: e.activation(out=v_.t[:], in_=v_.t[:], func=AF.Identity,
                                                                              bias=rs_.t[:, 1:2], scale=rs_.t[:, 0:1]),
                              reads=[v_, rs_], writes=[v_])
                        kb.op("pool", lambda e, v_=v_: e.tensor_tensor(out=v_.t[:], in0=v_.t[:], in1=gn.t[:, 0, :], op=ALU.mult),
                              reads=[v_, gn], writes=[v_])
                        kb.op("dve", lambda e, v_=v_, vb_=vb_: e.tensor_tensor(out=vb_.t[:], in0=v_.t[:], in1=gn.t[:, 1, :], op=ALU.add),
                              reads=[v_, gn], writes=[vb_])
                        rot["tp"] ^= 1
                        mp = tpb[rot["tp"]]
                        for g in range(4):
                            kb.mm_group(mp.t[:, g * 128:(g + 1) * 128], [(vb_.t[:, g * 128:(g + 1) * 128], ws.t[:, g * 128:(g + 1) * 128])],
                                        reads=[vb_, ws], out_buf=mp, part=True)
                        kb.op("dve", lambda e, mp=mp, yb_=yb_: e.tensor_tensor(out=yb_.t[:], in0=mp.t[:], in1=bsb.t[:], op=ALU.add),
                              reads=[mp, bsb], writes=[yb_])
                        sg_ = next_stg()
                        kb.op("pool", lambda e, yb_=yb_, sg_=sg_, ti=ti: e.tensor_tensor(
                            out=sg_.t[:].rearrange("p (g t) -> p g t", g=4), in0=yb_.t[:].rearrange("p (g t) -> p g t", g=4),
                            in1=uT.t[:, :, ti * 128:(ti + 1) * 128], op=ALU.mult), reads=[yb_, uT], writes=[sg_])
                        kb.dma("sp", fm[FM_YB:FM_YB + 4, :, tks].rearrange("g p t -> p g t"),
                               sg_.t[:].rearrange("p (g t) -> p g t", g=4), sg_, reads=[sg_], final=True)
                    else:
                        oc0 = {1: 0, 2: 640, 3: 512}[bi]
                        sg_ = next_stg()
                        if ti % 2:
                            kb.op("act", lambda e, ptb=ptb, sg_=sg_: e.activation(out=sg_.t[:, :ncol], in_=ptb.t[:, :ncol], func=AF.Identity),
                                  reads=[ptb], writes=[sg_])
                        else:
                            kb.op("dve", lambda e, ptb=ptb, sg_=sg_: e.tensor_copy(out=sg_.t[:, :ncol], in_=ptb.t[:, :ncol]),
                                  reads=[ptb], writes=[sg_])
                        kb.dma("sp", tm[tks, oc0:oc0 + ncol], sg_.t[:, :ncol], sg_, reads=[sg_], final=True)
        kb.finish()
    return nc


PROJ_CUTS = np.cumsum([512, 512, 512, 512, 512, 512, 128, 128, 512, 512, 512])[:-1].tolist()


def swap_cols(w):
    k, n = w.shape
    return np.ascontiguousarray(w.reshape(k, n // 64, 2, 32)[:, :, ::-1, :]).reshape(k, n)


def build_proj_weights(w):
    aq, ak, av, bu, bv, cq, ck, cv, dq, dk, dv = np.split(w, PROJ_CUTS, axis=1)
    chunks = []

    def pairs(m):
        ms = swap_cols(m)
        for i in range(m.shape[1] // 128):
            chunks.append(m[:, i * 128:(i + 1) * 128])
            chunks.append(ms[:, i * 128:(i + 1) * 128])

    pairs(aq)
    pairs(ak)
    pairs(cq)
    pairs(np.concatenate([ck[:, :64], ck[:, :64], ck[:, 64:], ck[:, 64:]], 1))
    for m in (dq, dk, bu):
        for i in range(4):
            chunks.append(m[:, i * 128:(i + 1) * 128])
    w_fm = np.ascontiguousarray(np.concatenate(chunks, 1))
    w_tm = np.ascontiguousarray(np.concatenate([bv, av, dv, cv], 1))
    return w_fm, w_tm


def rope_tables(core):
    t = np.arange(core * 2048, (core + 1) * 2048)
    row = (t // 64).astype(np.float32)
    col = (t % 64).astype(np.float32)
    inv = (np.float32(10000.0) ** (-np.arange(16, dtype=np.float32) / np.float32(16))).astype(np.float32)
    ang = np.concatenate([row[:, None] * inv, col[:, None] * inv], -1).astype(np.float32)
    cos, sin = np.cos(ang).astype(np.float32), np.sin(ang).astype(np.float32)
    tab = np.zeros((2, 128, NTOK), np.float32)
    tab[0, :, 2048:] = 1.0
    for p in range(128):
        d = p % 64
        j = d % 32
        tab[0, p, :2048] = cos[:, j]
        tab[1, p, :2048] = -sin[:, j] if d < 32 else sin[:, j]
    return tab


def mod_cols(m_lat, m_ctx, idx):
    out = np.zeros((128, 2, 3, 16), np.float32)
    for s, m in enumerate((m_lat, m_ctx)):
        mm = m.reshape(9, 16, 128)
        for wi, i in enumerate(idx):
            if i is not None:
                out[:, s, wi, :] = mm[i].T
    return out.reshape(128, 96)


ORDER = (0, 2, 1, 3)
NKA = 256 + 16384
NEXT = 22 * 128
D_OFFS = {0: (-2, -1, 0, 1, 2, 3), 1: (-2, -1, 0, 1, 2), 2: (-2, -1, 0, 1, 2), 3: (-2, -1, 0, 1, 2), 4: (-3, -2, -1, 0, 1, 2)}


def d_class(T):
    return 0 if T == 0 else 1 if T == 1 else 3 if T == 14 else 4 if T == 15 else 2


def build_attn(do=("A", "C", "D", "O"), qgroups=GROUPS, dbg=False, stage=9, ctiles=18):
    nc = bass.Bass("TRN2", target_bir_lowering=False)
    xin = nc.dram_tensor("xin", [NTOK, D], F32, kind="ExternalInput").ap()
    fm = nc.dram_tensor("fm", [NFM, 128, NTOK], BF16, kind="ExternalInput").ap()
    ka = nc.dram_tensor("ka", [4, 128, NKA], BF16, kind="ExternalInput").ap()
    va = nc.dram_tensor("va", [NKA, 512], BF16, kind="ExternalInput").ap()
    kc = nc.dram_tensor("kc", [2, 128, NEXT], BF16, kind="ExternalInput").ap()
    vc = nc.dram_tensor("vc", [NEXT, 128], BF16, kind="ExternalInput").ap()
    kd = nc.dram_tensor("kd", [4, 128, NEXT], BF16, kind="ExternalInput").ap()
    vd = nc.dram_tensor("vd", [NEXT, 512], BF16, kind="ExternalInput").ap()
    cmask = nc.dram_tensor("cmask", [4, 128, 512], F32, kind="ExternalInput").ap()
    dbias = nc.dram_tensor("dbias", [5, 7, 2, 128, 512], F32, kind="ExternalInput").ap()
    alam = nc.dram_tensor("alam", [1, 258], F32, kind="ExternalInput").ap()
    sink = nc.dram_tensor("sink", [1, 8], F32, kind="ExternalInput").ap()
    ident = nc.dram_tensor("ident", [128, 128], F32, kind="ExternalInput").ap()
    gate = nc.dram_tensor("gate", [2, D], F32, kind="ExternalInput").ap()
    lnp = nc.dram_tensor("lnp", [2, D], F32, kind="ExternalInput").ap()
    w_ab = nc.dram_tensor("w_ab", [1024, D], F32, kind="ExternalInput").ap()
    w_cd = nc.dram_tensor("w_cd", [1024, D], F32, kind="ExternalInput").ap()
    xout = nc.dram_tensor("xout", [NTOK, D], F32, kind="ExternalOutput").ap()
    dk_ = {"kind": "ExternalOutput"} if dbg else {}
    catA = nc.dram_tensor("catA", [4, 128, NTOK], BF16, **dk_).ap()
    catC = nc.dram_tensor("catC", [8, 64, NTOK], BF16, **dk_).ap()
    catD = nc.dram_tensor("catD", [8, 64, NTOK], BF16, **dk_).ap()
    with ExitStack() as es:
        kb = KB(nc, es)
        catAb, catCb, catDb = Buf("catA"), Buf("catC"), Buf("catD")
        ps = [kb.psum("ps%d" % i, [128, 512]) for i in range(8)]
        ones = kb.sbuf("ones", [128, 128], BF16)
        onesf = kb.sbuf("onesf", [128, 128], F32)
        lam = kb.sbuf("lam", [128, 264], F32)
        esk = kb.sbuf("esk", [128, 8], F32)
        kb.op("pool", lambda e: e.memset(ones.t[:], 1.0), writes=[ones])
        kb.op("pool", lambda e: e.memset(onesf.t[:], 1.0), writes=[onesf])
        kb.dma("sp", lam.t[:, 0:258], alam[0:1, :].partition_broadcast(128), lam, writes=[lam])
        kb.dma("sp", esk.t[:], sink[0:1, :].partition_broadcast(128), esk, writes=[esk])
        lsc = kb.sbuf("lsc", [128, 128], F32)
        kb.op("dve", lambda e: e.tensor_tensor(out=lsc.t[:, 0:64], in0=lam.t[:, 0:64], in1=lam.t[:, 64:128], op=ALU.mult),
              reads=[lam], writes=[lsc])
        kb.op("dve", lambda e: e.tensor_tensor(out=lsc.t[:, 64:128], in0=lam.t[:, 128:192], in1=lam.t[:, 192:256], op=ALU.mult),
              reads=[lam, lsc], writes=[lsc])
        kb.op("dve", lambda e: e.reduce_sum(out=lam.t[:, 258:260], in_=lsc.t[:].rearrange("p (a b) -> p a b", a=2),
                                            axis=mybir.AxisListType.X), reads=[lsc, lam], writes=[lam])
        kb.op("act", lambda e: e.activation(out=lam.t[:, 258:260], in_=lam.t[:, 258:260], func=AF.Exp), reads=[lam], writes=[lam])
        kb.op("act", lambda e: e.activation(out=esk.t[:], in_=esk.t[:], func=AF.Exp), reads=[esk], writes=[esk])
        kb.op("dve", lambda e: e.tensor_tensor(out=lam.t[:, 260:261], in0=lam.t[:, 259:260], in1=lam.t[:, 258:259], op=ALU.subtract),
              reads=[lam], writes=[lam])
        kb.op("dve", lambda e: e.tensor_tensor(out=lam.t[:, 260:261], in0=lam.t[:, 260:261], in1=lam.t[:, 256:257], op=ALU.subtract),
              reads=[lam], writes=[lam])
        NLAM, OML = lam.t[:, 260:261], lam.t[:, 257:258]

        if "A" in do:
          with ExitStack() as pes:
            qa = kb.sbuf("qa", [128, 512], BF16, pes)
            kblk = [kb.sbuf("kblk%d" % i, [128, 2048], BF16, pes) for i in range(2)]
            vblk = [kb.sbuf("vblk%d" % i, [128, 16, 128], BF16, pes) for i in range(2)]
            pt_ = [[kb.sbuf("p%d_%d" % (c, i), [128, 512], BF16, pes) for i in range(2)] for c in range(2)]
            r0 = kb.sbuf("r0", [128, 512], F32, pes)
            r1 = kb.sbuf("r1", [128, 512], F32, pes)
            a0 = kb.sbuf("a0", [128, 512], F32, pes)
            a1 = kb.sbuf("a1", [128, 512], F32, pes)
            sq = kb.sbuf("sq", [128, 512], F32, pes)
            yst = [kb.sbuf("yst%d" % i, [128, 512], BF16, pes) for i in range(2)]
            O0, O1, Z0, Z1 = ps[4], ps[5], ps[6], ps[7]
            nblk = 0
            nst = 0
            for h in range(4):
                for (t0, ntl) in qgroups:
                    nq = ntl * 128
                    tk = slice(t0 * 128, t0 * 128 + nq)
                    isctx = t0 >= 16
                    kb.dma("sp", qa.t[:, :nq], fm[FM_QA + h, :, tk], qa, writes=[qa])
                    blocks = [(0, 256)] + ([] if isctx else [(256 + i * 2048, 2048) for i in range(8)])
                    units = []
                    for (k0, ksz) in blocks:
                        for kt in range(ksz // 128):
                            units.append((k0, ksz, kt))
                    nu = len(units)
                    cur = {}

                    def emit_scores(u):
                        nonlocal nblk
                        k0, ksz, kt = units[u]
                        if kt == 0:
                            kbb, vbb = kblk[nblk % 2], vblk[nblk % 2]
                            nblk += 1
                            kb.dma("sp", kbb.t[:, :ksz], ka[h, :, k0:k0 + ksz], kbb, writes=[kbb])
                            kb.dma("sp", vbb.t[:, :ksz // 128, :],
                                   va[k0:k0 + ksz, h * 128:(h + 1) * 128].rearrange("(t p) c -> p t c", p=128), vbb, writes=[vbb])
                            cur["kv"] = (kbb, vbb)
                        kbb, vbb = cur["kv"]
                        par = u % 2
                        S0, S1 = ps[par * 2], ps[par * 2 + 1]
                        kb.mm(S0.t[:, :nq], kbb.t[0:64, kt * 128:(kt + 1) * 128], qa.t[0:64, :nq], True, True, [kbb, qa], S0)
                        kb.mm(S1.t[:, :nq], kbb.t[64:128, kt * 128:(kt + 1) * 128], qa.t[64:128, :nq], True, True, [kbb, qa], S1)
                        cur[u] = (vbb, kt)

                    def emit_exp(u):
                        par = u % 2
                        S0, S1 = ps[par * 2], ps[par * 2 + 1]
                        P0, P1 = pt_[0][par], pt_[1][par]
                        kb.op("act", lambda e: e.activation(out=P0.t[:, :nq], in_=S0.t[:, :nq], func=AF.Exp, scale=0.125),
                              reads=[S0], writes=[P0])
                        kb.op("act", lambda e: e.activation(out=P1.t[:, :nq], in_=S1.t[:, :nq], func=AF.Exp, scale=0.125),
                              reads=[S1], writes=[P1])

                    def emit_pv(u):
                        par = u % 2
                        P0, P1 = pt_[0][par], pt_[1][par]
                        vbb, kt = cur.pop(u)
                        first, last = u == 0, u == nu - 1
                        kb.mm(O0.t[:, :nq], vbb.t[:, kt, :], P0.t[:, :nq], first, last, [vbb, P0], O0, inc=False)
                        kb.mm(Z0.t[:, :nq], ones.t[:], P0.t[:, :nq], first, last, [ones, P0], Z0, inc=False)
                        kb.mm(O1.t[:, :nq], vbb.t[:, kt, :], P1.t[:, :nq], first, last, [vbb, P1], O1, inc=False)
                        kb.mm(Z1.t[:, :nq], ones.t[:], P1.t[:, :nq], first, last, [ones, P1], Z1, inc=True)

                    emit_scores(0)
                    emit_exp(0)
                    for u in range(1, nu):
                        emit_scores(u)
                        emit_pv(u - 1)
                        emit_exp(u)
                    emit_pv(nu - 1)
                    kb.op("dve", lambda e: e.reciprocal(out=r0.t[:, :nq], in_=Z0.t[:, :nq]), reads=[Z0], writes=[r0])
                    kb.op("dve", lambda e: e.reciprocal(out=r1.t[:, :nq], in_=Z1.t[:, :nq]), reads=[Z1], writes=[r1])
                    kb.op("dve", lambda e: e.tensor_tensor(out=a0.t[:, :nq], in0=O0.t[:, :nq], in1=r0.t[:, :nq], op=ALU.mult),
                          reads=[O0, r0], writes=[a0])
                    kb.op("dve", lambda e: e.tensor_tensor(out=a1.t[:, :nq], in0=O1.t[:, :nq], in1=r1.t[:, :nq], op=ALU.mult),
                          reads=[O1, r1], writes=[a1])
                    kb.op("dve", lambda e: e.scalar_tensor_tensor(out=a0.t[:, :nq], in0=a1.t[:, :nq], scalar=NLAM, in1=a0.t[:, :nq],
                                                                  op0=ALU.mult, op1=ALU.add), reads=[a0, a1, lam], writes=[a0])
                    kb.op("pool", lambda e: e.tensor_tensor(out=sq.t[:, :nq], in0=a0.t[:, :nq], in1=a0.t[:, :nq], op=ALU.mult),
                          reads=[a0], writes=[sq])
                    MS = ps[0]
                    kb.mm(MS.t[:, :nq], onesf.t[:], sq.t[:, :nq], True, True, [onesf, sq], MS)
                    kb.op("act", lambda e, MS=MS: e.activation(out=r0.t[:, :nq], in_=MS.t[:, :nq], func=AF.Sqrt, bias=LN_EPS, scale=1.0 / 128),
                          reads=[MS], writes=[r0])
                    kb.op("dve", lambda e: e.reciprocal(out=r0.t[:, :nq], in_=r0.t[:, :nq]), reads=[r0], writes=[r0])
                    ys = yst[nst % 2]
                    nst += 1
                    kb.op("dve", lambda e, ys=ys: e.scalar_tensor_tensor(out=ys.t[:, :nq], in0=a0.t[:, :nq], scalar=OML, in1=r0.t[:, :nq],
                                                                         op0=ALU.mult, op1=ALU.mult), reads=[a0, r0, lam], writes=[ys])
                    kb.dma("sp", catA[h, :, tk], ys.t[:, :nq], ys, reads=[ys], writes=[(catAb, True)])
            kb.barrier()

        if "C" in do or "D" in do:
          with ExitStack() as pes:
            qc = kb.sbuf("qc", [128, 4, NTOK], BF16, pes)
            kcx = kb.sbuf("kcx", [128, 2, NEXT], BF16, pes)
            vcx = kb.sbuf("vcx", [128, 44, 65], BF16, pes)
            qd = kb.sbuf("qd", [128, 4, NTOK], BF16, pes)
            kdx = kb.sbuf("kdx", [128, 4, NEXT], BF16, pes)
            vdx = kb.sbuf("vdx", [128, 176, 65], BF16, pes)
            vstage = kb.sbuf("vstage", [128, 22 * 512], BF16, pes)
            cmk = kb.sbuf("cmk", [128, 4, 512], BF16, pes)
            db = [kb.sbuf("db%d" % i, [128, 512], F32, pes) for i in range(2)]
            tS = [kb.sbuf("tS%d" % i, [128, 512], F32, pes) for i in range(2)]
            pcd = [kb.sbuf("pcd%d" % i, [128, 512], BF16, pes) for i in range(2)]
            zr = kb.sbuf("zr", [128, 512], F32, pes)
            bcs = kb.sbuf("bcs", [128, 512], F32, pes)
            stgc = [kb.sbuf("stgc%d" % i, [128, 512], BF16, pes) for i in range(2)]
            kb.dma("sp", qc.t[:], fm[FM_QC:FM_QC + 4].rearrange("c p t -> p c t"), qc, writes=[qc])
            kb.dma("sp", qd.t[:], fm[FM_QD:FM_QD + 4].rearrange("c p t -> p c t"), qd, writes=[qd])
            kb.dma("sp", kcx.t[:], kc.rearrange("c p t -> p c t"), kcx, writes=[kcx])
            kb.dma("sp", kdx.t[:], kd.rearrange("c p t -> p c t"), kdx, writes=[kdx])
            kb.dma("pool", cmk.t[:], cmask.rearrange("m p c -> p m c"), cmk, writes=[cmk])
            kb.op("pool", lambda e: e.memset(vcx.t[:], 1.0), writes=[vcx])
            kb.op("pool", lambda e: e.memset(vdx.t[:], 1.0), writes=[vdx])
            kb.dma("sp", vstage.t[:, 0:22 * 128].rearrange("p (t c) -> p t c", c=128), vc.rearrange("(t p) c -> p t c", p=128),
                   vstage, writes=[vstage])
            kb.op("dve", lambda e: e.tensor_copy(out=vcx.t[:, :, 0:64], in_=vstage.t[:, 0:22 * 128].rearrange("p (a d) -> p a d", d=64)),
                  reads=[vstage], writes=[vcx])
            kb.dma("sp", vstage.t[:].rearrange("p (t c) -> p t c", c=512), vd.rearrange("(t p) c -> p t c", p=128),
                   vstage, writes=[vstage])
            kb.op("dve", lambda e: e.tensor_copy(out=vdx.t[:, :, 0:64], in_=vstage.t[:].rearrange("p (a d) -> p a d", d=64)),
                  reads=[vstage], writes=[vdx])
            cnt = {"s": 0, "o": 0, "p": 0, "b": 0, "g": 0, "db": 0}

            def finalize_cd(O, sink_cols, dst, dst_buf):
                if sink_cols is not None:
                    for sl, g in enumerate(ORDER):
                        kb.op("dve", lambda e, g=g, sl=sl: e.tensor_scalar(
                            out=zr.t[64:65, sl * 128:(sl + 1) * 128], in0=O.t[64:65, sl * 128:(sl + 1) * 128],
                            scalar1=esk.t[64:65, sink_cols + g:sink_cols + g + 1], scalar2=None, op0=ALU.add),
                            reads=[O, esk], writes=[(zr, True)])
                else:
                    kb.op("dve", lambda e: e.tensor_copy(out=zr.t[64:65, :], in_=O.t[64:65, :]), reads=[O], writes=[zr])
                kb.op("dve", lambda e: e.reciprocal(out=zr.t[64:65, :], in_=zr.t[64:65, :]), reads=[zr], writes=[zr])
                BC = ps[6 + cnt["b"] % 2]
                cnt["b"] += 1
                kb.mm(BC.t[0:64, :], onesf.t[64:65, 0:64], zr.t[64:65, :], True, True, [onesf, zr], BC)
                kb.op("act", lambda e: e.activation(out=bcs.t[0:64, :], in_=BC.t[0:64, :], func=AF.Identity), reads=[BC], writes=[bcs])
                sg_ = stgc[cnt["g"] % 2]
                cnt["g"] += 1
                kb.op("dve", lambda e: e.tensor_tensor(out=sg_.t[0:64, :].rearrange("p (a b t) -> p b a t", a=2, b=2),
                                                        in0=O.t[0:64, :].rearrange("p (b a t) -> p b a t", a=2, b=2),
                                                        in1=bcs.t[0:64, :].rearrange("p (b a t) -> p b a t", a=2, b=2), op=ALU.mult),
                      reads=[O, bcs], writes=[sg_])
                kb.dma("sp", dst.rearrange("g p t -> p g t"), sg_.t[0:64, :].rearrange("p (g t) -> p g t", g=4), sg_,
                       reads=[sg_], writes=[(dst_buf, True)])

            def run_pipeline(steps):
                n = len(steps)
                pend_fin = []
                steps[0]["scores"]()
                steps[0]["exp"]()
                for u in range(1, n):
                    steps[u]["scores"]()
                    steps[u - 1]["pv"]()
                    for f in pend_fin:
                        f()
                    pend_fin = [steps[u - 1]["fin"]] if "fin" in steps[u - 1] else []
                    steps[u]["exp"]()
                steps[n - 1]["pv"]()
                for f in pend_fin:
                    f()
                if "fin" in steps[n - 1]:
                    steps[n - 1]["fin"]()

            def c_step(T, kv, ki, nk, e_, mk, OC, u):
                qtk = slice(T * 128, (T + 1) * 128)
                Sa, Sb = ps[(u % 2) * 2], ps[(u % 2) * 2 + 1]
                P = pcd[u % 2]

                def scores():
                    for sl, g in enumerate(ORDER):
                        j = kv * 4 + g
                        ch, hf = j // 2, j % 2
                        Sx = Sa if hf == 0 else Sb
                        kb.mm(Sx.t[:, (sl % 2) * 128:(sl % 2 + 1) * 128], kcx.t[hf * 64:(hf + 1) * 64, kv, e_ * 128:(e_ + 1) * 128],
                              qc.t[hf * 64:(hf + 1) * 64, ch, qtk], True, True, [kcx, qc], Sx, inc=(sl % 2 == 1))

                def exp():
                    kb.op("act", lambda e: e.activation(out=P.t[:, 0:256], in_=Sa.t[:, 0:256], func=AF.Exp, scale=0.125),
                          reads=[Sa], writes=[(P, True)])
                    kb.op("act", lambda e: e.activation(out=P.t[:, 256:512], in_=Sb.t[:, 0:256], func=AF.Exp, scale=0.125),
                          reads=[Sb], writes=[(P, True)])
                    if mk is not None:
                        kb.op("dve", lambda e: e.tensor_tensor(out=P.t[:], in0=P.t[:], in1=cmk.t[:, mk, :], op=ALU.mult),
                              reads=[P, cmk], writes=[P])

                def pv():
                    kb.mm(OC.t[0:65, :], vcx.t[:, e_ * 2 + kv, :], P.t[:], ki == 0, ki == nk - 1, [vcx, P], OC)

                st = {"scores": scores, "exp": exp, "pv": pv}
                if ki == nk - 1:
                    st["fin"] = lambda: finalize_cd(OC, kv * 4, catC[kv * 4:(kv + 1) * 4, :, qtk], catCb)
                return st

            def d_step(T, hg, ki, nk, e_, bi, OD, u):
                qtk = slice(T * 128, (T + 1) * 128)
                Sa, Sb = ps[(u % 2) * 2], ps[(u % 2) * 2 + 1]
                P = pcd[u % 2]

                def scores():
                    for sl, i in enumerate(ORDER):
                        h = hg * 4 + i
                        ch, hf = h // 2, h % 2
                        Sx = Sa if hf == 0 else Sb
                        kb.mm(Sx.t[:, (sl % 2) * 128:(sl % 2 + 1) * 128], kdx.t[hf * 64:(hf + 1) * 64, ch, e_ * 128:(e_ + 1) * 128],
                              qd.t[hf * 64:(hf + 1) * 64, ch, qtk], True, True, [kdx, qd], Sx, inc=(sl % 2 == 1))

                def exp():
                    if bi is None:
                        kb.op("act", lambda e: e.activation(out=P.t[:, 0:256], in_=Sa.t[:, 0:256], func=AF.Exp, scale=0.125),
                              reads=[Sa], writes=[(P, True)])
                        kb.op("act", lambda e: e.activation(out=P.t[:, 256:512], in_=Sb.t[:, 0:256], func=AF.Exp, scale=0.125),
                              reads=[Sb], writes=[(P, True)])
                    else:
                        dbb, tsb = db[cnt["db"] % 2], tS[cnt["db"] % 2]
                        cnt["db"] += 1
                        kb.dma("sp", dbb.t[:], dbias[bi[0], bi[1], hg], dbb, writes=[dbb])
                        kb.op("dve", lambda e: e.scalar_tensor_tensor(
                            out=tsb.t[:, 0:256], in0=Sa.t[:, 0:256], scalar=0.125, in1=dbb.t[:, 0:256], op0=ALU.mult, op1=ALU.add),
                            reads=[Sa, dbb], writes=[(tsb, True)])
                        kb.op("dve", lambda e: e.scalar_tensor_tensor(
                            out=tsb.t[:, 256:512], in0=Sb.t[:, 0:256], scalar=0.125, in1=dbb.t[:, 256:512], op0=ALU.mult, op1=ALU.add),
                            reads=[Sb, dbb], writes=[(tsb, True)])
                        kb.op("act", lambda e: e.activation(out=P.t[:], in_=tsb.t[:], func=AF.Exp), reads=[tsb], writes=[P])

                def pv():
                    for sl, i in enumerate(ORDER):
                        h = hg * 4 + i
                        kb.mm(OD.t[0:65, sl * 128:(sl + 1) * 128], vdx.t[:, e_ * 8 + h, :], P.t[:, sl * 128:(sl + 1) * 128],
                              ki == 0 and sl == 0, ki == nk - 1, [vdx, P], OD, inc=(sl == 3))

                st = {"scores": scores, "exp": exp, "pv": pv}
                if ki == nk - 1:
                    st["fin"] = lambda: finalize_cd(OD, None, catD[hg * 4:(hg + 1) * 4, :, qtk], catDb)
                return st

            steps = []
            ng = 0
            for T in range(ctiles if "C" in do else 0):
                keys = [(0, None), (1, None)]
                if T < 16:
                    keys += [(4 + T - 1, 0 if T == 0 else 1), (4 + T, None), (4 + T + 1, 3 if T == 15 else 2)]
                for kv in range(2):
                    OC = ps[4 + ng % 2]
                    ng += 1
                    for ki, (e_, mk) in enumerate(keys):
                        steps.append(c_step(T, kv, ki, len(keys), e_, mk, OC, len(steps)))
            for T in range(18 if "D" in do else 0):
                keys = [(0, None), (1, None)]
                if T < 16:
                    cls = d_class(T)
                    keys += [(4 + T + o, (cls, o + 3)) for o in D_OFFS[cls]]
                for hg in range(2):
                    OD = ps[4 + ng % 2]
                    ng += 1
                    for ki, (e_, bi) in enumerate(keys):
                        steps.append(d_step(T, hg, ki, len(keys), e_, bi, OD, len(steps)))
            if steps:
                run_pipeline(steps)
            kb.barrier()

        if "O" in do:
          with ExitStack() as pes:
            wab = kb.sbuf("wab", [128, 8, D], BF16, pes)
            wcd = kb.sbuf("wcd", [64, 16, D], BF16, pes)
            gb2 = kb.sbuf("gb2", [128, 2, D], F32, pes)
            lbc = kb.sbuf("lbc", [128, 2, D], F32, pes)
            xg = kb.sbuf("xg", [128, 4, D], F32, pes)
            xgb = [Buf("xg%d" % i) for i in range(4)]
            cA = kb.sbuf("cA", [128, 4, 512], BF16, pes)
            cB = kb.sbuf("cB", [128, 4, 512], BF16, pes)
            cC = kb.sbuf("cC", [64, 8, 512], BF16, pes)
            cD = kb.sbuf("cD", [64, 8, 512], BF16, pes)
            tmp = [kb.sbuf("tmp%d" % i, [128, 512], F32, pes) for i in range(2)]
            stats = [kb.sbuf("stats%d" % i, [128, 4, 6], F32, pes) for i in range(2)]
            mv = [kb.sbuf("mv%d" % i, [128, 2], F32, pes) for i in range(2)]
            rs = [kb.sbuf("rs%d" % i, [128, 2], F32, pes) for i in range(2)]
            for c4 in range(2):
                kb.dma("pool", wab.t[:, c4 * 4:(c4 + 1) * 4, :], w_ab[c4 * 512:(c4 + 1) * 512, :].rearrange("(c p) n -> p c n", p=128),
                       wab, writes=[(wab, True)])
            for c4 in range(4):
                kb.dma("pool", wcd.t[:, c4 * 4:(c4 + 1) * 4, :], w_cd[c4 * 256:(c4 + 1) * 256, :].rearrange("(h p) n -> p h n", p=64),
                       wcd, writes=[(wcd, True)])
            for i in range(2):
                kb.dma("sp", gb2.t[:, i, :], gate[i:i + 1, :].partition_broadcast(128), gb2, writes=[(gb2, True)])
                kb.dma("sp", lbc.t[:, i, :], lnp[i:i + 1, :].partition_broadcast(128), lbc, writes=[(lbc, True)])
            ny = 0
            for (t0, ntl) in GROUPS:
                ntok = ntl * 128
                s = 1 if t0 >= 16 else 0
                tk = slice(t0 * 128, t0 * 128 + ntok)
                kb.dma("sp", xg.t[:, 0:ntl, :], xin[tk, :].rearrange("(t p) c -> p t c", p=128), xg, writes=xgb[:ntl])
                kb.dma("sp", cA.t[:, :, :ntok], catA[:, :, tk].rearrange("c p t -> p c t"), cA, reads=[catAb], writes=[cA])
                kb.dma("sp", cB.t[:, :, :ntok], fm[FM_YB:FM_YB + 4, :, tk].rearrange("c p t -> p c t"), cB, writes=[cB])
                kb.dma("sp", cC.t[:, :, :ntok], catC[:, :, tk].rearrange("h p t -> p h t"), cC, reads=[catCb], writes=[cC])
                kb.dma("sp", cD.t[:, :, :ntok], catD[:, :, tk].rearrange("h p t -> p h t"), cD, reads=[catDb], writes=[cD])
                for ti in range(ntl):
                    tt = slice(ti * 128, (ti + 1) * 128)
                    for n in range(4):
                        ncs = slice(n * 512, (n + 1) * 512)
                        Y = ps[ny % 4]
                        tm_ = tmp[ny % 2]
                        ny += 1
                        pairs = [(cA.t[:, c, tt], wab.t[:, c, ncs]) for c in range(4)]
                        pairs += [(cB.t[:, c, tt], wab.t[:, 4 + c, ncs]) for c in range(4)]
                        pairs += [(cC.t[0:64, h, tt], wcd.t[0:64, h, ncs]) for h in range(8)]
                        pairs += [(cD.t[0:64, h, tt], wcd.t[0:64, 8 + h, ncs]) for h in range(8)]
                        kb.mm_group(Y.t[:], pairs, reads=[cA, cB, cC, cD, wab, wcd], out_buf=Y)
                        kb.op("dve", lambda e, Y=Y, tm_=tm_, ncs=ncs: e.tensor_tensor(out=tm_.t[:], in0=Y.t[:], in1=gb2.t[:, s, ncs], op=ALU.mult),
                              reads=[Y, gb2], writes=[tm_])
                        kb.op("dve", lambda e, tm_=tm_, ti=ti, ncs=ncs: e.scalar_tensor_tensor(
                            out=xg.t[:, ti, ncs], in0=xg.t[:, ti, ncs], scalar=ALPHA, in1=tm_.t[:], op0=ALU.mult, op1=ALU.add),
                            reads=[tm_, xgb[ti]], writes=[(xgb[ti], True)])
                    ln_epilogue(kb, xg.t[:, ti, :], xgb[ti], stats[ti % 2], mv[ti % 2], rs[ti % 2], lbc.t[:, 0, :], lbc.t[:, 1, :], lbc)
                    kb.dma("sp", xout[(t0 + ti) * 128:(t0 + ti + 1) * 128, :], xg.t[:, ti, :], xg, reads=[xgb[ti]], final=True)
            kb.barrier()
        kb.finish()
    return nc


MCOLS = 2304


def build_mod():
    nc = bass.Bass("TRN2", target_bir_lowering=False)
    cc = nc.dram_tensor("cc", [128, 32], F32, kind="ExternalInput").ap()
    wm = nc.dram_tensor("wm", [4, D, MCOLS], F32, kind="ExternalInput").ap()
    bm = nc.dram_tensor("bm", [1, 4 * MCOLS], F32, kind="ExternalInput").ap()
    mo = nc.dram_tensor("mo", [2, 4 * MCOLS], F32, kind="ExternalOutput").ap()
    with ExitStack() as es:
        kb = KB(nc, es)
        c_sb = kb.sbuf("c_sb", [128, 16, 2], F32)
        bb = kb.sbuf("bb", [2, 4 * MCOLS], F32)
        ob = kb.sbuf("ob", [2, 4 * MCOLS], F32)
        wbuf = [kb.sbuf("wbuf%d" % i, [128, 16, 512], F32) for i in range(2)]
        pm = [kb.psum("pm%d" % i, [128, 512]) for i in range(2)]
        kb.dma("sp", c_sb.t[:].rearrange("p a b -> p (a b)"), cc, c_sb, writes=[c_sb])
        kb.dma("sp", bb.t[:], bm[0:1, :].partition_broadcast(2), bb, writes=[bb])
        kb.op("act", lambda e: e.activation(out=c_sb.t[:], in_=c_sb.t[:], func=AF.Silu), reads=[c_sb], writes=[c_sb])
        n = 0
        for l in range(4):
            for c0 in range(0, MCOLS, 512):
                ncol = min(512, MCOLS - c0)
                wb_, pb_ = wbuf[n % 2], pm[n % 2]
                n += 1
                for k4 in range(4):
                    kb.dma("sp", wb_.t[:, k4 * 4:(k4 + 1) * 4, :ncol],
                           wm[l, k4 * 512:(k4 + 1) * 512, c0:c0 + ncol].rearrange("(kc p) c -> p kc c", p=128),
                           wb_, writes=[(wb_, True)])
                kb.mm_group(pb_.t[0:2, :ncol], [(c_sb.t[:, kc, :], wb_.t[:, kc, :ncol]) for kc in range(16)],
                            reads=[c_sb, wb_], out_buf=pb_)
                o0 = l * MCOLS + c0
                kb.op("dve", lambda e, pb_=pb_, o0=o0, ncol=ncol: e.tensor_tensor(
                    out=ob.t[:, o0:o0 + ncol], in0=pb_.t[0:2, :ncol], in1=bb.t[:, o0:o0 + ncol], op=ALU.add),
                    reads=[pb_, bb], writes=[(ob, True)])
        kb.dma("sp", mo, ob.t[:], ob, reads=[ob], final=True)
        kb.finish()
    return nc


NCORES = 8
_PROGS = {}


def _prog(name):
    if name not in _PROGS:
        _PROGS[name] = {"M": build_mod, "F": build_ffn, "J": build_proj, "T": build_attn}[name]()
    return _PROGS[name]


def _run(name, in_maps):
    import time, os, sys
    t0 = time.time()
    res = run_bass_kernel_spmd(_prog(name), in_maps, core_ids=list(range(NCORES)))
    if os.environ.get("K_TIMING"):
        nb = sum(v.nbytes for m in in_maps for v in m.values())
        print("[launch %s] %.1fs  in=%.0fMB" % (name, time.time() - t0, nb / 1e6), file=sys.stderr, flush=True)
    return res.results


def d_bias_tables(rpb, core):
    out = np.full((5, 7, 2, 128, 512), -30000.0, np.float32)
    ii = np.arange(128)
    for cls, tloc in enumerate((0, 1, 2, 14, 15)):
        tg = core * 16 + tloc
        rq = (2 * tg + ii // 64)[None, :]
        cq = (ii % 64)[None, :]
        rs = np.clip(rq - 4, 0, 256 - 8)
        cs = np.clip(cq - 8, 0, 64 - 16)
        for o in D_OFFS[cls]:
            kg = tg + o
            if kg < 0 or kg > 127:
                continue
            rk = (2 * kg + ii // 64)[:, None]
            ck = (ii % 64)[:, None]
            valid = (rk >= rs) & (rk < rs + 8) & (ck >= cs) & (ck < cs + 16)
            dr = np.clip(rk - rq + 7, 0, 14)
            dc = np.clip(ck - cq, -15, 15) + 15
            for hg in range(2):
                for sl, i in enumerate(ORDER):
                    g = rpb[hg * 4 + i][dr, dc]
                    out[cls, o + 3, hg, :, sl * 128:(sl + 1) * 128] = np.where(valid, g, np.float32(-30000.0))
    return out


def c_masks(core):
    i = np.arange(128)[:, None]
    j = np.arange(128)[None, :]
    mprev = np.tile((i >= j).astype(np.float32), (1, 4))
    mnext = np.tile((i <= j).astype(np.float32), (1, 4))
    z = np.zeros_like(mprev)
    return np.stack([z if core == 0 else mprev, mprev, mnext, z if core == NCORES - 1 else mnext])


def _ext_fm(fms, i, c0, nch):
    own = fms[i][c0:c0 + nch]
    z = np.zeros((nch, 128, 256), own.dtype)
    prev = fms[i - 1][c0:c0 + nch][:, :, 1792:2048] if i > 0 else z
    nxt = fms[i + 1][c0:c0 + nch][:, :, 0:256] if i < NCORES - 1 else z
    return np.ascontiguousarray(np.concatenate([own[:, :, 2048:2304], prev, own[:, :, :2048], nxt], axis=2))


def _ext_tm(tms, i, c0, ncol):
    own = tms[i][:, c0:c0 + ncol]
    z = np.zeros((256, ncol), own.dtype)
    prev = tms[i - 1][1792:2048, c0:c0 + ncol] if i > 0 else z
    nxt = tms[i + 1][0:256, c0:c0 + ncol] if i < NCORES - 1 else z
    return np.ascontiguousarray(np.concatenate([own[2048:2304], prev, own[:2048], nxt], axis=0))


def kernel(x, c, ctx, c_ctx, w_mod, b_mod, ln_g, ln_b, ffn1_w_in, ffn1_w_out, ffn2_w_in, ffn2_w_out,
           mix_w_in, mix_w_out, a_lambda, b_norm_g, b_norm_b, b_spatial_w, b_spatial_b, c_sink, d_rpb,
           _nlayers=4, _dump=None):
    f32 = np.float32
    x = np.asarray(x, f32)[0]
    ctx = np.asarray(ctx, f32)[0]
    ident = np.eye(128, dtype=f32)
    cc = np.stack([np.asarray(c, f32)[0].reshape(16, 128).T, np.asarray(c_ctx, f32).reshape(16, 128).T], axis=2)
    cc = np.ascontiguousarray(cc.reshape(128, 32))
    w_mod = np.asarray(w_mod, f32)
    b_mod = np.asarray(b_mod, f32)
    ins = [{"cc": cc, "wm": np.ascontiguousarray(w_mod[:, :, i * MCOLS:(i + 1) * MCOLS]),
            "bm": np.ascontiguousarray(b_mod[:, i * MCOLS:(i + 1) * MCOLS]).reshape(1, 4 * MCOLS)} for i in range(NCORES)]
    mo = _run("M", ins)
    mods = np.concatenate([r["mo"].reshape(2, 4, MCOLS) for r in mo], axis=2)
    xs = [np.ascontiguousarray(np.concatenate([x[i * 2048:(i + 1) * 2048], ctx], 0)) for i in range(NCORES)]
    ropes = [rope_tables(i) for i in range(NCORES)]
    cms = [c_masks(i) for i in range(NCORES)]

    def ffn(xs, l, idx, lnk, w_in, w_out):
        modc = mod_cols(mods[0, l], mods[1, l], idx)
        lnp = np.ascontiguousarray(np.stack([ln_g[l, lnk], ln_b[l, lnk]]).astype(f32))
        w_in = np.ascontiguousarray(w_in, f32)
        w_out = np.ascontiguousarray(w_out, f32)
        r = _run("F", [{"xin": xs[i], "modc": modc, "lnp": lnp, "ident": ident, "w_in": w_in, "w_out": w_out}
                       for i in range(NCORES)])
        return [q["xout"] for q in r]

    if _dump is not None:
        _dump["mods"] = mods
    for l in range(_nlayers):
        xs = ffn(xs, l, (0, 1, 2), 0, ffn1_w_in[l], ffn1_w_out[l])
        if _dump is not None:
            _dump["x1_%d" % l] = xs
        w_fm, w_tm = build_proj_weights(np.asarray(mix_w_in[l], f32))
        modc = mod_cols(mods[0, l], mods[1, l], (3, 4, None))
        gnp = np.ascontiguousarray(np.stack([b_norm_g[l], b_norm_b[l]]).astype(f32))
        wsT = np.ascontiguousarray(np.transpose(np.asarray(b_spatial_w[l], f32), (2, 0, 1)).reshape(128, 512))
        bsr = np.ascontiguousarray(np.asarray(b_spatial_b[l], f32).reshape(1, 512))
        r = _run("J", [{"xin": xs[i], "modc": modc, "ident": ident, "w_fm": w_fm, "w_tm": w_tm, "rope": ropes[i],
                        "gnp": gnp, "wsT": wsT, "bsr": bsr} for i in range(NCORES)])
        fms = [q["fm"] for q in r]
        tms = [q["tm"] for q in r]
        ka = np.ascontiguousarray(np.concatenate([fms[0][FM_KA:FM_KA + 4][:, :, 2048:2304]] +
                                                 [fms[i][FM_KA:FM_KA + 4][:, :, :2048] for i in range(NCORES)], axis=2))
        va = np.ascontiguousarray(np.concatenate([tms[0][2048:2304, 0:512]] + [tms[i][:2048, 0:512] for i in range(NCORES)], axis=0))
        lam_init = 0.8 - 0.6 * np.exp(-0.3 * l)
        alam = np.concatenate([np.asarray(a_lambda[l], f32).reshape(256), np.array([lam_init, 1.0 - lam_init], f32)]).reshape(1, 258)
        alam = np.ascontiguousarray(alam.astype(f32))
        sink = np.ascontiguousarray(np.asarray(c_sink[l], f32).reshape(1, 8))
        m9 = mods[:, l].reshape(2, 9, D)
        gate = np.ascontiguousarray(m9[:, 5, :])
        lnp = np.ascontiguousarray(np.stack([ln_g[l, 1], ln_b[l, 1]]).astype(f32))
        wo = np.asarray(mix_w_out[l], f32)
        w_ab, w_cd = np.ascontiguousarray(wo[:1024]), np.ascontiguousarray(wo[1024:])
        rpb = np.asarray(d_rpb[l], f32)
        ins = []
        for i in range(NCORES):
            ins.append({"xin": xs[i], "fm": fms[i], "ka": ka, "va": va,
                        "kc": _ext_fm(fms, i, FM_KC, 2), "vc": _ext_tm(tms, i, 512, 128),
                        "kd": _ext_fm(fms, i, FM_KD, 4), "vd": _ext_tm(tms, i, 640, 512),
                        "cmask": cms[i], "dbias": d_bias_tables(rpb, i), "alam": alam, "sink": sink, "ident": ident,
                        "gate": gate, "lnp": lnp, "w_ab": w_ab, "w_cd": w_cd})
        r = _run("T", ins)
        xs = [q["xout"] for q in r]
        if _dump is not None:
            _dump["x2_%d" % l] = xs
            _dump["fm_%d" % l] = fms
        xs = ffn(xs, l, (6, 7, 8), 2, ffn2_w_in[l], ffn2_w_out[l])
    out = np.concatenate([xs[i][:2048] for i in range(NCORES)], axis=0)
    return np.ascontiguousarray(out[None].astype(f32))
```

```python
import os
import numpy as np
import concourse.bass as bass
import concourse.mybir as mybir
from concourse.bass_utils import run_bass_kernel_spmd
from contextlib import ExitStack

F32 = mybir.dt.float32
BF16 = mybir.dt.bfloat16
AF = mybir.ActivationFunctionType
ALU = mybir.AluOpType


class Buf:
    def __init__(self, name, t=None):
        self.name = name
        self.t = t
        self.w = {}
        self.r = {}
        self.dsem = None
        self.dcnt = 0


class KB:
    def __init__(self, nc, es):
        self.nc, self.es = nc, es
        self.e = {"pe": nc.tensor, "act": nc.scalar, "dve": nc.vector, "pool": nc.gpsimd, "sp": nc.sync}
        self.sem = {k: es.enter_context(nc.semaphore("s_" + k)) for k in ("pe", "act", "dve", "pool")}
        self.cnt = {k: 0 for k in self.sem}
        self.seen = {q: {} for q in self.e}
        self.finals = {}
        self.dbufs = {}
        self.pend = []
        self.pe_pend = []
        self.attach = False
        self.nbuf = 0

    def sbuf(self, name, shape, dtype, es=None):
        t = (es or self.es).enter_context(self.nc.sbuf_tensor(name, shape, dtype))
        return Buf(name, t)

    def psum(self, name, shape, dtype=F32):
        t = self.es.enter_context(self.nc.psum_tensor(name, shape, dtype))
        return Buf(name, t)

    def _wait(self, q, key, sem, val):
        if key == "pe" and q == "pe":
            return
        if key in self.dbufs:
            val = self.dbufs[key].dcnt
        if self.seen[q].get(key, 0) >= val:
            return
        self.pend.append((q, sem, val))
        self.seen[q][key] = val

    def _flush(self, q, ins=None):
        pend, self.pend = self.pend, []
        if ins is not None and pend and self.attach:
            last = pend.pop()
        else:
            last = None
        for (qq, sem, val) in pend:
            self.e[qq].wait_ge(sem, val)
        return last

    def _pre(self, q, reads, writes):
        for b in reads:
            for k, (s, v) in b.w.items():
                self._wait(q, k, s, v)
        for w in writes:
            b, part = (w if isinstance(w, tuple) else (w, False))
            if part and not b.r:
                continue
            for k, (s, v) in b.r.items():
                self._wait(q, k, s, v)
            for k, (s, v) in b.w.items():
                self._wait(q, k, s, v)

    def _post(self, key, sem, val, reads, writes):
        for w in writes:
            b, part = (w if isinstance(w, tuple) else (w, False))
            if (not part) or b.r:
                b.w = {}
                b.r = {}
            b.w[key] = (sem, val)
        for b in reads:
            if any((w[0] if isinstance(w, tuple) else w) is b for w in writes):
                continue
            b.r[key] = (sem, val)

    def op(self, q, fn, reads=(), writes=()):
        self._pre(q, reads, writes)
        self._flush(q)
        ins = fn(self.e[q])
        self.cnt[q] += 1
        ins.then_inc(self.sem[q], 1)
        self._post(q, self.sem[q], self.cnt[q], reads, writes)
        return ins

    def mm_group(self, out_ap, pairs, reads, out_buf, part=False):
        wr = [(out_buf, part)]
        self._pre("pe", reads, wr)
        self._flush("pe")
        n = len(pairs)
        ins = None
        for i, (l, r) in enumerate(pairs):
            ins = self.nc.tensor.matmul(out_ap, lhsT=l, rhs=r, start=(i == 0), stop=(i == n - 1))
        self.cnt["pe"] += 1
        ins.then_inc(self.sem["pe"], 1)
        self._post("pe", self.sem["pe"], self.cnt["pe"], reads, wr)

    def mm(self, out_ap, lhsT, rhs, start, stop, reads, out_buf, inc=True):
        wr = [(out_buf, True)]
        self._pre("pe", reads, wr)
        self._flush("pe")
        ins = self.nc.tensor.matmul(out_ap, lhsT=lhsT, rhs=rhs, start=start, stop=stop)
        self.pe_pend.append((reads, wr))
        if inc:
            self.cnt["pe"] += 1
            ins.then_inc(self.sem["pe"], 1)
            for (r, w) in self.pe_pend:
                self._post("pe", self.sem["pe"], self.cnt["pe"], r, w)
            self.pe_pend = []
        return ins

    def barrier(self):
        assert not self.pe_pend
        for q in self.e:
            for k in self.sem:
                if self.cnt[k]:
                    self._wait(q, k, self.sem[k], self.cnt[k])
            for k, b in self.dbufs.items():
                self._wait(q, k, b.dsem, b.dcnt)
            self._flush(q)

    def dma(self, q, out_ap, in_ap, sb, reads=(), writes=(), final=False):
        self._pre(q, reads, writes)
        self._flush(q)
        if sb.dsem is None:
            self.nbuf += 1
            sb.dsem = self.es.enter_context(self.nc.semaphore("d%d_%s" % (self.nbuf, sb.name)))
        ins = self.e[q].dma_start(out=out_ap, in_=in_ap)
        sb.dcnt += 16
        ins.then_inc(sb.dsem, 16)
        key = "d_" + sb.name
        self.dbufs[key] = sb
        self._post(key, sb.dsem, sb.dcnt, reads, writes)
        if final:
            self.finals[key] = (sb.dsem, sb.dcnt)
        return ins

    def finish(self):
        for k, (s, v) in self.finals.items():
            self._wait("sp", k, s, v)
        self._flush("sp")


D = 2048
DFF = 5632
NT_TILES = 18
NTOK = NT_TILES * 128
ALPHA = 8 ** 0.25
LN_EPS = 1e-6
GROUPS = [(0, 4), (4, 4), (8, 4), (12, 4), (16, 2)]


def ln_epilogue(kb, xt, xb, stats, mv, rs, gbc, bbc, gb, nr=128):
    for q4 in range(4):
        kb.op("dve", lambda e, q4=q4: e.bn_stats(out=stats.t[0:nr, q4, :], in_=xt[:, q4 * 512:(q4 + 1) * 512]),
              reads=[xb], writes=[(stats, True)])
    kb.op("dve", lambda e: e.bn_aggr(out=mv.t[0:nr, :], in_=stats.t[0:nr].rearrange("p a b -> p (a b)")), reads=[stats], writes=[mv])
    kb.op("act", lambda e: e.activation(out=rs.t[0:nr, 0:1], in_=mv.t[0:nr, 1:2], func=AF.Sqrt, bias=LN_EPS, scale=1.0),
          reads=[mv], writes=[rs])
    kb.op("dve", lambda e: e.reciprocal(out=rs.t[0:nr, 0:1], in_=rs.t[0:nr, 0:1]), reads=[rs], writes=[rs])
    kb.op("dve", lambda e: e.tensor_scalar(out=rs.t[0:nr, 1:2], in0=mv.t[0:nr, 0:1], scalar1=rs.t[0:nr, 0:1], scalar2=-1.0,
                                            op0=ALU.mult, op1=ALU.mult), reads=[mv, rs], writes=[rs])
    kb.op("act", lambda e: e.activation(out=xt, in_=xt, func=AF.Identity, bias=rs.t[0:nr, 1:2], scale=rs.t[0:nr, 0:1]),
          reads=[xb, rs], writes=[xb])
    kb.op("pool", lambda e: e.tensor_tensor(out=xt, in0=xt, in1=gbc[0:nr], op=ALU.mult), reads=[xb, gb], writes=[xb])
    kb.op("dve", lambda e: e.tensor_tensor(out=xt, in0=xt, in1=bbc[0:nr], op=ALU.add), reads=[xb, gb], writes=[xb])


NCTX_F = 32
NTOK_F = 2048 + NCTX_F
GROUPS_F = [(0, 512, 0), (512, 512, 0), (1024, 512, 0), (1536, 512, 0), (2048, NCTX_F, 1)]


def build_ffn(phases=("tr", "in", "out", "ln"), groups=GROUPS_F):
    nc = bass.Bass("TRN2", target_bir_lowering=False)
    xin = nc.dram_tensor("xin", [NTOK_F, D], F32, kind="ExternalInput").ap()
    modc = nc.dram_tensor("modc", [128, 96], F32, kind="ExternalInput").ap()
    lnp = nc.dram_tensor("lnp", [2, D], F32, kind="ExternalInput").ap()
    ident = nc.dram_tensor("ident", [128, 128], F32, kind="ExternalInput").ap()
    w_in = nc.dram_tensor("w_in", [D, 2 * DFF], F32, kind="ExternalInput").ap()
    w_out = nc.dram_tensor("w_out", [DFF, D], F32, kind="ExternalInput").ap()
    xout = nc.dram_tensor("xout", [NTOK_F, D], F32, kind="ExternalOutput").ap()
    with ExitStack() as es:
        kb = KB(nc, es)
        mod = kb.sbuf("mod", [128, 2, 3, 16], F32)
        idb = kb.sbuf("ident_sb", [128, 128], F32)
        gbc = kb.sbuf("gbc", [128, 2, D], F32)
        xg = kb.sbuf("xg", [128, 4, D], F32)
        xgb = [Buf("xg%d" % i) for i in range(4)]
        hT = kb.sbuf("hT", [128, 16, 512], BF16)
        hid = kb.sbuf("hid", [128, 44, 512], BF16)
        wg = [kb.sbuf("wg%d" % i, [128, 16, 256], BF16) for i in range(2)]
        wu = [kb.sbuf("wu%d" % i, [128, 16, 256], BF16) for i in range(2)]
        wo = [kb.sbuf("wo%d" % i, [128, 44, 256], BF16) for i in range(2)]
        sg = [kb.sbuf("sg%d" % i, [128, 512], F32) for i in range(2)]
        yT = [kb.sbuf("yT%d" % i, [128, 512], F32) for i in range(2)]
        stats = [kb.sbuf("stats%d" % i, [128, 4, 6], F32) for i in range(2)]
        mv = [kb.sbuf("mv%d" % i, [128, 2], F32) for i in range(2)]
        rs = [kb.sbuf("rs%d" % i, [128, 2], F32) for i in range(2)]
        tpb = [kb.psum("tp%d" % i, [128, 512]) for i in range(2)]
        gps = [kb.psum("gp%d" % i, [128, 512]) for i in range(2)]
        ups = [kb.psum("up%d" % i, [128, 512]) for i in range(2)]
        yps = [kb.psum("yp%d" % i, [128, 512]) for i in range(2)]

        kb.dma("sp", mod.t[:].rearrange("p a b c -> p (a b c)"), modc, mod, writes=[mod])
        kb.dma("sp", idb.t[:], ident, idb, writes=[idb])
        kb.dma("sp", gbc.t[:, 0, :], lnp[0:1, :].partition_broadcast(128), gbc, writes=[(gbc, True)])
        kb.dma("sp", gbc.t[:, 1, :], lnp[1:2, :].partition_broadcast(128), gbc, writes=[(gbc, True)])
        kb.op("dve", lambda e: e.tensor_scalar(out=mod.t[:, :, 1, :], in0=mod.t[:, :, 1, :], scalar1=1.0, scalar2=None,
                                                op0=ALU.add), reads=[mod], writes=[mod])
        kb.op("dve", lambda e: e.tensor_scalar(out=mod.t[:, :, 2, :], in0=mod.t[:, :, 2, :], scalar1=0.5, scalar2=None,
                                                op0=ALU.mult), reads=[mod], writes=[mod])
        rot = {"tp": 0, "ev": 0}

        def next_tp():
            rot["tp"] ^= 1
            return tpb[rot["tp"]]

        def ev_eng():
            rot["ev"] ^= 1
            return "dve" if rot["ev"] else "act"

        xt = kb.sbuf("xt", [128, D], F32)

        def tiles_of(ntok):
            return [(i * 128, min(128, ntok - i * 128)) for i in range((ntok + 127) // 128)]

        def phase_a(r0, ntok, s):
            for ti, (ro, nr) in enumerate(tiles_of(ntok) if "tr" in phases else []):
                kb.dma("sp", xt.t[0:nr, :], xin[r0 + ro:r0 + ro + nr, :], xt, writes=[xt])
                for kq in range(4):
                    tp = next_tp()
                    for k4 in range(4):
                        kc = kq * 4 + k4
                        kb.op("pe", lambda e, tp=tp, k4=k4, kc=kc, nr=nr: e.transpose(
                            out=tp.t[:, k4 * 128:k4 * 128 + nr], in_=xt.t[0:nr, kc * 128:(kc + 1) * 128], identity=idb.t[0:nr, 0:nr]),
                            reads=[xt, idb], writes=[(tp, True)])
                    eng = ev_eng()
                    for k4 in range(4):
                        kc = kq * 4 + k4
                        if eng == "dve":
                            kb.op("dve", lambda e, tp=tp, k4=k4, kc=kc, ti=ti, nr=nr: e.tensor_scalar(
                                out=hT.t[:, kc, ti * 128:ti * 128 + nr], in0=tp.t[:, k4 * 128:k4 * 128 + nr],
                                scalar1=mod.t[:, s, 1, kc:kc + 1], scalar2=mod.t[:, s, 0, kc:kc + 1],
                                op0=ALU.mult, op1=ALU.add), reads=[tp, mod], writes=[(hT, True)])
                        else:
                            kb.op("act", lambda e, tp=tp, k4=k4, kc=kc, ti=ti, nr=nr: e.activation(
                                out=hT.t[:, kc, ti * 128:ti * 128 + nr], in_=tp.t[:, k4 * 128:k4 * 128 + nr],
                                func=AF.Identity, bias=mod.t[:, s, 0, kc:kc + 1], scale=mod.t[:, s, 1, kc:kc + 1]),
                                reads=[tp, mod], writes=[(hT, True)])

        def phase_b(r0, ntok, s):
            for jb in range(22 if "in" in phases else 0):
                wgb, wub = wg[jb % 2], wu[jb % 2]
                kb.dma("pool", wgb.t[:], w_in[:, jb * 256:(jb + 1) * 256].rearrange("(kc p) c -> p kc c", p=128),
                       wgb, writes=[wgb])
                kb.dma("pool", wub.t[:], w_in[:, DFF + jb * 256:DFF + (jb + 1) * 256].rearrange("(kc p) c -> p kc c", p=128),
                       wub, writes=[wub])
                for jj in range(2):
                    j = jb * 2 + jj
                    gp, up, sgb = gps[j % 2], ups[j % 2], sg[j % 2]
                    kb.mm_group(gp.t[:, :ntok], [(wgb.t[:, kc, jj * 128:(jj + 1) * 128], hT.t[:, kc, :ntok]) for kc in range(16)],
                                reads=[wgb, hT], out_buf=gp)
                    kb.mm_group(up.t[:, :ntok], [(wub.t[:, kc, jj * 128:(jj + 1) * 128], hT.t[:, kc, :ntok]) for kc in range(16)],
                                reads=[wub, hT], out_buf=up)
                    kb.op("act", lambda e, gp=gp, sgb=sgb: e.activation(out=sgb.t[:, :ntok], in_=gp.t[:, :ntok], func=AF.Silu),
                          reads=[gp], writes=[sgb])
                    kb.op("dve", lambda e, up=up, sgb=sgb, j=j: e.tensor_tensor(out=hid.t[:, j, :ntok], in0=sgb.t[:, :ntok],
                                                                              in1=up.t[:, :ntok], op=ALU.mult),
                          reads=[sgb, up], writes=[(hid, True)])

        def phase_c(r0, ntok, s):
            tl = tiles_of(ntok)
            ntl = len(tl)
            full = ntok % 128 == 0
            if full:
                kb.dma("sp", xg.t[:, 0:ntl, :], xin[r0:r0 + ntok, :].rearrange("(t p) c -> p t c", p=128), xg, writes=xgb[:ntl])
            else:
                assert ntl == 1
                kb.dma("sp", xg.t[0:ntok, 0, :], xin[r0:r0 + ntok, :], xg, writes=xgb[:1])
            for ob in range(8 if "out" in phases else 0):
                wob = wo[ob % 2]
                kb.dma("pool", wob.t[:], w_out[:, ob * 256:(ob + 1) * 256].rearrange("(j p) c -> p j c", p=128),
                       wob, writes=[wob])
                for oo in range(2):
                    oc = ob * 2 + oo
                    yp, yTb = yps[oc % 2], yT[oc % 2]
                    kb.mm_group(yp.t[:, :ntok], [(wob.t[:, j, oo * 128:(oo + 1) * 128], hid.t[:, j, :ntok]) for j in range(44)],
                                reads=[wob, hid], out_buf=yp)
                    kb.op("act", lambda e, yp=yp, yTb=yTb, oc=oc: e.activation(
                        out=yTb.t[:, :ntok], in_=yp.t[:, :ntok], func=AF.Identity, scale=mod.t[:, s, 2, oc:oc + 1]),
                        reads=[yp, mod], writes=[yTb])
                    tp = next_tp()
                    for ti, (ro, nr) in enumerate(tl):
                        kb.op("pe", lambda e, tp=tp, ti=ti, yTb=yTb, ro=ro, nr=nr: e.transpose(
                            out=tp.t[0:nr, ti * 128:(ti + 1) * 128], in_=yTb.t[:, ro:ro + nr], identity=idb.t[:]),
                            reads=[yTb, idb], writes=[(tp, True)])
                    if full:
                        kb.op("dve", lambda e, tp=tp, oc=oc: e.scalar_tensor_tensor(
                            out=xg.t[:, 0:ntl, oc * 128:(oc + 1) * 128], in0=xg.t[:, 0:ntl, oc * 128:(oc + 1) * 128], scalar=ALPHA,
                            in1=tp.t[:, :ntok].rearrange("p (t c) -> p t c", c=128), op0=ALU.mult, op1=ALU.add),
                            reads=[tp] + xgb[:ntl], writes=[(b, True) for b in xgb[:ntl]])
                    else:
                        kb.op("dve", lambda e, tp=tp, oc=oc: e.scalar_tensor_tensor(
                            out=xg.t[0:ntok, 0, oc * 128:(oc + 1) * 128], in0=xg.t[0:ntok, 0, oc * 128:(oc + 1) * 128], scalar=ALPHA,
                            in1=tp.t[0:ntok, 0:128], op0=ALU.mult, op1=ALU.add),
                            reads=[tp] + xgb[:1], writes=[(xgb[0], True)])

        def phase_e(r0, ntok, s):
            for ti, (ro, nr) in enumerate(tiles_of(ntok)):
                if "ln" in phases:
                    ln_epilogue(kb, xg.t[0:nr, ti, :], xgb[ti], stats[ti % 2], mv[ti % 2], rs[ti % 2],
                                gbc.t[:, 0, :], gbc.t[:, 1, :], gbc, nr=nr)
                kb.dma("sp", xout[r0 + ro:r0 + ro + nr, :], xg.t[0:nr, ti, :], xg, reads=[xgb[ti]], final=True)

        ng = len(groups)
        phase_a(*groups[0])
        for gi in range(ng):
            phase_b(*groups[gi])
            phase_c(*groups[gi])
            if gi + 1 < ng:
                phase_a(*groups[gi + 1])
            phase_e(*groups[gi])
        kb.finish()
    return nc


NFM = 26
FM_QA, FM_KA, FM_QC, FM_KC, FM_QD, FM_KD, FM_YB = 0, 4, 8, 12, 14, 18, 22
NTM = 1152
WFM_COLS = 40 * 128
WTM_COLS = 1664
GELU_C = 0.7978845608028654


def load_x_transpose_mod(kb, xin, t0, ntl, s, xg, xgb, hT, idb, mod, tpb, rot):
    kb.dma("sp", xg.t[:, 0:ntl, :], xin[t0 * 128:(t0 + ntl) * 128, :].rearrange("(t p) c -> p t c", p=128),
           xg, writes=xgb[:ntl])
    for ti in range(ntl):
        for kq in range(4):
            rot["tp"] ^= 1
            tp = tpb[rot["tp"]]
            for k4 in range(4):
                kc = kq * 4 + k4
                kb.op("pe", lambda e, tp=tp, k4=k4, kc=kc, ti=ti: e.transpose(
                    out=tp.t[:, k4 * 128:(k4 + 1) * 128], in_=xg.t[:, ti, kc * 128:(kc + 1) * 128], identity=idb.t[:]),
                    reads=[xgb[ti], idb], writes=[(tp, True)])
            rot["ev"] ^= 1
            for k4 in range(4):
                kc = kq * 4 + k4
                if rot["ev"]:
                    kb.op("dve", lambda e, tp=tp, k4=k4, kc=kc, ti=ti: e.tensor_scalar(
                        out=hT.t[:, kc, ti * 128:(ti + 1) * 128], in0=tp.t[:, k4 * 128:(k4 + 1) * 128],
                        scalar1=mod.t[:, s, 1, kc:kc + 1], scalar2=mod.t[:, s, 0, kc:kc + 1],
                        op0=ALU.mult, op1=ALU.add), reads=[tp, mod], writes=[(hT, True)])
                else:
                    kb.op("act", lambda e, tp=tp, k4=k4, kc=kc, ti=ti: e.activation(
                        out=hT.t[:, kc, ti * 128:(ti + 1) * 128], in_=tp.t[:, k4 * 128:(k4 + 1) * 128],
                        func=AF.Identity, bias=mod.t[:, s, 0, kc:kc + 1], scale=mod.t[:, s, 1, kc:kc + 1]),
                        reads=[tp, mod], writes=[(hT, True)])


def build_proj(groups=GROUPS):
    nc = bass.Bass("TRN2", target_bir_lowering=False)
    xin = nc.dram_tensor("xin", [NTOK, D], F32, kind="ExternalInput").ap()
    modc = nc.dram_tensor("modc", [128, 96], F32, kind="ExternalInput").ap()
    ident = nc.dram_tensor("ident", [128, 128], F32, kind="ExternalInput").ap()
    w_fm = nc.dram_tensor("w_fm", [D, WFM_COLS], F32, kind="ExternalInput").ap()
    w_tm = nc.dram_tensor("w_tm", [D, WTM_COLS], F32, kind="ExternalInput").ap()
    rope = nc.dram_tensor("rope", [2, 128, NTOK], F32, kind="ExternalInput").ap()
    gnp = nc.dram_tensor("gnp", [2, 512], F32, kind="ExternalInput").ap()
    wsT = nc.dram_tensor("wsT", [128, 512], F32, kind="ExternalInput").ap()
    bsr = nc.dram_tensor("bsr", [1, 512], F32, kind="ExternalInput").ap()
    fm = nc.dram_tensor("fm", [NFM, 128, NTOK], BF16, kind="ExternalOutput").ap()
    tm = nc.dram_tensor("tm", [NTOK, NTM], BF16, kind="ExternalOutput").ap()
    with ExitStack() as es:
        kb = KB(nc, es)
        mod = kb.sbuf("mod", [128, 2, 3, 16], F32)
        idb = kb.sbuf("ident_sb", [128, 128], F32)
        rp = kb.sbuf("rope_sb", [128, 2, NTOK], F32)
        gn = kb.sbuf("gn", [128, 2, 512], F32)
        ws = kb.sbuf("ws", [128, 512], BF16)
        bsb = kb.sbuf("bsb", [128, 512], F32)
        xg = kb.sbuf("xg", [128, 4, D], F32)
        xgb = [Buf("xg%d" % i) for i in range(4)]
        hT = kb.sbuf("hT", [128, 16, 512], BF16)
        uT = kb.sbuf("uT", [128, 4, 512], BF16)
        wb = [kb.sbuf("wb%d" % i, [128, 16, 256], BF16) for i in range(2)]
        wt = [kb.sbuf("wt%d" % i, [128, 16, 512], BF16) for i in range(2)]
        t1 = [kb.sbuf("t1_%d" % i, [128, 512], F32) for i in range(2)]
        t2 = [kb.sbuf("t2_%d" % i, [128, 512], F32) for i in range(2)]
        stg = [kb.sbuf("stg%d" % i, [128, 512], BF16) for i in range(3)]
        vv = [kb.sbuf("vv%d" % i, [128, 512], F32) for i in range(2)]
        vt = [kb.sbuf("vt%d" % i, [128, 512], BF16) for i in range(2)]
        ybs = [kb.sbuf("ybs%d" % i, [128, 512], F32) for i in range(2)]
        stats = [kb.sbuf("stats%d" % i, [128, 6], F32) for i in range(2)]
        mv = [kb.sbuf("mv%d" % i, [128, 2], F32) for i in range(2)]
        rs = [kb.sbuf("rs%d" % i, [128, 2], F32) for i in range(2)]
        tpb = [kb.psum("tp%d" % i, [128, 512]) for i in range(2)]
        pp = [kb.psum("pp%d" % i, [128, 512]) for i in range(4)]
        pt = [kb.psum("pt%d" % i, [128, 512]) for i in range(2)]

        kb.dma("sp", mod.t[:].rearrange("p a b c -> p (a b c)"), modc, mod, writes=[mod])
        kb.dma("sp", idb.t[:], ident, idb, writes=[idb])
        kb.dma("sp", rp.t[:, 0, :], rope[0], rp, writes=[(rp, True)])
        kb.dma("sp", rp.t[:, 1, :], rope[1], rp, writes=[(rp, True)])
        kb.dma("sp", gn.t[:, 0, :], gnp[0:1, :].partition_broadcast(128), gn, writes=[(gn, True)])
        kb.dma("sp", gn.t[:, 1, :], gnp[1:2, :].partition_broadcast(128), gn, writes=[(gn, True)])
        kb.dma("sp", bsb.t[:], bsr[0:1, :].partition_broadcast(128), bsb, writes=[bsb])
        kb.dma("pool", ws.t[:], wsT, ws, writes=[ws])
        kb.op("dve", lambda e: e.tensor_scalar(out=mod.t[:, :, 1, :], in0=mod.t[:, :, 1, :], scalar1=1.0, scalar2=None,
                                                op0=ALU.add), reads=[mod], writes=[mod])
        rot = {"tp": 0, "ev": 0, "stg": 0, "wt": 0}

        def next_stg():
            rot["stg"] = (rot["stg"] + 1) % 3
            return stg[rot["stg"]]

        for (t0, ntl) in groups:
            ntok = ntl * 128
            s = 1 if t0 >= 16 else 0
            tk = slice(t0 * 128, t0 * 128 + ntok)
            load_x_transpose_mod(kb, xin, t0, ntl, s, xg, xgb, hT, idb, mod, tpb, rot)
            for blk in range(20):
                wbb = wb[blk % 2]
                kb.dma("pool", wbb.t[:], w_fm[:, blk * 256:(blk + 1) * 256].rearrange("(kc p) c -> p kc c", p=128),
                       wbb, writes=[wbb])
                pa, pb = pp[(blk % 2) * 2], pp[(blk % 2) * 2 + 1]
                for jj, pq in ((0, pa), (1, pb)):
                    kb.mm_group(pq.t[:, :ntok], [(wbb.t[:, kc, jj * 128:(jj + 1) * 128], hT.t[:, kc, :ntok]) for kc in range(16)],
                                reads=[wbb, hT], out_buf=pq)
                if blk < 14:
                    oc = (FM_QA + blk) if blk < 4 else (FM_KA + blk - 4) if blk < 8 else (FM_QC + blk - 8) if blk < 12 \
                        else (FM_KC + blk - 12)
                    a1, a2, sg_ = t1[blk % 2], t2[blk % 2], next_stg()
                    kb.op("dve", lambda e, pa=pa, a1=a1: e.tensor_tensor(out=a1.t[:, :ntok], in0=pa.t[:, :ntok], in1=rp.t[:, 0, tk],
                                                                           op=ALU.mult), reads=[pa, rp], writes=[a1])
                    kb.op("dve", lambda e, pb=pb, a2=a2: e.tensor_tensor(out=a2.t[:, :ntok], in0=pb.t[:, :ntok], in1=rp.t[:, 1, tk],
                                                                           op=ALU.mult), reads=[pb, rp], writes=[a2])
                    kb.op("pool", lambda e, a1=a1, a2=a2, sg_=sg_: e.tensor_tensor(out=sg_.t[:, :ntok], in0=a1.t[:, :ntok],
                                                                                      in1=a2.t[:, :ntok], op=ALU.add),
                          reads=[a1, a2], writes=[sg_])
                    kb.dma("sp", fm[oc, :, tk], sg_.t[:, :ntok], sg_, reads=[sg_], final=True)
                elif blk < 18:
                    for jj, pq in ((0, pa), (1, pb)):
                        ci = (blk - 14) * 2 + jj
                        oc = (FM_QD + ci) if ci < 4 else (FM_KD + ci - 4)
                        sg_ = next_stg()
                        kb.op("act", lambda e, pq=pq, sg_=sg_: e.activation(out=sg_.t[:, :ntok], in_=pq.t[:, :ntok], func=AF.Identity),
                              reads=[pq], writes=[sg_])
                        kb.dma("sp", fm[oc, :, tk], sg_.t[:, :ntok], sg_, reads=[sg_], final=True)
                else:
                    for jj, pq in ((0, pa), (1, pb)):
                        ci = (blk - 18) * 2 + jj
                        kb.op("act", lambda e, pq=pq, ci=ci: e.activation(out=uT.t[:, ci, :ntok], in_=pq.t[:, :ntok],
                                                                            func=AF.Gelu_apprx_tanh), reads=[pq], writes=[(uT, True)])
            for bi, (c0, ncol) in enumerate(((0, 512), (512, 512), (1024, 512), (1536, 128))):
                rot["wt"] ^= 1
                wtb = wt[rot["wt"]]
                kb.dma("pool", wtb.t[:, :, :ncol], w_tm[:, c0:c0 + ncol].rearrange("(kc p) c -> p kc c", p=128),
                       wtb, writes=[wtb])
                for ti in range(ntl):
                    ptb = pt[ti % 2]
                    tks = slice((t0 + ti) * 128, (t0 + ti + 1) * 128)
                    kb.mm_group(ptb.t[:, :ncol], [(hT.t[:, kc, ti * 128:(ti + 1) * 128], wtb.t[:, kc, :ncol]) for kc in range(16)],
                                reads=[wtb, hT], out_buf=ptb)
                    if bi == 0:
                        v_, vb_, st_, mv_, rs_, yb_ = vv[ti % 2], vt[ti % 2], stats[ti % 2], mv[ti % 2], rs[ti % 2], ybs[ti % 2]
                        kb.op("act", lambda e, ptb=ptb, v_=v_: e.activation(out=v_.t[:], in_=ptb.t[:], func=AF.Gelu_apprx_tanh),
                              reads=[ptb], writes=[v_])
                        kb.op("dve", lambda e, v_=v_, st_=st_: e.bn_stats(out=st_.t[:], in_=v_.t[:]), reads=[v_], writes=[st_])
                        kb.op("dve", lambda e, st_=st_, mv_=mv_: e.bn_aggr(out=mv_.t[:], in_=st_.t[:]), reads=[st_], writes=[mv_])
                        kb.op("act", lambda e, mv_=mv_, rs_=rs_: e.activation(out=rs_.t[:, 0:1], in_=mv_.t[:, 1:2], func=AF.Sqrt,
                                                                                bias=LN_EPS, scale=1.0), reads=[mv_], writes=[rs_])
                        kb.op("dve", lambda e, rs_=rs_: e.reciprocal(out=rs_.t[:, 0:1], in_=rs_.t[:, 0:1]), reads=[rs_], writes=[rs_])
                        kb.op("dve", lambda e, rs_=rs_, mv_=mv_: e.tensor_scalar(out=rs_.t[:, 1:2], in0=mv_.t[:, 0:1],
                                                                                   scalar1=rs_.t[:, 0:1], scalar2=-1.0,
                                                                                   op0=ALU.mult, op1=ALU.mult),
                              reads=[mv_, rs_], writes=[rs_])
                        kb.op("act", lambda e, v_=v_, rs_=rs_: e.activation(out=v_.t[:], in_=v_.t[:], func=AF.Identity,
                                                                              bias=rs_.t[:, 1:2], scale=rs_.t[:, 0:1]),
                              reads=[v_, rs_], writes=[v_])
                        kb.op("pool", lambda e, v_=v_: e.tensor_tensor(out=v_.t[:], in0=v_.t[:], in1=gn.t[:, 0, :], op=ALU.mult),
                              reads=[v_, gn], writes=[v_])
                        kb.op("dve", lambda e, v_=v_, vb_=vb_: e.tensor_tensor(out=vb_.t[:], in0=v_.t[:], in1=gn.t[:, 1, :], op=ALU.add),
                              reads=[v_, gn], writes=[vb_])
                        rot["tp"] ^= 1
                        mp = tpb[rot["tp"]]
                        for g in range(4):
                            kb.mm_group(mp.t[:, g * 128:(g + 1) * 128], [(vb_.t[:, g * 128:(g + 1) * 128], ws.t[:, g * 128:(g + 1) * 128])],
                                        reads=[vb_, ws], out_buf=mp, part=True)
                        kb.op("dve", lambda e, mp=mp, yb_=yb_: e.tensor_tensor(out=yb_.t[:], in0=mp.t[:], in1=bsb.t[:], op=ALU.add),
                              reads=[mp, bsb], writes=[yb_])
                        sg_ = next_stg()
                        kb.op("pool", lambda e, yb_=yb_, sg_=sg_, ti=ti: e.tensor_tensor(
                            out=sg_.t[:].rearrange("p (g t) -> p g t", g=4), in0=yb_.t[:].rearrange("p (g t) -> p g t", g=4),
                            in1=uT.t[:, :, ti * 128:(ti + 1) * 128], op=ALU.mult), reads=[yb_, uT], writes=[sg_])
                        kb.dma("sp", fm[FM_YB:FM_YB + 4, :, tks].rearrange("g p t -> p g t"),
                               sg_.t[:].rearrange("p (g t) -> p g t", g=4), sg_, reads=[sg_], final=True)
                    else:
                        oc0 = {1: 0, 2: 640, 3: 512}[bi]
                        sg_ = next_stg()
                        if ti % 2:
                            kb.op("act", lambda e, ptb=ptb, sg_=sg_: e.activation(out=sg_.t[:, :ncol], in_=ptb.t[:, :ncol], func=AF.Identity),
                                  reads=[ptb], writes=[sg_])
                        else:
                            kb.op("dve", lambda e, ptb=ptb, sg_=sg_: e.tensor_copy(out=sg_.t[:, :ncol], in_=ptb.t[:, :ncol]),
                                  reads=[ptb], writes=[sg_])
                        kb.dma("sp", tm[tks, oc0:oc0 + ncol], sg_.t[:, :ncol], sg_, reads=[sg_], final=True)
        kb.finish()
    return nc


PROJ_CUTS = np.cumsum([512, 512, 512, 512, 512, 512, 128, 128, 512, 512, 512])[:-1].tolist()


def swap_cols(w):
    k, n = w.shape
    return np.ascontiguousarray(w.reshape(k, n // 64, 2, 32)[:, :, ::-1, :]).reshape(k, n)


def build_proj_weights(w):
    aq, ak, av, bu, bv, cq, ck, cv, dq, dk, dv = np.split(w, PROJ_CUTS, axis=1)
    chunks = []

    def pairs(m):
        ms = swap_cols(m)
        for i in range(m.shape[1] // 128):
            chunks.append(m[:, i * 128:(i + 1) * 128])
            chunks.append(ms[:, i * 128:(i + 1) * 128])

    pairs(aq)
    pairs(ak)
    pairs(cq)
    pairs(np.concatenate([ck[:, :64], ck[:, :64], ck[:, 64:], ck[:, 64:]], 1))
    for m in (dq, dk, bu):
        for i in range(4):
            chunks.append(m[:, i * 128:(i + 1) * 128])
    w_fm = np.ascontiguousarray(np.concatenate(chunks, 1))
    w_tm = np.ascontiguousarray(np.concatenate([bv, av, dv, cv], 1))
    return w_fm, w_tm


def rope_tables(core):
    t = np.arange(core * 2048, (core + 1) * 2048)
    row = (t // 64).astype(np.float32)
    col = (t % 64).astype(np.float32)
    inv = (np.float32(10000.0) ** (-np.arange(16, dtype=np.float32) / np.float32(16))).astype(np.float32)
    ang = np.concatenate([row[:, None] * inv, col[:, None] * inv], -1).astype(np.float32)
    cos, sin = np.cos(ang).astype(np.float32), np.sin(ang).astype(np.float32)
    tab = np.zeros((2, 128, NTOK), np.float32)
    tab[0, :, 2048:] = 1.0
    for p in range(128):
        d = p % 64
        j = d % 32
        tab[0, p, :2048] = cos[:, j]
        tab[1, p, :2048] = -sin[:, j] if d < 32 else sin[:, j]
    return tab


def mod_cols(m_lat, m_ctx, idx):
    out = np.zeros((128, 2, 3, 16), np.float32)
    for s, m in enumerate((m_lat, m_ctx)):
        mm = m.reshape(9, 16, 128)
        for wi, i in enumerate(idx):
            if i is not None:
                out[:, s, wi, :] = mm[i].T
    return out.reshape(128, 96)


ORDER = (0, 2, 1, 3)
NKA = 256 + 16384
NEXT = 22 * 128
D_OFFS = {0: (-2, -1, 0, 1, 2, 3), 1: (-2, -1, 0, 1, 2), 2: (-2, -1, 0, 1, 2), 3: (-2, -1, 0, 1, 2), 4: (-3, -2, -1, 0, 1, 2)}


def d_class(T):
    return 0 if T == 0 else 1 if T == 1 else 3 if T == 14 else 4 if T == 15 else 2


def build_attn(do=("A", "C", "D", "O"), qgroups=GROUPS, dbg=False, stage=9, ctiles=18):
    nc = bass.Bass("TRN2", target_bir_lowering=False)
    xin = nc.dram_tensor("xin", [NTOK, D], F32, kind="ExternalInput").ap()
    fm = nc.dram_tensor("fm", [NFM, 128, NTOK], BF16, kind="ExternalInput").ap()
    ka = nc.dram_tensor("ka", [4, 128, NKA], BF16, kind="ExternalInput").ap()
    va = nc.dram_tensor("va", [NKA, 512], BF16, kind="ExternalInput").ap()
    kc = nc.dram_tensor("kc", [2, 128, NEXT], BF16, kind="ExternalInput").ap()
    vc = nc.dram_tensor("vc", [NEXT, 128], BF16, kind="ExternalInput").ap()
    kd = nc.dram_tensor("kd", [4, 128, NEXT], BF16, kind="ExternalInput").ap()
    vd = nc.dram_tensor("vd", [NEXT, 512], BF16, kind="ExternalInput").ap()
    cmask = nc.dram_tensor("cmask", [4, 128, 512], F32, kind="ExternalInput").ap()
    dbias = nc.dram_tensor("dbias", [5, 7, 2, 128, 512], F32, kind="ExternalInput").ap()
    alam = nc.dram_tensor("alam", [1, 258], F32, kind="ExternalInput").ap()
    sink = nc.dram_tensor("sink", [1, 8], F32, kind="ExternalInput").ap()
    ident = nc.dram_tensor("ident", [128, 128], F32, kind="ExternalInput").ap()
    gate = nc.dram_tensor("gate", [2, D], F32, kind="ExternalInput").ap()
    lnp = nc.dram_tensor("lnp", [2, D], F32, kind="ExternalInput").ap()
    w_ab = nc.dram_tensor("w_ab", [1024, D], F32, kind="ExternalInput").ap()
    w_cd = nc.dram_tensor("w_cd", [1024, D], F32, kind="ExternalInput").ap()
    xout = nc.dram_tensor("xout", [NTOK, D], F32, kind="ExternalOutput").ap()
    dk_ = {"kind": "ExternalOutput"} if dbg else {}
    catA = nc.dram_tensor("catA", [4, 128, NTOK], BF16, **dk_).ap()
    catC = nc.dram_tensor("catC", [8, 64, NTOK], BF16, **dk_).ap()
    catD = nc.dram_tensor("catD", [8, 64, NTOK], BF16, **dk_).ap()
    with ExitStack() as es:
        kb = KB(nc, es)
        catAb, catCb, catDb = Buf("catA"), Buf("catC"), Buf("catD")
        ps = [kb.psum("ps%d" % i, [128, 512]) for i in range(8)]
        ones = kb.sbuf("ones", [128, 128], BF16)
        onesf = kb.sbuf("onesf", [128, 128], F32)
        lam = kb.sbuf("lam", [128, 264], F32)
        esk = kb.sbuf("esk", [128, 8], F32)
        kb.op("pool", lambda e: e.memset(ones.t[:], 1.0), writes=[ones])
        kb.op("pool", lambda e: e.memset(onesf.t[:], 1.0), writes=[onesf])
        kb.dma("sp", lam.t[:, 0:258], alam[0:1, :].partition_broadcast(128), lam, writes=[lam])
        kb.dma("sp", esk.t[:], sink[0:1, :].partition_broadcast(128), esk, writes=[esk])
        lsc = kb.sbuf("lsc", [128, 128], F32)
        kb.op("dve", lambda e: e.tensor_tensor(out=lsc.t[:, 0:64], in0=lam.t[:, 0:64], in1=lam.t[:, 64:128], op=ALU.mult),
              reads=[lam], writes=[lsc])
        kb.op("dve", lambda e: e.tensor_tensor(out=lsc.t[:, 64:128], in0=lam.t[:, 128:192], in1=lam.t[:, 192:256], op=ALU.mult),
              reads=[lam, lsc], writes=[lsc])
        kb.op("dve", lambda e: e.reduce_sum(out=lam.t[:, 258:260], in_=lsc.t[:].rearrange("p (a b) -> p a b", a=2),
                                            axis=mybir.AxisListType.X), reads=[lsc, lam], writes=[lam])
        kb.op("act", lambda e: e.activation(out=lam.t[:, 258:260], in_=lam.t[:, 258:260], func=AF.Exp), reads=[lam], writes=[lam])
        kb.op("act", lambda e: e.activation(out=esk.t[:], in_=esk.t[:], func=AF.Exp), reads=[esk], writes=[esk])
        kb.op("dve", lambda e: e.tensor_tensor(out=lam.t[:, 260:261], in0=lam.t[:, 259:260], in1=lam.t[:, 258:259], op=ALU.subtract),
              reads=[lam], writes=[lam])
        kb.op("dve", lambda e: e.tensor_tensor(out=lam.t[:, 260:261], in0=lam.t[:, 260:261], in1=lam.t[:, 256:257], op=ALU.subtract),
              reads=[lam], writes=[lam])
        NLAM, OML = lam.t[:, 260:261], lam.t[:, 257:258]

        if "A" in do:
          with ExitStack() as pes:
            qa = kb.sbuf("qa", [128, 512], BF16, pes)
            kblk = [kb.sbuf("kblk%d" % i, [128, 2048], BF16, pes) for i in range(2)]
            vblk = [kb.sbuf("vblk%d" % i, [128, 16, 128], BF16, pes) for i in range(2)]
            pt_ = [[kb.sbuf("p%d_%d" % (c, i), [128, 512], BF16, pes) for i in range(2)] for c in range(2)]
            r0 = kb.sbuf("r0", [128, 512], F32, pes)
            r1 = kb.sbuf("r1", [128, 512], F32, pes)
            a0 = kb.sbuf("a0", [128, 512], F32, pes)
            a1 = kb.sbuf("a1", [128, 512], F32, pes)
            sq = kb.sbuf("sq", [128, 512], F32, pes)
            yst = [kb.sbuf("yst%d" % i, [128, 512], BF16, pes) for i in range(2)]
            O0, O1, Z0, Z1 = ps[4], ps[5], ps[6], ps[7]
            nblk = 0
            nst = 0
            for h in range(4):
                for (t0, ntl) in qgroups:
                    nq = ntl * 128
                    tk = slice(t0 * 128, t0 * 128 + nq)
                    isctx = t0 >= 16
                    kb.dma("sp", qa.t[:, :nq], fm[FM_QA + h, :, tk], qa, writes=[qa])
                    blocks = [(0, 256)] + ([] if isctx else [(256 + i * 2048, 2048) for i in range(8)])
                    units = []
                    for (k0, ksz) in blocks:
                        for kt in range(ksz // 128):
                            units.append((k0, ksz, kt))
                    nu = len(units)
                    cur = {}

                    def emit_scores(u):
                        nonlocal nblk
                        k0, ksz, kt = units[u]
                        if kt == 0:
                            kbb, vbb = kblk[nblk % 2], vblk[nblk % 2]
                            nblk += 1
                            kb.dma("sp", kbb.t[:, :ksz], ka[h, :, k0:k0 + ksz], kbb, writes=[kbb])
                            kb.dma("sp", vbb.t[:, :ksz // 128, :],
                                   va[k0:k0 + ksz, h * 128:(h + 1) * 128].rearrange("(t p) c -> p t c", p=128), vbb, writes=[vbb])
                            cur["kv"] = (kbb, vbb)
                        kbb, vbb = cur["kv"]
                        par = u % 2
                        S0, S1 = ps[par * 2], ps[par * 2 + 1]
                        kb.mm(S0.t[:, :nq], kbb.t[0:64, kt * 128:(kt + 1) * 128], qa.t[0:64, :nq], True, True, [kbb, qa], S0)
                        kb.mm(S1.t[:, :nq], kbb.t[64:128, kt * 128:(kt + 1) * 128], qa.t[64:128, :nq], True, True, [kbb, qa], S1)
                        cur[u] = (vbb, kt)

                    def emit_exp(u):
                        par = u % 2
                        S0, S1 = ps[par * 2], ps[par * 2 + 1]
                        P0, P1 = pt_[0][par], pt_[1][par]
                        kb.op("act", lambda e: e.activation(out=P0.t[:, :nq], in_=S0.t[:, :nq], func=AF.Exp, scale=0.125),
                              reads=[S0], writes=[P0])
                        kb.op("act", lambda e: e.activation(out=P1.t[:, :nq], in_=S1.t[:, :nq], func=AF.Exp, scale=0.125),
                              reads=[S1], writes=[P1])

                    def emit_pv(u):
                        par = u % 2
                        P0, P1 = pt_[0][par], pt_[1][par]
                        vbb, kt = cur.pop(u)
                        first, last = u == 0, u == nu - 1
                        kb.mm(O0.t[:, :nq], vbb.t[:, kt, :], P0.t[:, :nq], first, last, [vbb, P0], O0, inc=False)
                        kb.mm(Z0.t[:, :nq], ones.t[:], P0.t[:, :nq], first, last, [ones, P0], Z0, inc=False)
                        kb.mm(O1.t[:, :nq], vbb.t[:, kt, :], P1.t[:, :nq], first, last, [vbb, P1], O1, inc=False)
                        kb.mm(Z1.t[:, :nq], ones.t[:], P1.t[:, :nq], first, last, [ones, P1], Z1, inc=True)

                    emit_scores(0)
                    emit_exp(0)
                    for u in range(1, nu):
                        emit_scores(u)
                        emit_pv(u - 1)
                        emit_exp(u)
                    emit_pv(nu - 1)
                    kb.op("dve", lambda e: e.reciprocal(out=r0.t[:, :nq], in_=Z0.t[:, :nq]), reads=[Z0], writes=[r0])
                    kb.op("dve", lambda e: e.reciprocal(out=r1.t[:, :nq], in_=Z1.t[:, :nq]), reads=[Z1], writes=[r1])
                    kb.op("dve", lambda e: e.tensor_tensor(out=a0.t[:, :nq], in0=O0.t[:, :nq], in1=r0.t[:, :nq], op=ALU.mult),
                          reads=[O0, r0], writes=[a0])
                    kb.op("dve", lambda e: e.tensor_tensor(out=a1.t[:, :nq], in0=O1.t[:, :nq], in1=r1.t[:, :nq], op=ALU.mult),
                          reads=[O1, r1], writes=[a1])
                    kb.op("dve", lambda e: e.scalar_tensor_tensor(out=a0.t[:, :nq], in0=a1.t[:, :nq], scalar=NLAM, in1=a0.t[:, :nq],
                                                                  op0=ALU.mult, op1=ALU.add), reads=[a0, a1, lam], writes=[a0])
                    kb.op("pool", lambda e: e.tensor_tensor(out=sq.t[:, :nq], in0=a0.t[:, :nq], in1=a0.t[:, :nq], op=ALU.mult),
                          reads=[a0], writes=[sq])
                    MS = ps[0]
                    kb.mm(MS.t[:, :nq], onesf.t[:], sq.t[:, :nq], True, True, [onesf, sq], MS)
                    kb.op("act", lambda e, MS=MS: e.activation(out=r0.t[:, :nq], in_=MS.t[:, :nq], func=AF.Sqrt, bias=LN_EPS, scale=1.0 / 128),
                          reads=[MS], writes=[r0])
                    kb.op("dve", lambda e: e.reciprocal(out=r0.t[:, :nq], in_=r0.t[:, :nq]), reads=[r0], writes=[r0])
                    ys = yst[nst % 2]
                    nst += 1
                    kb.op("dve", lambda e, ys=ys: e.scalar_tensor_tensor(out=ys.t[:, :nq], in0=a0.t[:, :nq], scalar=OML, in1=r0.t[:, :nq],
                                                                         op0=ALU.mult, op1=ALU.mult), reads=[a0, r0, lam], writes=[ys])
                    kb.dma("sp", catA[h, :, tk], ys.t[:, :nq], ys, reads=[ys], writes=[(catAb, True)])
            kb.barrier()

        if "C" in do or "D" in do:
          with ExitStack() as pes:
            qc = kb.sbuf("qc", [128, 4, NTOK], BF16, pes)
            kcx = kb.sbuf("kcx", [128, 2, NEXT], BF16, pes)
            vcx = kb.sbuf("vcx", [128, 44, 65], BF16, pes)
            qd = kb.sbuf("qd", [128, 4, NTOK], BF16, pes)
            kdx = kb.sbuf("kdx", [128, 4, NEXT], BF16, pes)
            vdx = kb.sbuf("vdx", [128, 176, 65], BF16, pes)
            vstage = kb.sbuf("vstage", [128, 22 * 512], BF16, pes)
            cmk = kb.sbuf("cmk", [128, 4, 512], BF16, pes)
            db = [kb.sbuf("db%d" % i, [128, 512], F32, pes) for i in range(2)]
            tS = [kb.sbuf("tS%d" % i, [128, 512], F32, pes) for i in range(2)]
            pcd = [kb.sbuf("pcd%d" % i, [128, 512], BF16, pes) for i in range(2)]
            zr = kb.sbuf("zr", [128, 512], F32, pes)
            bcs = kb.sbuf("bcs", [128, 512], F32, pes)
            stgc = [kb.sbuf("stgc%d" % i, [128, 512], BF16, pes) for i in range(2)]
            kb.dma("sp", qc.t[:], fm[FM_QC:FM_QC + 4].rearrange("c p t -> p c t"), qc, writes=[qc])
            kb.dma("sp", qd.t[:], fm[FM_QD:FM_QD + 4].rearrange("c p t -> p c t"), qd, writes=[qd])
            kb.dma("sp", kcx.t[:], kc.rearrange("c p t -> p c t"), kcx, writes=[kcx])
            kb.dma("sp", kdx.t[:], kd.rearrange("c p t -> p c t"), kdx, writes=[kdx])
            kb.dma("pool", cmk.t[:], cmask.rearrange("m p c -> p m c"), cmk, writes=[cmk])
            kb.op("pool", lambda e: e.memset(vcx.t[:], 1.0), writes=[vcx])
            kb.op("pool", lambda e: e.memset(vdx.t[:], 1.0), writes=[vdx])
            kb.dma("sp", vstage.t[:, 0:22 * 128].rearrange("p (t c) -> p t c", c=128), vc.rearrange("(t p) c -> p t c", p=128),
                   vstage, writes=[vstage])
            kb.op("dve", lambda e: e.tensor_copy(out=vcx.t[:, :, 0:64], in_=vstage.t[:, 0:22 * 128].rearrange("p (a d) -> p a d", d=64)),
                  reads=[vstage], writes=[vcx])
            kb.dma("sp", vstage.t[:].rearrange("p (t c) -> p t c", c=512), vd.rearrange("(t p) c -> p t c", p=128),
                   vstage, writes=[vstage])
            kb.op("dve", lambda e: e.tensor_copy(out=vdx.t[:, :, 0:64], in_=vstage.t[:].rearrange("p (a d) -> p a d", d=64)),
                  reads=[vstage], writes=[vdx])
            cnt = {"s": 0, "o": 0, "p": 0, "b": 0, "g": 0, "db": 0}

            def finalize_cd(O, sink_cols, dst, dst_buf):
                if sink_cols is not None:
                    for sl, g in enumerate(ORDER):
                        kb.op("dve", lambda e, g=g, sl=sl: e.tensor_scalar(
                            out=zr.t[64:65, sl * 128:(sl + 1) * 128], in0=O.t[64:65, sl * 128:(sl + 1) * 128],
                            scalar1=esk.t[64:65, sink_cols + g:sink_cols + g + 1], scalar2=None, op0=ALU.add),
                            reads=[O, esk], writes=[(zr, True)])
                else:
                    kb.op("dve", lambda e: e.tensor_copy(out=zr.t[64:65, :], in_=O.t[64:65, :]), reads=[O], writes=[zr])
                kb.op("dve", lambda e: e.reciprocal(out=zr.t[64:65, :], in_=zr.t[64:65, :]), reads=[zr], writes=[zr])
                BC = ps[6 + cnt["b"] % 2]
                cnt["b"] += 1
                kb.mm(BC.t[0:64, :], onesf.t[64:65, 0:64], zr.t[64:65, :], True, True, [onesf, zr], BC)
                kb.op("act", lambda e: e.activation(out=bcs.t[0:64, :], in_=BC.t[0:64, :], func=AF.Identity), reads=[BC], writes=[bcs])
                sg_ = stgc[cnt["g"] % 2]
                cnt["g"] += 1
                kb.op("dve", lambda e: e.tensor_tensor(out=sg_.t[0:64, :].rearrange("p (a b t) -> p b a t", a=2, b=2),
                                                        in0=O.t[0:64, :].rearrange("p (b a t) -> p b a t", a=2, b=2),
                                                        in1=bcs.t[0:64, :].rearrange("p (b a t) -> p b a t", a=2, b=2), op=ALU.mult),
                      reads=[O, bcs], writes=[sg_])
                kb.dma("sp", dst.rearrange("g p t -> p g t"), sg_.t[0:64, :].rearrange("p (g t) -> p g t", g=4), sg_,
                       reads=[sg_], writes=[(dst_buf, True)])

            def run_pipeline(steps):
                n = len(steps)
                pend_fin = []
                steps[0]["scores"]()
                steps[0]["exp"]()
                for u in range(1, n):
                    steps[u]["scores"]()
                    steps[u - 1]["pv"]()
                    for f in pend_fin:
                        f()
                    pend_fin = [steps[u - 1]["fin"]] if "fin" in steps[u - 1] else []
                    steps[u]["exp"]()
                steps[n - 1]["pv"]()
                for f in pend_fin:
                    f()
                if "fin" in steps[n - 1]:
                    steps[n - 1]["fin"]()

            def c_step(T, kv, ki, nk, e_, mk, OC, u):
                qtk = slice(T * 128, (T + 1) * 128)
                Sa, Sb = ps[(u % 2) * 2], ps[(u % 2) * 2 + 1]
                P = pcd[u % 2]

                def scores():
                    for sl, g in enumerate(ORDER):
                        j = kv * 4 + g
                        ch, hf = j // 2, j % 2
                        Sx = Sa if hf == 0 else Sb
                        kb.mm(Sx.t[:, (sl % 2) * 128:(sl % 2 + 1) * 128], kcx.t[hf * 64:(hf + 1) * 64, kv, e_ * 128:(e_ + 1) * 128],
                              qc.t[hf * 64:(hf + 1) * 64, ch, qtk], True, True, [kcx, qc], Sx, inc=(sl % 2 == 1))

                def exp():
                    kb.op("act", lambda e: e.activation(out=P.t[:, 0:256], in_=Sa.t[:, 0:256], func=AF.Exp, scale=0.125),
                          reads=[Sa], writes=[(P, True)])
                    kb.op("act", lambda e: e.activation(out=P.t[:, 256:512], in_=Sb.t[:, 0:256], func=AF.Exp, scale=0.125),
                          reads=[Sb], writes=[(P, True)])
                    if mk is not None:
                        kb.op("dve", lambda e: e.tensor_tensor(out=P.t[:], in0=P.t[:], in1=cmk.t[:, mk, :], op=ALU.mult),
                              reads=[P, cmk], writes=[P])

                def pv():
                    kb.mm(OC.t[0:65, :], vcx.t[:, e_ * 2 + kv, :], P.t[:], ki == 0, ki == nk - 1, [vcx, P], OC)

                st = {"scores": scores, "exp": exp, "pv": pv}
                if ki == nk - 1:
                    st["fin"] = lambda: finalize_cd(OC, kv * 4, catC[kv * 4:(kv + 1) * 4, :, qtk], catCb)
                return st

            def d_step(T, hg, ki, nk, e_, bi, OD, u):
                qtk = slice(T * 128, (T + 1) * 128)
                Sa, Sb = ps[(u % 2) * 2], ps[(u % 2) * 2 + 1]
                P = pcd[u % 2]

                def scores():
                    for sl, i in enumerate(ORDER):
                        h = hg * 4 + i
                        ch, hf = h // 2, h % 2
                        Sx = Sa if hf == 0 else Sb
                        kb.mm(Sx.t[:, (sl % 2) * 128:(sl % 2 + 1) * 128], kdx.t[hf * 64:(hf + 1) * 64, ch, e_ * 128:(e_ + 1) * 128],
                              qd.t[hf * 64:(hf + 1) * 64, ch, qtk], True, True, [kdx, qd], Sx, inc=(sl % 2 == 1))

                def exp():
                    if bi is None:
                        kb.op("act", lambda e: e.activation(out=P.t[:, 0:256], in_=Sa.t[:, 0:256], func=AF.Exp, scale=0.125),
                              reads=[Sa], writes=[(P, True)])
                        kb.op("act", lambda e: e.activation(out=P.t[:, 256:512], in_=Sb.t[:, 0:256], func=AF.Exp, scale=0.125),
                              reads=[Sb], writes=[(P, True)])
                    else:
                        dbb, tsb = db[cnt["db"] % 2], tS[cnt["db"] % 2]
                        cnt["db"] += 1
                        kb.dma("sp", dbb.t[:], dbias[bi[0], bi[1], hg], dbb, writes=[dbb])
                        kb.op("dve", lambda e: e.scalar_tensor_tensor(
                            out=tsb.t[:, 0:256], in0=Sa.t[:, 0:256], scalar=0.125, in1=dbb.t[:, 0:256], op0=ALU.mult, op1=ALU.add),
                            reads=[Sa, dbb], writes=[(tsb, True)])
                        kb.op("dve", lambda e: e.scalar_tensor_tensor(
                            out=tsb.t[:, 256:512], in0=Sb.t[:, 0:256], scalar=0.125, in1=dbb.t[:, 256:512], op0=ALU.mult, op1=ALU.add),
                            reads=[Sb, dbb], writes=[(tsb, True)])
                        kb.op("act", lambda e: e.activation(out=P.t[:], in_=tsb.t[:], func=AF.Exp), reads=[tsb], writes=[P])

                def pv():
                    for sl, i in enumerate(ORDER):
                        h = hg * 4 + i
                        kb.mm(OD.t[0:65, sl * 128:(sl + 1) * 128], vdx.t[:, e_ * 8 + h, :], P.t[:, sl * 128:(sl + 1) * 128],
                              ki == 0 and sl == 0, ki == nk - 1, [vdx, P], OD, inc=(sl == 3))

                st = {"scores": scores, "exp": exp, "pv": pv}
                if ki == nk - 1:
                    st["fin"] = lambda: finalize_cd(OD, None, catD[hg * 4:(hg + 1) * 4, :, qtk], catDb)
                return st

            steps = []
            ng = 0
            for T in range(ctiles if "C" in do else 0):
                keys = [(0, None), (1, None)]
                if T < 16:
                    keys += [(4 + T - 1, 0 if T == 0 else 1), (4 + T, None), (4 + T + 1, 3 if T == 15 else 2)]
                for kv in range(2):
                    OC = ps[4 + ng % 2]
                    ng += 1
                    for ki, (e_, mk) in enumerate(keys):
                        steps.append(c_step(T, kv, ki, len(keys), e_, mk, OC, len(steps)))
            for T in range(18 if "D" in do else 0):
                keys = [(0, None), (1, None)]
                if T < 16:
                    cls = d_class(T)
                    keys += [(4 + T + o, (cls, o + 3)) for o in D_OFFS[cls]]
                for hg in range(2):
                    OD = ps[4 + ng % 2]
                    ng += 1
                    for ki, (e_, bi) in enumerate(keys):
                        steps.append(d_step(T, hg, ki, len(keys), e_, bi, OD, len(steps)))
            if steps:
                run_pipeline(steps)
            kb.barrier()

        if "O" in do:
          with ExitStack() as pes:
            wab = kb.sbuf("wab", [128, 8, D], BF16, pes)
            wcd = kb.sbuf("wcd", [64, 16, D], BF16, pes)
            gb2 = kb.sbuf("gb2", [128, 2, D], F32, pes)
            lbc = kb.sbuf("lbc", [128, 2, D], F32, pes)
            xg = kb.sbuf("xg", [128, 4, D], F32, pes)
            xgb = [Buf("xg%d" % i) for i in range(4)]
            cA = kb.sbuf("cA", [128, 4, 512], BF16, pes)
            cB = kb.sbuf("cB", [128, 4, 512], BF16, pes)
            cC = kb.sbuf("cC", [64, 8, 512], BF16, pes)
            cD = kb.sbuf("cD", [64, 8, 512], BF16, pes)
            tmp = [kb.sbuf("tmp%d" % i, [128, 512], F32, pes) for i in range(2)]
            stats = [kb.sbuf("stats%d" % i, [128, 4, 6], F32, pes) for i in range(2)]
            mv = [kb.sbuf("mv%d" % i, [128, 2], F32, pes) for i in range(2)]
            rs = [kb.sbuf("rs%d" % i, [128, 2], F32, pes) for i in range(2)]
            for c4 in range(2):
                kb.dma("pool", wab.t[:, c4 * 4:(c4 + 1) * 4, :], w_ab[c4 * 512:(c4 + 1) * 512, :].rearrange("(c p) n -> p c n", p=128),
                       wab, writes=[(wab, True)])
            for c4 in range(4):
                kb.dma("pool", wcd.t[:, c4 * 4:(c4 + 1) * 4, :], w_cd[c4 * 256:(c4 + 1) * 256, :].rearrange("(h p) n -> p h n", p=64),
                       wcd, writes=[(wcd, True)])
            for i in range(2):
                kb.dma("sp", gb2.t[:, i, :], gate[i:i + 1, :].partition_broadcast(128), gb2, writes=[(gb2, True)])
                kb.dma("sp", lbc.t[:, i, :], lnp[i:i + 1, :].partition_broadcast(128), lbc, writes=[(lbc, True)])
            ny = 0
            for (t0, ntl) in GROUPS:
                ntok = ntl * 128
                s = 1 if t0 >= 16 else 0
                tk = slice(t0 * 128, t0 * 128 + ntok)
                kb.dma("sp", xg.t[:, 0:ntl, :], xin[tk, :].rearrange("(t p) c -> p t c", p=128), xg, writes=xgb[:ntl])
                kb.dma("sp", cA.t[:, :, :ntok], catA[:, :, tk].rearrange("c p t -> p c t"), cA, reads=[catAb], writes=[cA])
                kb.dma("sp", cB.t[:, :, :ntok], fm[FM_YB:FM_YB + 4, :, tk].rearrange("c p t -> p c t"), cB, writes=[cB])
                kb.dma("sp", cC.t[:, :, :ntok], catC[:, :, tk].rearrange("h p t -> p h t"), cC, reads=[catCb], writes=[cC])
                kb.dma("sp", cD.t[:, :, :ntok], catD[:, :, tk].rearrange("h p t -> p h t"), cD, reads=[catDb], writes=[cD])
                for ti in range(ntl):
                    tt = slice(ti * 128, (ti + 1) * 128)
                    for n in range(4):
                        ncs = slice(n * 512, (n + 1) * 512)
                        Y = ps[ny % 4]
                        tm_ = tmp[ny % 2]
                        ny += 1
                        pairs = [(cA.t[:, c, tt], wab.t[:, c, ncs]) for c in range(4)]
                        pairs += [(cB.t[:, c, tt], wab.t[:, 4 + c, ncs]) for c in range(4)]
                        pairs += [(cC.t[0:64, h, tt], wcd.t[0:64, h, ncs]) for h in range(8)]
                        pairs += [(cD.t[0:64, h, tt], wcd.t[0:64, 8 + h, ncs]) for h in range(8)]
                        kb.mm_group(Y.t[:], pairs, reads=[cA, cB, cC, cD, wab, wcd], out_buf=Y)
                        kb.op("dve", lambda e, Y=Y, tm_=tm_, ncs=ncs: e.tensor_tensor(out=tm_.t[:], in0=Y.t[:], in1=gb2.t[:, s, ncs], op=ALU.mult),
                              reads=[Y, gb2], writes=[tm_])
                        kb.op("dve", lambda e, tm_=tm_, ti=ti, ncs=ncs: e.scalar_tensor_tensor(
                            out=xg.t[:, ti, ncs], in0=xg.t[:, ti, ncs], scalar=ALPHA, in1=tm_.t[:], op0=ALU.mult, op1=ALU.add),
                            reads=[tm_, xgb[ti]], writes=[(xgb[ti], True)])
                    ln_epilogue(kb, xg.t[:, ti, :], xgb[ti], stats[ti % 2], mv[ti % 2], rs[ti % 2], lbc.t[:, 0, :], lbc.t[:, 1, :], lbc)
                    kb.dma("sp", xout[(t0 + ti) * 128:(t0 + ti + 1) * 128, :], xg.t[:, ti, :], xg, reads=[xgb[ti]], final=True)
            kb.barrier()
        kb.finish()
    return nc


MCOLS = 2304


def build_mod():
    nc = bass.Bass("TRN2", target_bir_lowering=False)
    cc = nc.dram_tensor("cc", [128, 32], F32, kind="ExternalInput").ap()
    wm = nc.dram_tensor("wm", [4, D, MCOLS], F32, kind="ExternalInput").ap()
    bm = nc.dram_tensor("bm", [1, 4 * MCOLS], F32, kind="ExternalInput").ap()
    mo = nc.dram_tensor("mo", [2, 4 * MCOLS], F32, kind="ExternalOutput").ap()
    with ExitStack() as es:
        kb = KB(nc, es)
        c_sb = kb.sbuf("c_sb", [128, 16, 2], F32)
        bb = kb.sbuf("bb", [2, 4 * MCOLS], F32)
        ob = kb.sbuf("ob", [2, 4 * MCOLS], F32)
        wbuf = [kb.sbuf("wbuf%d" % i, [128, 16, 512], F32) for i in range(2)]
        pm = [kb.psum("pm%d" % i, [128, 512]) for i in range(2)]
        kb.dma("sp", c_sb.t[:].rearrange("p a b -> p (a b)"), cc, c_sb, writes=[c_sb])
        kb.dma("sp", bb.t[:], bm[0:1, :].partition_broadcast(2), bb, writes=[bb])
        kb.op("act", lambda e: e.activation(out=c_sb.t[:], in_=c_sb.t[:], func=AF.Silu), reads=[c_sb], writes=[c_sb])
        n = 0
        for l in range(4):
            for c0 in range(0, MCOLS, 512):
                ncol = min(512, MCOLS - c0)
                wb_, pb_ = wbuf[n % 2], pm[n % 2]
                n += 1
                for k4 in range(4):
                    kb.dma("sp", wb_.t[:, k4 * 4:(k4 + 1) * 4, :ncol],
                           wm[l, k4 * 512:(k4 + 1) * 512, c0:c0 + ncol].rearrange("(kc p) c -> p kc c", p=128),
                           wb_, writes=[(wb_, True)])
                kb.mm_group(pb_.t[0:2, :ncol], [(c_sb.t[:, kc, :], wb_.t[:, kc, :ncol]) for kc in range(16)],
                            reads=[c_sb, wb_], out_buf=pb_)
                o0 = l * MCOLS + c0
                kb.op("dve", lambda e, pb_=pb_, o0=o0, ncol=ncol: e.tensor_tensor(
                    out=ob.t[:, o0:o0 + ncol], in0=pb_.t[0:2, :ncol], in1=bb.t[:, o0:o0 + ncol], op=ALU.add),
                    reads=[pb_, bb], writes=[(ob, True)])
        kb.dma("sp", mo, ob.t[:], ob, reads=[ob], final=True)
        kb.finish()
    return nc


NCORES = 8
_PROGS = {}


def _prog(name):
    if name not in _PROGS:
        _PROGS[name] = {"M": build_mod, "F": build_ffn, "J": build_proj, "T": build_attn}[name]()
    return _PROGS[name]


def _run(name, in_maps):
    import time, os, sys
    t0 = time.time()
    res = run_bass_kernel_spmd(_prog(name), in_maps, core_ids=list(range(NCORES)))
    if os.environ.get("K_TIMING"):
        nb = sum(v.nbytes for m in in_maps for v in m.values())
        print("[launch %s] %.1fs  in=%.0fMB" % (name, time.time() - t0, nb / 1e6), file=sys.stderr, flush=True)
    return res.results


def d_bias_tables(rpb, core):
    out = np.full((5, 7, 2, 128, 512), -30000.0, np.float32)
    ii = np.arange(128)
    for cls, tloc in enumerate((0, 1, 2, 14, 15)):
        tg = core * 16 + tloc
        rq = (2 * tg + ii // 64)[None, :]
        cq = (ii % 64)[None, :]
        rs = np.clip(rq - 4, 0, 256 - 8)
        cs = np.clip(cq - 8, 0, 64 - 16)
        for o in D_OFFS[cls]:
            kg = tg + o
            if kg < 0 or kg > 127:
                continue
            rk = (2 * kg + ii // 64)[:, None]
            ck = (ii % 64)[:, None]
            valid = (rk >= rs) & (rk < rs + 8) & (ck >= cs) & (ck < cs + 16)
            dr = np.clip(rk - rq + 7, 0, 14)
            dc = np.clip(ck - cq, -15, 15) + 15
            for hg in range(2):
                for sl, i in enumerate(ORDER):
                    g = rpb[hg * 4 + i][dr, dc]
                    out[cls, o + 3, hg, :, sl * 128:(sl + 1) * 128] = np.where(valid, g, np.float32(-30000.0))
    return out


def c_masks(core):
    i = np.arange(128)[:, None]
    j = np.arange(128)[None, :]
    mprev = np.tile((i >= j).astype(np.float32), (1, 4))
    mnext = np.tile((i <= j).astype(np.float32), (1, 4))
    z = np.zeros_like(mprev)
    return np.stack([z if core == 0 else mprev, mprev, mnext, z if core == NCORES - 1 else mnext])


def _ext_fm(fms, i, c0, nch):
    own = fms[i][c0:c0 + nch]
    z = np.zeros((nch, 128, 256), own.dtype)
    prev = fms[i - 1][c0:c0 + nch][:, :, 1792:2048] if i > 0 else z
    nxt = fms[i + 1][c0:c0 + nch][:, :, 0:256] if i < NCORES - 1 else z
    return np.ascontiguousarray(np.concatenate([own[:, :, 2048:2304], prev, own[:, :, :2048], nxt], axis=2))


def _ext_tm(tms, i, c0, ncol):
    own = tms[i][:, c0:c0 + ncol]
    z = np.zeros((256, ncol), own.dtype)
    prev = tms[i - 1][1792:2048, c0:c0 + ncol] if i > 0 else z
    nxt = tms[i + 1][0:256, c0:c0 + ncol] if i < NCORES - 1 else z
    return np.ascontiguousarray(np.concatenate([own[2048:2304], prev, own[:2048], nxt], axis=0))


def kernel(x, c, ctx, c_ctx, w_mod, b_mod, ln_g, ln_b, ffn1_w_in, ffn1_w_out, ffn2_w_in, ffn2_w_out,
           mix_w_in, mix_w_out, a_lambda, b_norm_g, b_norm_b, b_spatial_w, b_spatial_b, c_sink, d_rpb,
           _nlayers=4, _dump=None):
    f32 = np.float32
    x = np.asarray(x, f32)[0]
    ctx = np.asarray(ctx, f32)[0]
    ident = np.eye(128, dtype=f32)
    cc = np.stack([np.asarray(c, f32)[0].reshape(16, 128).T, np.asarray(c_ctx, f32).reshape(16, 128).T], axis=2)
    cc = np.ascontiguousarray(cc.reshape(128, 32))
    w_mod = np.asarray(w_mod, f32)
    b_mod = np.asarray(b_mod, f32)
    ins = [{"cc": cc, "wm": np.ascontiguousarray(w_mod[:, :, i * MCOLS:(i + 1) * MCOLS]),
            "bm": np.ascontiguousarray(b_mod[:, i * MCOLS:(i + 1) * MCOLS]).reshape(1, 4 * MCOLS)} for i in range(NCORES)]
    mo = _run("M", ins)
    mods = np.concatenate([r["mo"].reshape(2, 4, MCOLS) for r in mo], axis=2)
    xl = [np.ascontiguousarray(x[i * 2048:(i + 1) * 2048]) for i in range(NCORES)]
    xc = np.ascontiguousarray(ctx)
    ropes = [rope_tables(i) for i in range(NCORES)]
    cms = [c_masks(i) for i in range(NCORES)]

    def full(i):
        return np.ascontiguousarray(np.concatenate([xl[i], xc], 0))

    def ffn(l, idx, lnk, w_in, w_out):
        nonlocal xl, xc
        modc = mod_cols(mods[0, l], mods[1, l], idx)
        lnp = np.ascontiguousarray(np.stack([ln_g[l, lnk], ln_b[l, lnk]]).astype(f32))
        w_in = np.ascontiguousarray(w_in, f32)
        w_out = np.ascontiguousarray(w_out, f32)
        r = _run("F", [{"xin": np.ascontiguousarray(np.concatenate([xl[i], xc[i * NCTX_F:(i + 1) * NCTX_F]], 0)),
                        "modc": modc, "lnp": lnp, "ident": ident, "w_in": w_in, "w_out": w_out} for i in range(NCORES)])
        xl = [q["xout"][:2048] for q in r]
        xc = np.ascontiguousarray(np.concatenate([q["xout"][2048:2048 + NCTX_F] for q in r], 0))

    if _dump is not None:
        _dump["mods"] = mods
    for l in range(_nlayers):
        ffn(l, (0, 1, 2), 0, ffn1_w_in[l], ffn1_w_out[l])
        xs = [full(i) for i in range(NCORES)]
        if _dump is not None:
            _dump["x1_%d" % l] = xs
        w_fm, w_tm = build_proj_weights(np.asarray(mix_w_in[l], f32))
        modc = mod_cols(mods[0, l], mods[1, l], (3, 4, None))
        gnp = np.ascontiguousarray(np.stack([b_norm_g[l], b_norm_b[l]]).astype(f32))
        wsT = np.ascontiguousarray(np.transpose(np.asarray(b_spatial_w[l], f32), (2, 0, 1)).reshape(128, 512))
        bsr = np.ascontiguousarray(np.asarray(b_spatial_b[l], f32).reshape(1, 512))
        r = _run("J", [{"xin": xs[i], "modc": modc, "ident": ident, "w_fm": w_fm, "w_tm": w_tm, "rope": ropes[i],
                        "gnp": gnp, "wsT": wsT, "bsr": bsr} for i in range(NCORES)])
        fms = [q["fm"] for q in r]
        tms = [q["tm"] for q in r]
        ka = np.ascontiguousarray(np.concatenate([fms[0][FM_KA:FM_KA + 4][:, :, 2048:2304]] +
                                                 [fms[i][FM_KA:FM_KA + 4][:, :, :2048] for i in range(NCORES)], axis=2))
        va = np.ascontiguousarray(np.concatenate([tms[0][2048:2304, 0:512]] + [tms[i][:2048, 0:512] for i in range(NCORES)], axis=0))
        lam_init = 0.8 - 0.6 * np.exp(-0.3 * l)
        alam = np.concatenate([np.asarray(a_lambda[l], f32).reshape(256), np.array([lam_init, 1.0 - lam_init], f32)]).reshape(1, 258)
        alam = np.ascontiguousarray(alam.astype(f32))
        sink = np.ascontiguousarray(np.asarray(c_sink[l], f32).reshape(1, 8))
        m9 = mods[:, l].reshape(2, 9, D)
        gate = np.ascontiguousarray(m9[:, 5, :])
        lnp = np.ascontiguousarray(np.stack([ln_g[l, 1], ln_b[l, 1]]).astype(f32))
        wo = np.asarray(mix_w_out[l], f32)
        w_ab, w_cd = np.ascontiguousarray(wo[:1024]), np.ascontiguousarray(wo[1024:])
        rpb = np.asarray(d_rpb[l], f32)
        ins = []
        for i in range(NCORES):
            ins.append({"xin": xs[i], "fm": fms[i], "ka": ka, "va": va,
                        "kc": _ext_fm(fms, i, FM_KC, 2), "vc": _ext_tm(tms, i, 512, 128),
                        "kd": _ext_fm(fms, i, FM_KD, 4), "vd": _ext_tm(tms, i, 640, 512),
                        "cmask": cms[i], "dbias": d_bias_tables(rpb, i), "alam": alam, "sink": sink, "ident": ident,
                        "gate": gate, "lnp": lnp, "w_ab": w_ab, "w_cd": w_cd})
        r = _run("T", ins)
        xs = [q["xout"] for q in r]
        if _dump is not None:
            _dump["x2_%d" % l] = xs
            _dump["fm_%d" % l] = fms
        xl = [q[:2048] for q in xs]
        xc = np.ascontiguousarray(xs[0][2048:2304])
        ffn(l, (6, 7, 8), 2, ffn2_w_in[l], ffn2_w_out[l])
    out = np.concatenate(xl, axis=0)
    return np.ascontiguousarray(out[None].astype(f32))
```

```python
import os
import numpy as np
import concourse.bass as bass
import concourse.mybir as mybir
from concourse.bass_utils import run_bass_kernel_spmd
from contextlib import ExitStack

F32 = mybir.dt.float32
BF16 = mybir.dt.bfloat16
AF = mybir.ActivationFunctionType
ALU = mybir.AluOpType


class Buf:
    def __init__(self, name, t=None):
        self.name = name
        self.t = t
        self.w = {}
        self.r = {}
        self.dsem = None
        self.dcnt = 0


class KB:
    def __init__(self, nc, es):
        self.nc, self.es = nc, es
        self.e = {"pe": nc.tensor, "act": nc.scalar, "dve": nc.vector, "pool": nc.gpsimd, "sp": nc.sync}
        self.sem = {k: es.enter_context(nc.semaphore("s_" + k)) for k in ("pe", "act", "dve", "pool")}
        self.cnt = {k: 0 for k in self.sem}
        self.seen = {q: {} for q in self.e}
        self.finals = {}
        self.dbufs = {}
        self.pend = []
        self.pe_pend = []
        self.attach = False
        self.nbuf = 0

    def sbuf(self, name, shape, dtype, es=None):
        t = (es or self.es).enter_context(self.nc.sbuf_tensor(name, shape, dtype))
        return Buf(name, t)

    def psum(self, name, shape, dtype=F32):
        t = self.es.enter_context(self.nc.psum_tensor(name, shape, dtype))
        return Buf(name, t)

    def _wait(self, q, key, sem, val):
        if key == "pe" and q == "pe":
            return
        if key in self.dbufs:
            val = self.dbufs[key].dcnt
        if self.seen[q].get(key, 0) >= val:
            return
        self.pend.append((q, sem, val))
        self.seen[q][key] = val

    def _flush(self, q, ins=None):
        pend, self.pend = self.pend, []
        if ins is not None and pend and self.attach:
            last = pend.pop()
        else:
            last = None
        for (qq, sem, val) in pend:
            self.e[qq].wait_ge(sem, val)
        return last

    def _pre(self, q, reads, writes):
        for b in reads:
            for k, (s, v) in b.w.items():
                self._wait(q, k, s, v)
        for w in writes:
            b, part = (w if isinstance(w, tuple) else (w, False))
            if part and not b.r:
                continue
            for k, (s, v) in b.r.items():
                self._wait(q, k, s, v)
            for k, (s, v) in b.w.items():
                self._wait(q, k, s, v)

    def _post(self, key, sem, val, reads, writes):
        for w in writes:
            b, part = (w if isinstance(w, tuple) else (w, False))
            if (not part) or b.r:
                b.w = {}
                b.r = {}
            b.w[key] = (sem, val)
        for b in reads:
            if any((w[0] if isinstance(w, tuple) else w) is b for w in writes):
                continue
            b.r[key] = (sem, val)

    def op(self, q, fn, reads=(), writes=()):
        self._pre(q, reads, writes)
        self._flush(q)
        ins = fn(self.e[q])
        self.cnt[q] += 1
        ins.then_inc(self.sem[q], 1)
        self._post(q, self.sem[q], self.cnt[q], reads, writes)
        return ins

    def mm_group(self, out_ap, pairs, reads, out_buf, part=False):
        wr = [(out_buf, part)]
        self._pre("pe", reads, wr)
        self._flush("pe")
        n = len(pairs)
        ins = None
        for i, (l, r) in enumerate(pairs):
            ins = self.nc.tensor.matmul(out_ap, lhsT=l, rhs=r, start=(i == 0), stop=(i == n - 1))
        self.cnt["pe"] += 1
        ins.then_inc(self.sem["pe"], 1)
        self._post("pe", self.sem["pe"], self.cnt["pe"], reads, wr)

    def mm(self, out_ap, lhsT, rhs, start, stop, reads, out_buf, inc=True):
        wr = [(out_buf, True)]
        self._pre("pe", reads, wr)
        self._flush("pe")
        ins = self.nc.tensor.matmul(out_ap, lhsT=lhsT, rhs=rhs, start=start, stop=stop)
        self.pe_pend.append((reads, wr))
        if inc:
            self.cnt["pe"] += 1
            ins.then_inc(self.sem["pe"], 1)
            for (r, w) in self.pe_pend:
                self._post("pe", self.sem["pe"], self.cnt["pe"], r, w)
            self.pe_pend = []
        return ins

    def barrier(self):
        assert not self.pe_pend
        for q in self.e:
            for k in self.sem:
                if self.cnt[k]:
                    self._wait(q, k, self.sem[k], self.cnt[k])
            for k, b in self.dbufs.items():
                self._wait(q, k, b.dsem, b.dcnt)
            self._flush(q)

    def dma(self, q, out_ap, in_ap, sb, reads=(), writes=(), final=False):
        self._pre(q, reads, writes)
        self._flush(q)
        if sb.dsem is None:
            self.nbuf += 1
            sb.dsem = self.es.enter_context(self.nc.semaphore("d%d_%s" % (self.nbuf, sb.name)))
        ins = self.e[q].dma_start(out=out_ap, in_=in_ap)
        sb.dcnt += 16
        ins.then_inc(sb.dsem, 16)
        key = "d_" + sb.name
        self.dbufs[key] = sb
        self._post(key, sb.dsem, sb.dcnt, reads, writes)
        if final:
            self.finals[key] = (sb.dsem, sb.dcnt)
        return ins

    def finish(self):
        for k, (s, v) in self.finals.items():
            self._wait("sp", k, s, v)
        self._flush("sp")


D = 2048
DFF = 5632
NT_TILES = 18
NTOK = NT_TILES * 128
ALPHA = 8 ** 0.25
LN_EPS = 1e-6
GROUPS = [(0, 4), (4, 4), (8, 4), (12, 4), (16, 2)]


def ln_epilogue(kb, xt, xb, stats, mv, rs, gbc, bbc, gb, nr=128):
    for q4 in range(4):
        kb.op("dve", lambda e, q4=q4: e.bn_stats(out=stats.t[0:nr, q4, :], in_=xt[:, q4 * 512:(q4 + 1) * 512]),
              reads=[xb], writes=[(stats, True)])
    kb.op("dve", lambda e: e.bn_aggr(out=mv.t[0:nr, :], in_=stats.t[0:nr].rearrange("p a b -> p (a b)")), reads=[stats], writes=[mv])
    kb.op("act", lambda e: e.activation(out=rs.t[0:nr, 0:1], in_=mv.t[0:nr, 1:2], func=AF.Sqrt, bias=LN_EPS, scale=1.0),
          reads=[mv], writes=[rs])
    kb.op("dve", lambda e: e.reciprocal(out=rs.t[0:nr, 0:1], in_=rs.t[0:nr, 0:1]), reads=[rs], writes=[rs])
    kb.op("dve", lambda e: e.tensor_scalar(out=rs.t[0:nr, 1:2], in0=mv.t[0:nr, 0:1], scalar1=rs.t[0:nr, 0:1], scalar2=-1.0,
                                            op0=ALU.mult, op1=ALU.mult), reads=[mv, rs], writes=[rs])
    kb.op("act", lambda e: e.activation(out=xt, in_=xt, func=AF.Identity, bias=rs.t[0:nr, 1:2], scale=rs.t[0:nr, 0:1]),
          reads=[xb, rs], writes=[xb])
    kb.op("pool", lambda e: e.tensor_tensor(out=xt, in0=xt, in1=gbc[0:nr], op=ALU.mult), reads=[xb, gb], writes=[xb])
    kb.op("dve", lambda e: e.tensor_tensor(out=xt, in0=xt, in1=bbc[0:nr], op=ALU.add), reads=[xb, gb], writes=[xb])


NCTX_F = 32
NTOK_F = 2048 + NCTX_F
GROUPS_F = [(0, 512, 0), (512, 512, 0), (1024, 512, 0), (1536, 512, 0), (2048, NCTX_F, 1)]


def build_ffn(phases=("tr", "in", "out", "ln"), groups=GROUPS_F, nsub=1):
    nc = bass.Bass("TRN2", target_bir_lowering=False)
    xin = nc.dram_tensor("xin", [NTOK_F, D], F32, kind="ExternalInput").ap()
    modc = nc.dram_tensor("modc", [128, 96], F32, kind="ExternalInput").ap()
    lnp = nc.dram_tensor("lnp", [2, D], F32, kind="ExternalInput").ap()
    ident = nc.dram_tensor("ident", [128, 128], F32, kind="ExternalInput").ap()
    w_in = nc.dram_tensor("w_in", [D, 2 * DFF], F32, kind="ExternalInput").ap()
    w_out = nc.dram_tensor("w_out", [DFF, D], F32, kind="ExternalInput").ap()
    xout = nc.dram_tensor("xout", [NTOK_F, D], F32, kind="ExternalOutput").ap()
    subs = [(modc, lnp, w_in, w_out)]
    for k in range(1, nsub):
        subs.append((nc.dram_tensor("modc%d" % k, [128, 96], F32, kind="ExternalInput").ap(),
                     nc.dram_tensor("lnp%d" % k, [2, D], F32, kind="ExternalInput").ap(),
                     nc.dram_tensor("w_in%d" % k, [D, 2 * DFF], F32, kind="ExternalInput").ap(),
                     nc.dram_tensor("w_out%d" % k, [DFF, D], F32, kind="ExternalInput").ap()))
    xmid = [nc.dram_tensor("xmid%d" % k, [NTOK_F, D], F32).ap() for k in range(nsub - 1)]
    xmidb = [Buf("xmid%d" % k) for k in range(nsub - 1)]
    x_first, x_last = xin, xout
    with ExitStack() as es:
        kb = KB(nc, es)
        mod = kb.sbuf("mod", [128, 2, 3, 16], F32)
        idb = kb.sbuf("ident_sb", [128, 128], F32)
        gbc = kb.sbuf("gbc", [128, 2, D], F32)
        xg = kb.sbuf("xg", [128, 4, D], F32)
        xgb = [Buf("xg%d" % i) for i in range(4)]
        hT = kb.sbuf("hT", [128, 16, 512], BF16)
        hid = kb.sbuf("hid", [128, 44, 512], BF16)
        wg = [kb.sbuf("wg%d" % i, [128, 16, 256], BF16) for i in range(2)]
        wu = [kb.sbuf("wu%d" % i, [128, 16, 256], BF16) for i in range(2)]
        wo = [kb.sbuf("wo%d" % i, [128, 44, 256], BF16) for i in range(2)]
        sg = [kb.sbuf("sg%d" % i, [128, 512], F32) for i in range(2)]
        yT = [kb.sbuf("yT%d" % i, [128, 512], F32) for i in range(2)]
        stats = [kb.sbuf("stats%d" % i, [128, 4, 6], F32) for i in range(2)]
        mv = [kb.sbuf("mv%d" % i, [128, 2], F32) for i in range(2)]
        rs = [kb.sbuf("rs%d" % i, [128, 2], F32) for i in range(2)]
        tpb = [kb.psum("tp%d" % i, [128, 512]) for i in range(2)]
        gps = [kb.psum("gp%d" % i, [128, 512]) for i in range(2)]
        ups = [kb.psum("up%d" % i, [128, 512]) for i in range(2)]
        yps = [kb.psum("yp%d" % i, [128, 512]) for i in range(2)]

        kb.dma("sp", idb.t[:], ident, idb, writes=[idb])

        def load_consts():
            kb.dma("sp", mod.t[:].rearrange("p a b c -> p (a b c)"), modc, mod, writes=[mod])
            kb.dma("sp", gbc.t[:, 0, :], lnp[0:1, :].partition_broadcast(128), gbc, writes=[(gbc, True)])
            kb.dma("sp", gbc.t[:, 1, :], lnp[1:2, :].partition_broadcast(128), gbc, writes=[(gbc, True)])
            kb.op("dve", lambda e: e.tensor_scalar(out=mod.t[:, :, 1, :], in0=mod.t[:, :, 1, :], scalar1=1.0, scalar2=None,
                                                    op0=ALU.add), reads=[mod], writes=[mod])
            kb.op("dve", lambda e: e.tensor_scalar(out=mod.t[:, :, 2, :], in0=mod.t[:, :, 2, :], scalar1=0.5, scalar2=None,
                                                    op0=ALU.mult), reads=[mod], writes=[mod])

        rot = {"tp": 0, "ev": 0}

        def next_tp():
            rot["tp"] ^= 1
            return tpb[rot["tp"]]

        def ev_eng():
            rot["ev"] ^= 1
            return "dve" if rot["ev"] else "act"

        xt = kb.sbuf("xt", [128, D], F32)

        def tiles_of(ntok):
            return [(i * 128, min(128, ntok - i * 128)) for i in range((ntok + 127) // 128)]

        def phase_a(r0, ntok, s):
            for ti, (ro, nr) in enumerate(tiles_of(ntok) if "tr" in phases else []):
                kb.dma("sp", xt.t[0:nr, :], xin[r0 + ro:r0 + ro + nr, :], xt, reads=xin_dep, writes=[xt])
                for kq in range(4):
                    tp = next_tp()
                    for k4 in range(4):
                        kc = kq * 4 + k4
                        kb.op("pe", lambda e, tp=tp, k4=k4, kc=kc, nr=nr: e.transpose(
                            out=tp.t[:, k4 * 128:k4 * 128 + nr], in_=xt.t[0:nr, kc * 128:(kc + 1) * 128], identity=idb.t[0:nr, 0:nr]),
                            reads=[xt, idb], writes=[(tp, True)])
                    eng = ev_eng()
                    for k4 in range(4):
                        kc = kq * 4 + k4
                        if eng == "dve":
                            kb.op("dve", lambda e, tp=tp, k4=k4, kc=kc, ti=ti, nr=nr: e.tensor_scalar(
                                out=hT.t[:, kc, ti * 128:ti * 128 + nr], in0=tp.t[:, k4 * 128:k4 * 128 + nr],
                                scalar1=mod.t[:, s, 1, kc:kc + 1], scalar2=mod.t[:, s, 0, kc:kc + 1],
                                op0=ALU.mult, op1=ALU.add), reads=[tp, mod], writes=[(hT, True)])
                        else:
                            kb.op("act", lambda e, tp=tp, k4=k4, kc=kc, ti=ti, nr=nr: e.activation(
                                out=hT.t[:, kc, ti * 128:ti * 128 + nr], in_=tp.t[:, k4 * 128:k4 * 128 + nr],
                                func=AF.Identity, bias=mod.t[:, s, 0, kc:kc + 1], scale=mod.t[:, s, 1, kc:kc + 1]),
                                reads=[tp, mod], writes=[(hT, True)])

        def phase_b(r0, ntok, s):
            for jb in range(22 if "in" in phases else 0):
                wgb, wub = wg[jb % 2], wu[jb % 2]
                kb.dma("pool", wgb.t[:], w_in[:, jb * 256:(jb + 1) * 256].rearrange("(kc p) c -> p kc c", p=128),
                       wgb, writes=[wgb])
                kb.dma("pool", wub.t[:], w_in[:, DFF + jb * 256:DFF + (jb + 1) * 256].rearrange("(kc p) c -> p kc c", p=128),
                       wub, writes=[wub])
                for jj in range(2):
                    j = jb * 2 + jj
                    gp, up, sgb = gps[j % 2], ups[j % 2], sg[j % 2]
                    kb.mm_group(gp.t[:, :ntok], [(wgb.t[:, kc, jj * 128:(jj + 1) * 128], hT.t[:, kc, :ntok]) for kc in range(16)],
                                reads=[wgb, hT], out_buf=gp)
                    kb.mm_group(up.t[:, :ntok], [(wub.t[:, kc, jj * 128:(jj + 1) * 128], hT.t[:, kc, :ntok]) for kc in range(16)],
                                reads=[wub, hT], out_buf=up)
                    kb.op("act", lambda e, gp=gp, sgb=sgb: e.activation(out=sgb.t[:, :ntok], in_=gp.t[:, :ntok], func=AF.Silu),
                          reads=[gp], writes=[sgb])
                    kb.op("dve", lambda e, up=up, sgb=sgb, j=j: e.tensor_tensor(out=hid.t[:, j, :ntok], in0=sgb.t[:, :ntok],
                                                                              in1=up.t[:, :ntok], op=ALU.mult),
                          reads=[sgb, up], writes=[(hid, True)])

        def phase_c(r0, ntok, s):
            tl = tiles_of(ntok)
            ntl = len(tl)
            full = ntok % 128 == 0
            if full:
                kb.dma("sp", xg.t[:, 0:ntl, :], xin[r0:r0 + ntok, :].rearrange("(t p) c -> p t c", p=128), xg, reads=xin_dep, writes=xgb[:ntl])
            else:
                assert ntl == 1
                kb.dma("sp", xg.t[0:ntok, 0, :], xin[r0:r0 + ntok, :], xg, reads=xin_dep, writes=xgb[:1])
            for ob in range(8 if "out" in phases else 0):
                wob = wo[ob % 2]
                kb.dma("pool", wob.t[:], w_out[:, ob * 256:(ob + 1) * 256].rearrange("(j p) c -> p j c", p=128),
                       wob, writes=[wob])
                for oo in range(2):
                    oc = ob * 2 + oo
                    yp, yTb = yps[oc % 2], yT[oc % 2]
                    kb.mm_group(yp.t[:, :ntok], [(wob.t[:, j, oo * 128:(oo + 1) * 128], hid.t[:, j, :ntok]) for j in range(44)],
                                reads=[wob, hid], out_buf=yp)
                    kb.op("act", lambda e, yp=yp, yTb=yTb, oc=oc: e.activation(
                        out=yTb.t[:, :ntok], in_=yp.t[:, :ntok], func=AF.Identity, scale=mod.t[:, s, 2, oc:oc + 1]),
                        reads=[yp, mod], writes=[yTb])
                    tp = next_tp()
                    for ti, (ro, nr) in enumerate(tl):
                        kb.op("pe", lambda e, tp=tp, ti=ti, yTb=yTb, ro=ro, nr=nr: e.transpose(
                            out=tp.t[0:nr, ti * 128:(ti + 1) * 128], in_=yTb.t[:, ro:ro + nr], identity=idb.t[:]),
                            reads=[yTb, idb], writes=[(tp, True)])
                    if full:
                        kb.op("dve", lambda e, tp=tp, oc=oc: e.scalar_tensor_tensor(
                            out=xg.t[:, 0:ntl, oc * 128:(oc + 1) * 128], in0=xg.t[:, 0:ntl, oc * 128:(oc + 1) * 128], scalar=ALPHA,
                            in1=tp.t[:, :ntok].rearrange("p (t c) -> p t c", c=128), op0=ALU.mult, op1=ALU.add),
                            reads=[tp] + xgb[:ntl], writes=[(b, True) for b in xgb[:ntl]])
                    else:
                        kb.op("dve", lambda e, tp=tp, oc=oc: e.scalar_tensor_tensor(
                            out=xg.t[0:ntok, 0, oc * 128:(oc + 1) * 128], in0=xg.t[0:ntok, 0, oc * 128:(oc + 1) * 128], scalar=ALPHA,
                            in1=tp.t[0:ntok, 0:128], op0=ALU.mult, op1=ALU.add),
                            reads=[tp] + xgb[:1], writes=[(xgb[0], True)])

        def phase_e(r0, ntok, s):
            for ti, (ro, nr) in enumerate(tiles_of(ntok)):
                if "ln" in phases:
                    ln_epilogue(kb, xg.t[0:nr, ti, :], xgb[ti], stats[ti % 2], mv[ti % 2], rs[ti % 2],
                                gbc.t[:, 0, :], gbc.t[:, 1, :], gbc, nr=nr)
                kb.dma("sp", xout[r0 + ro:r0 + ro + nr, :], xg.t[0:nr, ti, :], xg, reads=[xgb[ti]], writes=xout_dep, final=True)

        ng = len(groups)
        for k in range(nsub):
            modc, lnp, w_in, w_out = subs[k]
            xin = x_first if k == 0 else xmid[k - 1]
            xout = x_last if k == nsub - 1 else xmid[k]
            xin_dep = [] if k == 0 else [xmidb[k - 1]]
            xout_dep = [] if k == nsub - 1 else [(xmidb[k], True)]
            load_consts()
            phase_a(*groups[0])
            for gi in range(ng):
                phase_b(*groups[gi])
                phase_c(*groups[gi])
                if gi + 1 < ng:
                    phase_a(*groups[gi + 1])
                phase_e(*groups[gi])
        kb.finish()
    return nc


NFM = 26
FM_QA, FM_KA, FM_QC, FM_KC, FM_QD, FM_KD, FM_YB = 0, 4, 8, 12, 14, 18, 22
NTM = 1152
WFM_COLS = 40 * 128
WTM_COLS = 1664
GELU_C = 0.7978845608028654


def load_x_transpose_mod(kb, xin, t0, ntl, s, xg, xgb, hT, idb, mod, tpb, rot):
    kb.dma("sp", xg.t[:, 0:ntl, :], xin[t0 * 128:(t0 + ntl) * 128, :].rearrange("(t p) c -> p t c", p=128),
           xg, writes=xgb[:ntl])
    for ti in range(ntl):
        for kq in range(4):
            rot["tp"] ^= 1
            tp = tpb[rot["tp"]]
            for k4 in range(4):
                kc = kq * 4 + k4
                kb.op("pe", lambda e, tp=tp, k4=k4, kc=kc, ti=ti: e.transpose(
                    out=tp.t[:, k4 * 128:(k4 + 1) * 128], in_=xg.t[:, ti, kc * 128:(kc + 1) * 128], identity=idb.t[:]),
                    reads=[xgb[ti], idb], writes=[(tp, True)])
            rot["ev"] ^= 1
            for k4 in range(4):
                kc = kq * 4 + k4
                if rot["ev"]:
                    kb.op("dve", lambda e, tp=tp, k4=k4, kc=kc, ti=ti: e.tensor_scalar(
                        out=hT.t[:, kc, ti * 128:(ti + 1) * 128], in0=tp.t[:, k4 * 128:(k4 + 1) * 128],
                        scalar1=mod.t[:, s, 1, kc:kc + 1], scalar2=mod.t[:, s, 0, kc:kc + 1],
                        op0=ALU.mult, op1=ALU.add), reads=[tp, mod], writes=[(hT, True)])
                else:
                    kb.op("act", lambda e, tp=tp, k4=k4, kc=kc, ti=ti: e.activation(
                        out=hT.t[:, kc, ti * 128:(ti + 1) * 128], in_=tp.t[:, k4 * 128:(k4 + 1) * 128],
                        func=AF.Identity, bias=mod.t[:, s, 0, kc:kc + 1], scale=mod.t[:, s, 1, kc:kc + 1]),
                        reads=[tp, mod], writes=[(hT, True)])


def build_proj(groups=GROUPS):
    nc = bass.Bass("TRN2", target_bir_lowering=False)
    xin = nc.dram_tensor("xin", [NTOK, D], F32, kind="ExternalInput").ap()
    modc = nc.dram_tensor("modc", [128, 96], F32, kind="ExternalInput").ap()
    ident = nc.dram_tensor("ident", [128, 128], F32, kind="ExternalInput").ap()
    w_fm = nc.dram_tensor("w_fm", [D, WFM_COLS], F32, kind="ExternalInput").ap()
    w_tm = nc.dram_tensor("w_tm", [D, WTM_COLS], F32, kind="ExternalInput").ap()
    rope = nc.dram_tensor("rope", [2, 128, NTOK], F32, kind="ExternalInput").ap()
    gnp = nc.dram_tensor("gnp", [2, 512], F32, kind="ExternalInput").ap()
    wsT = nc.dram_tensor("wsT", [128, 512], F32, kind="ExternalInput").ap()
    bsr = nc.dram_tensor("bsr", [1, 512], F32, kind="ExternalInput").ap()
    fm = nc.dram_tensor("fm", [NFM, 128, NTOK], BF16, kind="ExternalOutput").ap()
    tm = nc.dram_tensor("tm", [NTOK, NTM], BF16, kind="ExternalOutput").ap()
    with ExitStack() as es:
        kb = KB(nc, es)
        mod = kb.sbuf("mod", [128, 2, 3, 16], F32)
        idb = kb.sbuf("ident_sb", [128, 128], F32)
        rp = kb.sbuf("rope_sb", [128, 2, NTOK], F32)
        gn = kb.sbuf("gn", [128, 2, 512], F32)
        ws = kb.sbuf("ws", [128, 512], BF16)
        bsb = kb.sbuf("bsb", [128, 512], F32)
        xg = kb.sbuf("xg", [128, 4, D], F32)
        xgb = [Buf("xg%d" % i) for i in range(4)]
        hT = kb.sbuf("hT", [128, 16, 512], BF16)
        uT = kb.sbuf("uT", [128, 4, 512], BF16)
        wb = [kb.sbuf("wb%d" % i, [128, 16, 256], BF16) for i in range(2)]
        wt = [kb.sbuf("wt%d" % i, [128, 16, 512], BF16) for i in range(2)]
        t1 = [kb.sbuf("t1_%d" % i, [128, 512], F32) for i in range(2)]
        t2 = [kb.sbuf("t2_%d" % i, [128, 512], F32) for i in range(2)]
        stg = [kb.sbuf("stg%d" % i, [128, 512], BF16) for i in range(3)]
        vv = [kb.sbuf("vv%d" % i, [128, 512], F32) for i in range(2)]
        vt = [kb.sbuf("vt%d" % i, [128, 512], BF16) for i in range(2)]
        ybs = [kb.sbuf("ybs%d" % i, [128, 512], F32) for i in range(2)]
        stats = [kb.sbuf("stats%d" % i, [128, 6], F32) for i in range(2)]
        mv = [kb.sbuf("mv%d" % i, [128, 2], F32) for i in range(2)]
        rs = [kb.sbuf("rs%d" % i, [128, 2], F32) for i in range(2)]
        tpb = [kb.psum("tp%d" % i, [128, 512]) for i in range(2)]
        pp = [kb.psum("pp%d" % i, [128, 512]) for i in range(4)]
        pt = [kb.psum("pt%d" % i, [128, 512]) for i in range(2)]

        kb.dma("sp", mod.t[:].rearrange("p a b c -> p (a b c)"), modc, mod, writes=[mod])
        kb.dma("sp", idb.t[:], ident, idb, writes=[idb])
        kb.dma("sp", rp.t[:, 0, :], rope[0], rp, writes=[(rp, True)])
        kb.dma("sp", rp.t[:, 1, :], rope[1], rp, writes=[(rp, True)])
        kb.dma("sp", gn.t[:, 0, :], gnp[0:1, :].partition_broadcast(128), gn, writes=[(gn, True)])
        kb.dma("sp", gn.t[:, 1, :], gnp[1:2, :].partition_broadcast(128), gn, writes=[(gn, True)])
        kb.dma("sp", bsb.t[:], bsr[0:1, :].partition_broadcast(128), bsb, writes=[bsb])
        kb.dma("pool", ws.t[:], wsT, ws, writes=[ws])
        kb.op("dve", lambda e: e.tensor_scalar(out=mod.t[:, :, 1, :], in0=mod.t[:, :, 1, :], scalar1=1.0, scalar2=None,
                                                op0=ALU.add), reads=[mod], writes=[mod])
        rot = {"tp": 0, "ev": 0, "stg": 0, "wt": 0}

        def next_stg():
            rot["stg"] = (rot["stg"] + 1) % 3
            return stg[rot["stg"]]

        for (t0, ntl) in groups:
            ntok = ntl * 128
            s = 1 if t0 >= 16 else 0
            tk = slice(t0 * 128, t0 * 128 + ntok)
            load_x_transpose_mod(kb, xin, t0, ntl, s, xg, xgb, hT, idb, mod, tpb, rot)
            for blk in range(20):
                wbb = wb[blk % 2]
                kb.dma("pool", wbb.t[:], w_fm[:, blk * 256:(blk + 1) * 256].rearrange("(kc p) c -> p kc c", p=128),
                       wbb, writes=[wbb])
                pa, pb = pp[(blk % 2) * 2], pp[(blk % 2) * 2 + 1]
                for jj, pq in ((0, pa), (1, pb)):
                    kb.mm_group(pq.t[:, :ntok], [(wbb.t[:, kc, jj * 128:(jj + 1) * 128], hT.t[:, kc, :ntok]) for kc in range(16)],
                                reads=[wbb, hT], out_buf=pq)
                if blk < 14:
                    oc = (FM_QA + blk) if blk < 4 else (FM_KA + blk - 4) if blk < 8 else (FM_QC + blk - 8) if blk < 12 \
                        else (FM_KC + blk - 12)
                    a1, a2, sg_ = t1[blk % 2], t2[blk % 2], next_stg()
                    kb.op("dve", lambda e, pa=pa, a1=a1: e.tensor_tensor(out=a1.t[:, :ntok], in0=pa.t[:, :ntok], in1=rp.t[:, 0, tk],
                                                                           op=ALU.mult), reads=[pa, rp], writes=[a1])
                    kb.op("dve", lambda e, pb=pb, a2=a2: e.tensor_tensor(out=a2.t[:, :ntok], in0=pb.t[:, :ntok], in1=rp.t[:, 1, tk],
                                                                           op=ALU.mult), reads=[pb, rp], writes=[a2])
                    kb.op("pool", lambda e, a1=a1, a2=a2, sg_=sg_: e.tensor_tensor(out=sg_.t[:, :ntok], in0=a1.t[:, :ntok],
                                                                                      in1=a2.t[:, :ntok], op=ALU.add),
                          reads=[a1, a2], writes=[sg_])
                    kb.dma("sp", fm[oc, :, tk], sg_.t[:, :ntok], sg_, reads=[sg_], final=True)
                elif blk < 18:
                    for jj, pq in ((0, pa), (1, pb)):
                        ci = (blk - 14) * 2 + jj
                        oc = (FM_QD + ci) if ci < 4 else (FM_KD + ci - 4)
                        sg_ = next_stg()
                        kb.op("act", lambda e, pq=pq, sg_=sg_: e.activation(out=sg_.t[:, :ntok], in_=pq.t[:, :ntok], func=AF.Identity),
                              reads=[pq], writes=[sg_])
                        kb.dma("sp", fm[oc, :, tk], sg_.t[:, :ntok], sg_, reads=[sg_], final=True)
                else:
                    for jj, pq in ((0, pa), (1, pb)):
                        ci = (blk - 18) * 2 + jj
                        kb.op("act", lambda e, pq=pq, ci=ci: e.activation(out=uT.t[:, ci, :ntok], in_=pq.t[:, :ntok],
                                                                            func=AF.Gelu_apprx_tanh), reads=[pq], writes=[(uT, True)])
            for bi, (c0, ncol) in enumerate(((0, 512), (512, 512), (1024, 512), (1536, 128))):
                rot["wt"] ^= 1
                wtb = wt[rot["wt"]]
                kb.dma("pool", wtb.t[:, :, :ncol], w_tm[:, c0:c0 + ncol].rearrange("(kc p) c -> p kc c", p=128),
                       wtb, writes=[wtb])
                for ti in range(ntl):
                    ptb = pt[ti % 2]
                    tks = slice((t0 + ti) * 128, (t0 + ti + 1) * 128)
                    kb.mm_group(ptb.t[:, :ncol], [(hT.t[:, kc, ti * 128:(ti + 1) * 128], wtb.t[:, kc, :ncol]) for kc in range(16)],
                                reads=[wtb, hT], out_buf=ptb)
                    if bi == 0:
                        v_, vb_, st_, mv_, rs_, yb_ = vv[ti % 2], vt[ti % 2], stats[ti % 2], mv[ti % 2], rs[ti % 2], ybs[ti % 2]
                        kb.op("act", lambda e, ptb=ptb, v_=v_: e.activation(out=v_.t[:], in_=ptb.t[:], func=AF.Gelu_apprx_tanh),
                              reads=[ptb], writes=[v_])
                        kb.op("dve", lambda e, v_=v_, st_=st_: e.bn_stats(out=st_.t[:], in_=v_.t[:]), reads=[v_], writes=[st_])
                        kb.op("dve", lambda e, st_=st_, mv_=mv_: e.bn_aggr(out=mv_.t[:], in_=st_.t[:]), reads=[st_], writes=[mv_])
                        kb.op("act", lambda e, mv_=mv_, rs_=rs_: e.activation(out=rs_.t[:, 0:1], in_=mv_.t[:, 1:2], func=AF.Sqrt,
                                                                                bias=LN_EPS, scale=1.0), reads=[mv_], writes=[rs_])
                        kb.op("dve", lambda e, rs_=rs_: e.reciprocal(out=rs_.t[:, 0:1], in_=rs_.t[:, 0:1]), reads=[rs_], writes=[rs_])
                        kb.op("dve", lambda e, rs_=rs_, mv_=mv_: e.tensor_scalar(out=rs_.t[:, 1:2], in0=mv_.t[:, 0:1],
                                                                                   scalar1=rs_.t[:, 0:1], scalar2=-1.0,
                                                                                   op0=ALU.mult, op1=ALU.mult),
                              reads=[mv_, rs_], writes=[rs_])
                        kb.op("act", lambda e, v_=v_, rs_=rs_: e.activation(out=v_.t[:], in_=v_.t[:], func=AF.Identity,
                                                                              bias=rs_.t[:, 1:2], scale=rs_.t[:, 0:1]),
                              reads=[v_, rs_], writes=[v_])
                        kb.op("pool", lambda e, v_=v_: e.tensor_tensor(out=v_.t[:], in0=v_.t[:], in1=gn.t[:, 0, :], op=ALU.mult),
                              reads=[v_, gn], writes=[v_])
                        kb.op("dve", lambda e, v_=v_, vb_=vb_: e.tensor_tensor(out=vb_.t[:], in0=v_.t[:], in1=gn.t[:, 1, :], op=ALU.add),
                              reads=[v_, gn], writes=[vb_])
                        rot["tp"] ^= 1
                        mp = tpb[rot["tp"]]
                        for g in range(4):
                            kb.mm_group(mp.t[:, g * 128:(g + 1) * 128], [(vb_.t[:, g * 128:(g + 1) * 128], ws.t[:, g * 128:(g + 1) * 128])],
                                        reads=[vb_, ws], out_buf=mp, part=True)
                        kb.op("dve", lambda e, mp=mp, yb_=yb_: e.tensor_tensor(out=yb_.t[:], in0=mp.t[:], in1=bsb.t[:], op=ALU.add),
                              reads=[mp, bsb], writes=[yb_])
                        sg_ = next_stg()
                        kb.op("pool", lambda e, yb_=yb_, sg_=sg_, ti=ti: e.tensor_tensor(
                            out=sg_.t[:].rearrange("p (g t) -> p g t", g=4), in0=yb_.t[:].rearrange("p (g t) -> p g t", g=4),
                            in1=uT.t[:, :, ti * 128:(ti + 1) * 128], op=ALU.mult), reads=[yb_, uT], writes=[sg_])
                        kb.dma("sp", fm[FM_YB:FM_YB + 4, :, tks].rearrange("g p t -> p g t"),
                               sg_.t[:].rearrange("p (g t) -> p g t", g=4), sg_, reads=[sg_], final=True)
                    else:
                        oc0 = {1: 0, 2: 640, 3: 512}[bi]
                        sg_ = next_stg()
                        if ti % 2:
                            kb.op("act", lambda e, ptb=ptb, sg_=sg_: e.activation(out=sg_.t[:, :ncol], in_=ptb.t[:, :ncol], func=AF.Identity),
                                  reads=[ptb], writes=[sg_])
                        else:
                            kb.op("dve", lambda e, ptb=ptb, sg_=sg_: e.tensor_copy(out=sg_.t[:, :ncol], in_=ptb.t[:, :ncol]),
                                  reads=[ptb], writes=[sg_])
                        kb.dma("sp", tm[tks, oc0:oc0 + ncol], sg_.t[:, :ncol], sg_, reads=[sg_], final=True)
        kb.finish()
    return nc


PROJ_CUTS = np.cumsum([512, 512, 512, 512, 512, 512, 128, 128, 512, 512, 512])[:-1].tolist()


def swap_cols(w):
    k, n = w.shape
    return np.ascontiguousarray(w.reshape(k, n // 64, 2, 32)[:, :, ::-1, :]).reshape(k, n)


def build_proj_weights(w):
    aq, ak, av, bu, bv, cq, ck, cv, dq, dk, dv = np.split(w, PROJ_CUTS, axis=1)
    chunks = []

    def pairs(m):
        ms = swap_cols(m)
        for i in range(m.shape[1] // 128):
            chunks.append(m[:, i * 128:(i + 1) * 128])
            chunks.append(ms[:, i * 128:(i + 1) * 128])

    pairs(aq)
    pairs(ak)
    pairs(cq)
    pairs(np.concatenate([ck[:, :64], ck[:, :64], ck[:, 64:], ck[:, 64:]], 1))
    for m in (dq, dk, bu):
        for i in range(4):
            chunks.append(m[:, i * 128:(i + 1) * 128])
    w_fm = np.ascontiguousarray(np.concatenate(chunks, 1))
    w_tm = np.ascontiguousarray(np.concatenate([bv, av, dv, cv], 1))
    return w_fm, w_tm


def rope_tables(core):
    t = np.arange(core * 2048, (core + 1) * 2048)
    row = (t // 64).astype(np.float32)
    col = (t % 64).astype(np.float32)
    inv = (np.float32(10000.0) ** (-np.arange(16, dtype=np.float32) / np.float32(16))).astype(np.float32)
    ang = np.concatenate([row[:, None] * inv, col[:, None] * inv], -1).astype(np.float32)
    cos, sin = np.cos(ang).astype(np.float32), np.sin(ang).astype(np.float32)
    tab = np.zeros((2, 128, NTOK), np.float32)
    tab[0, :, 2048:] = 1.0
    for p in range(128):
        d = p % 64
        j = d % 32
        tab[0, p, :2048] = cos[:, j]
        tab[1, p, :2048] = -sin[:, j] if d < 32 else sin[:, j]
    return tab


def mod_cols(m_lat, m_ctx, idx):
    out = np.zeros((128, 2, 3, 16), np.float32)
    for s, m in enumerate((m_lat, m_ctx)):
        mm = m.reshape(9, 16, 128)
        for wi, i in enumerate(idx):
            if i is not None:
                out[:, s, wi, :] = mm[i].T
    return out.reshape(128, 96)


ORDER = (0, 2, 1, 3)
NKA = 256 + 16384
NEXT = 22 * 128
D_OFFS = {0: (-2, -1, 0, 1, 2, 3), 1: (-2, -1, 0, 1, 2), 2: (-2, -1, 0, 1, 2), 3: (-2, -1, 0, 1, 2), 4: (-3, -2, -1, 0, 1, 2)}


def d_class(T):
    return 0 if T == 0 else 1 if T == 1 else 3 if T == 14 else 4 if T == 15 else 2


def build_attn(do=("A", "C", "D", "O"), qgroups=GROUPS, dbg=False, stage=9, ctiles=18):
    nc = bass.Bass("TRN2", target_bir_lowering=False)
    xin = nc.dram_tensor("xin", [NTOK, D], F32, kind="ExternalInput").ap()
    fm = nc.dram_tensor("fm", [NFM, 128, NTOK], BF16, kind="ExternalInput").ap()
    ka = nc.dram_tensor("ka", [4, 128, NKA], BF16, kind="ExternalInput").ap()
    va = nc.dram_tensor("va", [NKA, 512], BF16, kind="ExternalInput").ap()
    kc = nc.dram_tensor("kc", [2, 128, NEXT], BF16, kind="ExternalInput").ap()
    vc = nc.dram_tensor("vc", [NEXT, 128], BF16, kind="ExternalInput").ap()
    kd = nc.dram_tensor("kd", [4, 128, NEXT], BF16, kind="ExternalInput").ap()
    vd = nc.dram_tensor("vd", [NEXT, 512], BF16, kind="ExternalInput").ap()
    cmask = nc.dram_tensor("cmask", [4, 128, 512], F32, kind="ExternalInput").ap()
    dbias = nc.dram_tensor("dbias", [5, 7, 2, 128, 512], F32, kind="ExternalInput").ap()
    alam = nc.dram_tensor("alam", [1, 258], F32, kind="ExternalInput").ap()
    sink = nc.dram_tensor("sink", [1, 8], F32, kind="ExternalInput").ap()
    ident = nc.dram_tensor("ident", [128, 128], F32, kind="ExternalInput").ap()
    gate = nc.dram_tensor("gate", [2, D], F32, kind="ExternalInput").ap()
    lnp = nc.dram_tensor("lnp", [2, D], F32, kind="ExternalInput").ap()
    w_ab = nc.dram_tensor("w_ab", [1024, D], F32, kind="ExternalInput").ap()
    w_cd = nc.dram_tensor("w_cd", [1024, D], F32, kind="ExternalInput").ap()
    xout = nc.dram_tensor("xout", [NTOK, D], F32, kind="ExternalOutput").ap()
    dk_ = {"kind": "ExternalOutput"} if dbg else {}
    catA = nc.dram_tensor("catA", [4, 128, NTOK], BF16, **dk_).ap()
    catC = nc.dram_tensor("catC", [8, 64, NTOK], BF16, **dk_).ap()
    catD = nc.dram_tensor("catD", [8, 64, NTOK], BF16, **dk_).ap()
    with ExitStack() as es:
        kb = KB(nc, es)
        catAb, catCb, catDb = Buf("catA"), Buf("catC"), Buf("catD")
        ps = [kb.psum("ps%d" % i, [128, 512]) for i in range(8)]
        ones = kb.sbuf("ones", [128, 128], BF16)
        onesf = kb.sbuf("onesf", [128, 128], F32)
        lam = kb.sbuf("lam", [128, 264], F32)
        esk = kb.sbuf("esk", [128, 8], F32)
        kb.op("pool", lambda e: e.memset(ones.t[:], 1.0), writes=[ones])
        kb.op("pool", lambda e: e.memset(onesf.t[:], 1.0), writes=[onesf])
        kb.dma("sp", lam.t[:, 0:258], alam[0:1, :].partition_broadcast(128), lam, writes=[lam])
        kb.dma("sp", esk.t[:], sink[0:1, :].partition_broadcast(128), esk, writes=[esk])
        lsc = kb.sbuf("lsc", [128, 128], F32)
        kb.op("dve", lambda e: e.tensor_tensor(out=lsc.t[:, 0:64], in0=lam.t[:, 0:64], in1=lam.t[:, 64:128], op=ALU.mult),
              reads=[lam], writes=[lsc])
        kb.op("dve", lambda e: e.tensor_tensor(out=lsc.t[:, 64:128], in0=lam.t[:, 128:192], in1=lam.t[:, 192:256], op=ALU.mult),
              reads=[lam, lsc], writes=[lsc])
        kb.op("dve", lambda e: e.reduce_sum(out=lam.t[:, 258:260], in_=lsc.t[:].rearrange("p (a b) -> p a b", a=2),
                                            axis=mybir.AxisListType.X), reads=[lsc, lam], writes=[lam])
        kb.op("act", lambda e: e.activation(out=lam.t[:, 258:260], in_=lam.t[:, 258:260], func=AF.Exp), reads=[lam], writes=[lam])
        kb.op("act", lambda e: e.activation(out=esk.t[:], in_=esk.t[:], func=AF.Exp), reads=[esk], writes=[esk])
        kb.op("dve", lambda e: e.tensor_tensor(out=lam.t[:, 260:261], in0=lam.t[:, 259:260], in1=lam.t[:, 258:259], op=ALU.subtract),
              reads=[lam], writes=[lam])
        kb.op("dve", lambda e: e.tensor_tensor(out=lam.t[:, 260:261], in0=lam.t[:, 260:261], in1=lam.t[:, 256:257], op=ALU.subtract),
              reads=[lam], writes=[lam])
        NLAM, OML = lam.t[:, 260:261], lam.t[:, 257:258]

        if "A" in do:
          with ExitStack() as pes:
            qa = kb.sbuf("qa", [128, 512], BF16, pes)
            kblk = [kb.sbuf("kblk%d" % i, [128, 2048], BF16, pes) for i in range(2)]
            vblk = [kb.sbuf("vblk%d" % i, [128, 16, 128], BF16, pes) for i in range(2)]
            pt_ = [[kb.sbuf("p%d_%d" % (c, i), [128, 512], BF16, pes) for i in range(2)] for c in range(2)]
            r0 = kb.sbuf("r0", [128, 512], F32, pes)
            r1 = kb.sbuf("r1", [128, 512], F32, pes)
            a0 = kb.sbuf("a0", [128, 512], F32, pes)
            a1 = kb.sbuf("a1", [128, 512], F32, pes)
            sq = kb.sbuf("sq", [128, 512], F32, pes)
            yst = [kb.sbuf("yst%d" % i, [128, 512], BF16, pes) for i in range(2)]
            O0, O1, Z0, Z1 = ps[4], ps[5], ps[6], ps[7]
            nblk = 0
            nst = 0
            for h in range(4):
                for (t0, ntl) in qgroups:
                    nq = ntl * 128
                    tk = slice(t0 * 128, t0 * 128 + nq)
                    isctx = t0 >= 16
                    kb.dma("sp", qa.t[:, :nq], fm[FM_QA + h, :, tk], qa, writes=[qa])
                    blocks = [(0, 256)] + ([] if isctx else [(256 + i * 2048, 2048) for i in range(8)])
                    units = []
                    for (k0, ksz) in blocks:
                        for kt in range(ksz // 128):
                            units.append((k0, ksz, kt))
                    nu = len(units)
                    cur = {}

                    def emit_scores(u):
                        nonlocal nblk
                        k0, ksz, kt = units[u]
                        if kt == 0:
                            kbb, vbb = kblk[nblk % 2], vblk[nblk % 2]
                            nblk += 1
                            kb.dma("sp", kbb.t[:, :ksz], ka[h, :, k0:k0 + ksz], kbb, writes=[kbb])
                            kb.dma("sp", vbb.t[:, :ksz // 128, :],
                                   va[k0:k0 + ksz, h * 128:(h + 1) * 128].rearrange("(t p) c -> p t c", p=128), vbb, writes=[vbb])
                            cur["kv"] = (kbb, vbb)
                        kbb, vbb = cur["kv"]
                        par = u % 2
                        S0, S1 = ps[par * 2], ps[par * 2 + 1]
                        kb.mm(S0.t[:, :nq], kbb.t[0:64, kt * 128:(kt + 1) * 128], qa.t[0:64, :nq], True, True, [kbb, qa], S0)
                        kb.mm(S1.t[:, :nq], kbb.t[64:128, kt * 128:(kt + 1) * 128], qa.t[64:128, :nq], True, True, [kbb, qa], S1)
                        cur[u] = (vbb, kt)

                    def emit_exp(u):
                        par = u % 2
                        S0, S1 = ps[par * 2], ps[par * 2 + 1]
                        P0, P1 = pt_[0][par], pt_[1][par]
                        kb.op("act", lambda e: e.activation(out=P0.t[:, :nq], in_=S0.t[:, :nq], func=AF.Exp, scale=0.125),
                              reads=[S0], writes=[P0])
                        kb.op("act", lambda e: e.activation(out=P1.t[:, :nq], in_=S1.t[:, :nq], func=AF.Exp, scale=0.125),
                              reads=[S1], writes=[P1])

                    def emit_pv(u):
                        par = u % 2
                        P0, P1 = pt_[0][par], pt_[1][par]
                        vbb, kt = cur.pop(u)
                        first, last = u == 0, u == nu - 1
                        kb.mm(O0.t[:, :nq], vbb.t[:, kt, :], P0.t[:, :nq], first, last, [vbb, P0], O0, inc=False)
                        kb.mm(Z0.t[:, :nq], ones.t[:], P0.t[:, :nq], first, last, [ones, P0], Z0, inc=False)
                        kb.mm(O1.t[:, :nq], vbb.t[:, kt, :], P1.t[:, :nq], first, last, [vbb, P1], O1, inc=False)
                        kb.mm(Z1.t[:, :nq], ones.t[:], P1.t[:, :nq], first, last, [ones, P1], Z1, inc=True)

                    emit_scores(0)
                    emit_exp(0)
                    for u in range(1, nu):
                        emit_scores(u)
                        emit_pv(u - 1)
                        emit_exp(u)
                    emit_pv(nu - 1)
                    kb.op("dve", lambda e: e.reciprocal(out=r0.t[:, :nq], in_=Z0.t[:, :nq]), reads=[Z0], writes=[r0])
                    kb.op("dve", lambda e: e.reciprocal(out=r1.t[:, :nq], in_=Z1.t[:, :nq]), reads=[Z1], writes=[r1])
                    kb.op("dve", lambda e: e.tensor_tensor(out=a0.t[:, :nq], in0=O0.t[:, :nq], in1=r0.t[:, :nq], op=ALU.mult),
                          reads=[O0, r0], writes=[a0])
                    kb.op("dve", lambda e: e.tensor_tensor(out=a1.t[:, :nq], in0=O1.t[:, :nq], in1=r1.t[:, :nq], op=ALU.mult),
                          reads=[O1, r1], writes=[a1])
                    kb.op("dve", lambda e: e.scalar_tensor_tensor(out=a0.t[:, :nq], in0=a1.t[:, :nq], scalar=NLAM, in1=a0.t[:, :nq],
                                                                  op0=ALU.mult, op1=ALU.add), reads=[a0, a1, lam], writes=[a0])
                    kb.op("pool", lambda e: e.tensor_tensor(out=sq.t[:, :nq], in0=a0.t[:, :nq], in1=a0.t[:, :nq], op=ALU.mult),
                          reads=[a0], writes=[sq])
                    MS = ps[0]
                    kb.mm(MS.t[:, :nq], onesf.t[:], sq.t[:, :nq], True, True, [onesf, sq], MS)
                    kb.op("act", lambda e, MS=MS: e.activation(out=r0.t[:, :nq], in_=MS.t[:, :nq], func=AF.Sqrt, bias=LN_EPS, scale=1.0 / 128),
                          reads=[MS], writes=[r0])
                    kb.op("dve", lambda e: e.reciprocal(out=r0.t[:, :nq], in_=r0.t[:, :nq]), reads=[r0], writes=[r0])
                    ys = yst[nst % 2]
                    nst += 1
                    kb.op("dve", lambda e, ys=ys: e.scalar_tensor_tensor(out=ys.t[:, :nq], in0=a0.t[:, :nq], scalar=OML, in1=r0.t[:, :nq],
                                                                         op0=ALU.mult, op1=ALU.mult), reads=[a0, r0, lam], writes=[ys])
                    kb.dma("sp", catA[h, :, tk], ys.t[:, :nq], ys, reads=[ys], writes=[(catAb, True)])
            kb.barrier()

        if "C" in do or "D" in do:
          with ExitStack() as pes:
            qc = kb.sbuf("qc", [128, 4, NTOK], BF16, pes)
            kcx = kb.sbuf("kcx", [128, 2, NEXT], BF16, pes)
            vcx = kb.sbuf("vcx", [128, 44, 65], BF16, pes)
            qd = kb.sbuf("qd", [128, 4, NTOK], BF16, pes)
            kdx = kb.sbuf("kdx", [128, 4, NEXT], BF16, pes)
            vdx = kb.sbuf("vdx", [128, 176, 65], BF16, pes)
            vstage = kb.sbuf("vstage", [128, 22 * 512], BF16, pes)
            cmk = kb.sbuf("cmk", [128, 4, 512], BF16, pes)
            db = [kb.sbuf("db%d" % i, [128, 512], F32, pes) for i in range(2)]
            tS = [kb.sbuf("tS%d" % i, [128, 512], F32, pes) for i in range(2)]
            pcd = [kb.sbuf("pcd%d" % i, [128, 512], BF16, pes) for i in range(2)]
            zr = kb.sbuf("zr", [128, 512], F32, pes)
            bcs = kb.sbuf("bcs", [128, 512], F32, pes)
            stgc = [kb.sbuf("stgc%d" % i, [128, 512], BF16, pes) for i in range(2)]
            kb.dma("sp", qc.t[:], fm[FM_QC:FM_QC + 4].rearrange("c p t -> p c t"), qc, writes=[qc])
            kb.dma("sp", qd.t[:], fm[FM_QD:FM_QD + 4].rearrange("c p t -> p c t"), qd, writes=[qd])
            kb.dma("sp", kcx.t[:], kc.rearrange("c p t -> p c t"), kcx, writes=[kcx])
            kb.dma("sp", kdx.t[:], kd.rearrange("c p t -> p c t"), kdx, writes=[kdx])
            kb.dma("pool", cmk.t[:], cmask.rearrange("m p c -> p m c"), cmk, writes=[cmk])
            kb.op("pool", lambda e: e.memset(vcx.t[:], 1.0), writes=[vcx])
            kb.op("pool", lambda e: e.memset(vdx.t[:], 1.0), writes=[vdx])
            kb.dma("sp", vstage.t[:, 0:22 * 128].rearrange("p (t c) -> p t c", c=128), vc.rearrange("(t p) c -> p t c", p=128),
                   vstage, writes=[vstage])
            kb.op("dve", lambda e: e.tensor_copy(out=vcx.t[:, :, 0:64], in_=vstage.t[:, 0:22 * 128].rearrange("p (a d) -> p a d", d=64)),
                  reads=[vstage], writes=[vcx])
            kb.dma("sp", vstage.t[:].rearrange("p (t c) -> p t c", c=512), vd.rearrange("(t p) c -> p t c", p=128),
                   vstage, writes=[vstage])
            kb.op("dve", lambda e: e.tensor_copy(out=vdx.t[:, :, 0:64], in_=vstage.t[:].rearrange("p (a d) -> p a d", d=64)),
                  reads=[vstage], writes=[vdx])
            cnt = {"s": 0, "o": 0, "p": 0, "b": 0, "g": 0, "db": 0}

            def finalize_cd(O, sink_cols, dst, dst_buf):
                if sink_cols is not None:
                    for sl, g in enumerate(ORDER):
                        kb.op("dve", lambda e, g=g, sl=sl: e.tensor_scalar(
                            out=zr.t[64:65, sl * 128:(sl + 1) * 128], in0=O.t[64:65, sl * 128:(sl + 1) * 128],
                            scalar1=esk.t[64:65, sink_cols + g:sink_cols + g + 1], scalar2=None, op0=ALU.add),
                            reads=[O, esk], writes=[(zr, True)])
                else:
                    kb.op("dve", lambda e: e.tensor_copy(out=zr.t[64:65, :], in_=O.t[64:65, :]), reads=[O], writes=[zr])
                kb.op("dve", lambda e: e.reciprocal(out=zr.t[64:65, :], in_=zr.t[64:65, :]), reads=[zr], writes=[zr])
                BC = ps[6 + cnt["b"] % 2]
                cnt["b"] += 1
                kb.mm(BC.t[0:64, :], onesf.t[64:65, 0:64], zr.t[64:65, :], True, True, [onesf, zr], BC)
                kb.op("act", lambda e: e.activation(out=bcs.t[0:64, :], in_=BC.t[0:64, :], func=AF.Identity), reads=[BC], writes=[bcs])
                sg_ = stgc[cnt["g"] % 2]
                cnt["g"] += 1
                kb.op("dve", lambda e: e.tensor_tensor(out=sg_.t[0:64, :].rearrange("p (a b t) -> p b a t", a=2, b=2),
                                                        in0=O.t[0:64, :].rearrange("p (b a t) -> p b a t", a=2, b=2),
                                                        in1=bcs.t[0:64, :].rearrange("p (b a t) -> p b a t", a=2, b=2), op=ALU.mult),
                      reads=[O, bcs], writes=[sg_])
                kb.dma("sp", dst.rearrange("g p t -> p g t"), sg_.t[0:64, :].rearrange("p (g t) -> p g t", g=4), sg_,
                       reads=[sg_], writes=[(dst_buf, True)])

            def run_pipeline(steps):
                n = len(steps)
                pend_fin = []
                steps[0]["scores"]()
                steps[0]["exp"]()
                for u in range(1, n):
                    steps[u]["scores"]()
                    steps[u - 1]["pv"]()
                    for f in pend_fin:
                        f()
                    pend_fin = [steps[u - 1]["fin"]] if "fin" in steps[u - 1] else []
                    steps[u]["exp"]()
                steps[n - 1]["pv"]()
                for f in pend_fin:
                    f()
                if "fin" in steps[n - 1]:
                    steps[n - 1]["fin"]()

            def c_step(T, kv, ki, nk, e_, mk, OC, u):
                qtk = slice(T * 128, (T + 1) * 128)
                Sa, Sb = ps[(u % 2) * 2], ps[(u % 2) * 2 + 1]
                P = pcd[u % 2]

                def scores():
                    for sl, g in enumerate(ORDER):
                        j = kv * 4 + g
                        ch, hf = j // 2, j % 2
                        Sx = Sa if hf == 0 else Sb
                        kb.mm(Sx.t[:, (sl % 2) * 128:(sl % 2 + 1) * 128], kcx.t[hf * 64:(hf + 1) * 64, kv, e_ * 128:(e_ + 1) * 128],
                              qc.t[hf * 64:(hf + 1) * 64, ch, qtk], True, True, [kcx, qc], Sx, inc=(sl % 2 == 1))

                def exp():
                    kb.op("act", lambda e: e.activation(out=P.t[:, 0:256], in_=Sa.t[:, 0:256], func=AF.Exp, scale=0.125),
                          reads=[Sa], writes=[(P, True)])
                    kb.op("act", lambda e: e.activation(out=P.t[:, 256:512], in_=Sb.t[:, 0:256], func=AF.Exp, scale=0.125),
                          reads=[Sb], writes=[(P, True)])
                    if mk is not None:
                        kb.op("dve", lambda e: e.tensor_tensor(out=P.t[:], in0=P.t[:], in1=cmk.t[:, mk, :], op=ALU.mult),
                              reads=[P, cmk], writes=[P])

                def pv():
                    kb.mm(OC.t[0:65, :], vcx.t[:, e_ * 2 + kv, :], P.t[:], ki == 0, ki == nk - 1, [vcx, P], OC)

                st = {"scores": scores, "exp": exp, "pv": pv}
                if ki == nk - 1:
                    st["fin"] = lambda: finalize_cd(OC, kv * 4, catC[kv * 4:(kv + 1) * 4, :, qtk], catCb)
                return st

            def d_step(T, hg, ki, nk, e_, bi, OD, u):
                qtk = slice(T * 128, (T + 1) * 128)
                Sa, Sb = ps[(u % 2) * 2], ps[(u % 2) * 2 + 1]
                P = pcd[u % 2]

                def scores():
                    for sl, i in enumerate(ORDER):
                        h = hg * 4 + i
                        ch, hf = h // 2, h % 2
                        Sx = Sa if hf == 0 else Sb
                        kb.mm(Sx.t[:, (sl % 2) * 128:(sl % 2 + 1) * 128], kdx.t[hf * 64:(hf + 1) * 64, ch, e_ * 128:(e_ + 1) * 128],
                              qd.t[hf * 64:(hf + 1) * 64, ch, qtk], True, True, [kdx, qd], Sx, inc=(sl % 2 == 1))

                def exp():
                    if bi is None:
                        kb.op("act", lambda e: e.activation(out=P.t[:, 0:256], in_=Sa.t[:, 0:256], func=AF.Exp, scale=0.125),
                              reads=[Sa], writes=[(P, True)])
                        kb.op("act", lambda e: e.activation(out=P.t[:, 256:512], in_=Sb.t[:, 0:256], func=AF.Exp, scale=0.125),
                              reads=[Sb], writes=[(P, True)])
                    else:
                        dbb, tsb = db[cnt["db"] % 2], tS[cnt["db"] % 2]
                        cnt["db"] += 1
                        kb.dma("sp", dbb.t[:], dbias[bi[0], bi[1], hg], dbb, writes=[dbb])
                        kb.op("dve", lambda e: e.scalar_tensor_tensor(
                            out=tsb.t[:, 0:256], in0=Sa.t[:, 0:256], scalar=0.125, in1=dbb.t[:, 0:256], op0=ALU.mult, op1=ALU.add),
                            reads=[Sa, dbb], writes=[(tsb, True)])
                        kb.op("dve", lambda e: e.scalar_tensor_tensor(
                            out=tsb.t[:, 256:512], in0=Sb.t[:, 0:256], scalar=0.125, in1=dbb.t[:, 256:512], op0=ALU.mult, op1=ALU.add),
                            reads=[Sb, dbb], writes=[(tsb, True)])
                        kb.op("act", lambda e: e.activation(out=P.t[:], in_=tsb.t[:], func=AF.Exp), reads=[tsb], writes=[P])

                def pv():
                    for sl, i in enumerate(ORDER):
                        h = hg * 4 + i
                        kb.mm(OD.t[0:65, sl * 128:(sl + 1) * 128], vdx.t[:, e_ * 8 + h, :], P.t[:, sl * 128:(sl + 1) * 128],
                              ki == 0 and sl == 0, ki == nk - 1, [vdx, P], OD, inc=(sl == 3))

                st = {"scores": scores, "exp": exp, "pv": pv}
                if ki == nk - 1:
                    st["fin"] = lambda: finalize_cd(OD, None, catD[hg * 4:(hg + 1) * 4, :, qtk], catDb)
                return st

            steps = []
            ng = 0
            for T in range(ctiles if "C" in do else 0):
                keys = [(0, None), (1, None)]
                if T < 16:
                    keys += [(4 + T - 1, 0 if T == 0 else 1), (4 + T, None), (4 + T + 1, 3 if T == 15 else 2)]
                for kv in range(2):
                    OC = ps[4 + ng % 2]
                    ng += 1
                    for ki, (e_, mk) in enumerate(keys):
                        steps.append(c_step(T, kv, ki, len(keys), e_, mk, OC, len(steps)))
            for T in range(18 if "D" in do else 0):
                keys = [(0, None), (1, None)]
                if T < 16:
                    cls = d_class(T)
                    keys += [(4 + T + o, (cls, o + 3)) for o in D_OFFS[cls]]
                for hg in range(2):
                    OD = ps[4 + ng % 2]
                    ng += 1
                    for ki, (e_, bi) in enumerate(keys):
                        steps.append(d_step(T, hg, ki, len(keys), e_, bi, OD, len(steps)))
            if steps:
                run_pipeline(steps)
            kb.barrier()

        if "O" in do:
          with ExitStack() as pes:
            wab = kb.sbuf("wab", [128, 8, D], BF16, pes)
            wcd = kb.sbuf("wcd", [64, 16, D], BF16, pes)
            gb2 = kb.sbuf("gb2", [128, 2, D], F32, pes)
            lbc = kb.sbuf("lbc", [128, 2, D], F32, pes)
            xg = kb.sbuf("xg", [128, 4, D], F32, pes)
            xgb = [Buf("xg%d" % i) for i in range(4)]
            cA = kb.sbuf("cA", [128, 4, 512], BF16, pes)
            cB = kb.sbuf("cB", [128, 4, 512], BF16, pes)
            cC = kb.sbuf("cC", [64, 8, 512], BF16, pes)
            cD = kb.sbuf("cD", [64, 8, 512], BF16, pes)
            tmp = [kb.sbuf("tmp%d" % i, [128, 512], F32, pes) for i in range(2)]
            stats = [kb.sbuf("stats%d" % i, [128, 4, 6], F32, pes) for i in range(2)]
            mv = [kb.sbuf("mv%d" % i, [128, 2], F32, pes) for i in range(2)]
            rs = [kb.sbuf("rs%d" % i, [128, 2], F32, pes) for i in range(2)]
            for c4 in range(2):
                kb.dma("pool", wab.t[:, c4 * 4:(c4 + 1) * 4, :], w_ab[c4 * 512:(c4 + 1) * 512, :].rearrange("(c p) n -> p c n", p=128),
                       wab, writes=[(wab, True)])
            for c4 in range(4):
                kb.dma("pool", wcd.t[:, c4 * 4:(c4 + 1) * 4, :], w_cd[c4 * 256:(c4 + 1) * 256, :].rearrange("(h p) n -> p h n", p=64),
                       wcd, writes=[(wcd, True)])
            for i in range(2):
                kb.dma("sp", gb2.t[:, i, :], gate[i:i + 1, :].partition_broadcast(128), gb2, writes=[(gb2, True)])
                kb.dma("sp", lbc.t[:, i, :], lnp[i:i + 1, :].partition_broadcast(128), lbc, writes=[(lbc, True)])
            ny = 0
            for (t0, ntl) in GROUPS:
                ntok = ntl * 128
                s = 1 if t0 >= 16 else 0
                tk = slice(t0 * 128, t0 * 128 + ntok)
                kb.dma("sp", xg.t[:, 0:ntl, :], xin[tk, :].rearrange("(t p) c -> p t c", p=128), xg, writes=xgb[:ntl])
                kb.dma("sp", cA.t[:, :, :ntok], catA[:, :, tk].rearrange("c p t -> p c t"), cA, reads=[catAb], writes=[cA])
                kb.dma("sp", cB.t[:, :, :ntok], fm[FM_YB:FM_YB + 4, :, tk].rearrange("c p t -> p c t"), cB, writes=[cB])
                kb.dma("sp", cC.t[:, :, :ntok], catC[:, :, tk].rearrange("h p t -> p h t"), cC, reads=[catCb], writes=[cC])
                kb.dma("sp", cD.t[:, :, :ntok], catD[:, :, tk].rearrange("h p t -> p h t"), cD, reads=[catDb], writes=[cD])
                for ti in range(ntl):
                    tt = slice(ti * 128, (ti + 1) * 128)
                    for n in range(4):
                        ncs = slice(n * 512, (n + 1) * 512)
                        Y = ps[ny % 4]
                        tm_ = tmp[ny % 2]
                        ny += 1
                        pairs = [(cA.t[:, c, tt], wab.t[:, c, ncs]) for c in range(4)]
                        pairs += [(cB.t[:, c, tt], wab.t[:, 4 + c, ncs]) for c in range(4)]
                        pairs += [(cC.t[0:64, h, tt], wcd.t[0:64, h, ncs]) for h in range(8)]
                        pairs += [(cD.t[0:64, h, tt], wcd.t[0:64, 8 + h, ncs]) for h in range(8)]
                        kb.mm_group(Y.t[:], pairs, reads=[cA, cB, cC, cD, wab, wcd], out_buf=Y)
                        kb.op("dve", lambda e, Y=Y, tm_=tm_, ncs=ncs: e.tensor_tensor(out=tm_.t[:], in0=Y.t[:], in1=gb2.t[:, s, ncs], op=ALU.mult),
                              reads=[Y, gb2], writes=[tm_])
                        kb.op("dve", lambda e, tm_=tm_, ti=ti, ncs=ncs: e.scalar_tensor_tensor(
                            out=xg.t[:, ti, ncs], in0=xg.t[:, ti, ncs], scalar=ALPHA, in1=tm_.t[:], op0=ALU.mult, op1=ALU.add),
                            reads=[tm_, xgb[ti]], writes=[(xgb[ti], True)])
                    ln_epilogue(kb, xg.t[:, ti, :], xgb[ti], stats[ti % 2], mv[ti % 2], rs[ti % 2], lbc.t[:, 0, :], lbc.t[:, 1, :], lbc)
                    kb.dma("sp", xout[(t0 + ti) * 128:(t0 + ti + 1) * 128, :], xg.t[:, ti, :], xg, reads=[xgb[ti]], final=True)
            kb.barrier()
        kb.finish()
    return nc


MCOLS = 2304


def build_mod():
    nc = bass.Bass("TRN2", target_bir_lowering=False)
    cc = nc.dram_tensor("cc", [128, 32], F32, kind="ExternalInput").ap()
    wm = nc.dram_tensor("wm", [4, D, MCOLS], F32, kind="ExternalInput").ap()
    bm = nc.dram_tensor("bm", [1, 4 * MCOLS], F32, kind="ExternalInput").ap()
    mo = nc.dram_tensor("mo", [2, 4 * MCOLS], F32, kind="ExternalOutput").ap()
    with ExitStack() as es:
        kb = KB(nc, es)
        c_sb = kb.sbuf("c_sb", [128, 16, 2], F32)
        bb = kb.sbuf("bb", [2, 4 * MCOLS], F32)
        ob = kb.sbuf("ob", [2, 4 * MCOLS], F32)
        wbuf = [kb.sbuf("wbuf%d" % i, [128, 16, 512], F32) for i in range(2)]
        pm = [kb.psum("pm%d" % i, [128, 512]) for i in range(2)]
        kb.dma("sp", c_sb.t[:].rearrange("p a b -> p (a b)"), cc, c_sb, writes=[c_sb])
        kb.dma("sp", bb.t[:], bm[0:1, :].partition_broadcast(2), bb, writes=[bb])
        kb.op("act", lambda e: e.activation(out=c_sb.t[:], in_=c_sb.t[:], func=AF.Silu), reads=[c_sb], writes=[c_sb])
        n = 0
        for l in range(4):
            for c0 in range(0, MCOLS, 512):
                ncol = min(512, MCOLS - c0)
                wb_, pb_ = wbuf[n % 2], pm[n % 2]
                n += 1
                for k4 in range(4):
                    kb.dma("sp", wb_.t[:, k4 * 4:(k4 + 1) * 4, :ncol],
                           wm[l, k4 * 512:(k4 + 1) * 512, c0:c0 + ncol].rearrange("(kc p) c -> p kc c", p=128),
                           wb_, writes=[(wb_, True)])
                kb.mm_group(pb_.t[0:2, :ncol], [(c_sb.t[:, kc, :], wb_.t[:, kc, :ncol]) for kc in range(16)],
                            reads=[c_sb, wb_], out_buf=pb_)
                o0 = l * MCOLS + c0
                kb.op("dve", lambda e, pb_=pb_, o0=o0, ncol=ncol: e.tensor_tensor(
                    out=ob.t[:, o0:o0 + ncol], in0=pb_.t[0:2, :ncol], in1=bb.t[:, o0:o0 + ncol], op=ALU.add),
                    reads=[pb_, bb], writes=[(ob, True)])
        kb.dma("sp", mo, ob.t[:], ob, reads=[ob], final=True)
        kb.finish()
    return nc


NCORES = 8
_PROGS = {}


def _prog(name):
    if name not in _PROGS:
        _PROGS[name] = {"M": build_mod, "F": build_ffn, "FF": lambda: build_ffn(nsub=2), "J": build_proj, "T": build_attn}[name]()
    return _PROGS[name]


def _run(name, in_maps):
    import time, os, sys
    t0 = time.time()
    res = run_bass_kernel_spmd(_prog(name), in_maps, core_ids=list(range(NCORES)))
    if os.environ.get("K_TIMING"):
        nb = sum(v.nbytes for m in in_maps for v in m.values())
        print("[launch %s] %.1fs  in=%.0fMB" % (name, time.time() - t0, nb / 1e6), file=sys.stderr, flush=True)
    return res.results


def d_bias_tables(rpb, core):
    out = np.full((5, 7, 2, 128, 512), -30000.0, np.float32)
    ii = np.arange(128)
    for cls, tloc in enumerate((0, 1, 2, 14, 15)):
        tg = core * 16 + tloc
        rq = (2 * tg + ii // 64)[None, :]
        cq = (ii % 64)[None, :]
        rs = np.clip(rq - 4, 0, 256 - 8)
        cs = np.clip(cq - 8, 0, 64 - 16)
        for o in D_OFFS[cls]:
            kg = tg + o
            if kg < 0 or kg > 127:
                continue
            rk = (2 * kg + ii // 64)[:, None]
            ck = (ii % 64)[:, None]
            valid = (rk >= rs) & (rk < rs + 8) & (ck >= cs) & (ck < cs + 16)
            dr = np.clip(rk - rq + 7, 0, 14)
            dc = np.clip(ck - cq, -15, 15) + 15
            for hg in range(2):
                for sl, i in enumerate(ORDER):
                    g = rpb[hg * 4 + i][dr, dc]
                    out[cls, o + 3, hg, :, sl * 128:(sl + 1) * 128] = np.where(valid, g, np.float32(-30000.0))
    return out


def c_masks(core):
    i = np.arange(128)[:, None]
    j = np.arange(128)[None, :]
    mprev = np.tile((i >= j).astype(np.float32), (1, 4))
    mnext = np.tile((i <= j).astype(np.float32), (1, 4))
    z = np.zeros_like(mprev)
    return np.stack([z if core == 0 else mprev, mprev, mnext, z if core == NCORES - 1 else mnext])


def _ext_fm(fms, i, c0, nch):
    own = fms[i][c0:c0 + nch]
    z = np.zeros((nch, 128, 256), own.dtype)
    prev = fms[i - 1][c0:c0 + nch][:, :, 1792:2048] if i > 0 else z
    nxt = fms[i + 1][c0:c0 + nch][:, :, 0:256] if i < NCORES - 1 else z
    return np.ascontiguousarray(np.concatenate([own[:, :, 2048:2304], prev, own[:, :, :2048], nxt], axis=2))


def _ext_tm(tms, i, c0, ncol):
    own = tms[i][:, c0:c0 + ncol]
    z = np.zeros((256, ncol), own.dtype)
    prev = tms[i - 1][1792:2048, c0:c0 + ncol] if i > 0 else z
    nxt = tms[i + 1][0:256, c0:c0 + ncol] if i < NCORES - 1 else z
    return np.ascontiguousarray(np.concatenate([own[2048:2304], prev, own[:2048], nxt], axis=0))


def kernel(x, c, ctx, c_ctx, w_mod, b_mod, ln_g, ln_b, ffn1_w_in, ffn1_w_out, ffn2_w_in, ffn2_w_out,
           mix_w_in, mix_w_out, a_lambda, b_norm_g, b_norm_b, b_spatial_w, b_spatial_b, c_sink, d_rpb,
           _nlayers=4, _dump=None):
    f32 = np.float32
    x = np.asarray(x, f32)[0]
    ctx = np.asarray(ctx, f32)[0]
    ident = np.eye(128, dtype=f32)
    cc = np.stack([np.asarray(c, f32)[0].reshape(16, 128).T, np.asarray(c_ctx, f32).reshape(16, 128).T], axis=2)
    cc = np.ascontiguousarray(cc.reshape(128, 32))
    w_mod = np.asarray(w_mod, f32)
    b_mod = np.asarray(b_mod, f32)
    ins = [{"cc": cc, "wm": np.ascontiguousarray(w_mod[:, :, i * MCOLS:(i + 1) * MCOLS]),
            "bm": np.ascontiguousarray(b_mod[:, i * MCOLS:(i + 1) * MCOLS]).reshape(1, 4 * MCOLS)} for i in range(NCORES)]
    mo = _run("M", ins)
    mods = np.concatenate([r["mo"].reshape(2, 4, MCOLS) for r in mo], axis=2)
    xl = [np.ascontiguousarray(x[i * 2048:(i + 1) * 2048]) for i in range(NCORES)]
    xc = np.ascontiguousarray(ctx)
    ropes = [rope_tables(i) for i in range(NCORES)]
    cms = [c_masks(i) for i in range(NCORES)]

    def full(i):
        return np.ascontiguousarray(np.concatenate([xl[i], xc], 0))

    def ffn(specs):
        nonlocal xl, xc
        common = {"ident": ident}
        for k, (l, idx, lnk, w_in, w_out) in enumerate(specs):
            sfx = "" if k == 0 else str(k)
            common["modc" + sfx] = mod_cols(mods[0, l], mods[1, l], idx)
            common["lnp" + sfx] = np.ascontiguousarray(np.stack([ln_g[l, lnk], ln_b[l, lnk]]).astype(f32))
            common["w_in" + sfx] = np.ascontiguousarray(w_in, f32)
            common["w_out" + sfx] = np.ascontiguousarray(w_out, f32)
        r = _run("F" if len(specs) == 1 else "FF",
                 [dict(common, xin=np.ascontiguousarray(np.concatenate([xl[i], xc[i * NCTX_F:(i + 1) * NCTX_F]], 0)))
                  for i in range(NCORES)])
        xl = [q["xout"][:2048] for q in r]
        xc = np.ascontiguousarray(np.concatenate([q["xout"][2048:2048 + NCTX_F] for q in r], 0))

    if _dump is not None:
        _dump["mods"] = mods
    for l in range(_nlayers):
        if l == 0:
            ffn([(0, (0, 1, 2), 0, ffn1_w_in[0], ffn1_w_out[0])])
        xs = [full(i) for i in range(NCORES)]
        if _dump is not None:
            _dump["x1_%d" % l] = xs
        w_fm, w_tm = build_proj_weights(np.asarray(mix_w_in[l], f32))
        modc = mod_cols(mods[0, l], mods[1, l], (3, 4, None))
        gnp = np.ascontiguousarray(np.stack([b_norm_g[l], b_norm_b[l]]).astype(f32))
        wsT = np.ascontiguousarray(np.transpose(np.asarray(b_spatial_w[l], f32), (2, 0, 1)).reshape(128, 512))
        bsr = np.ascontiguousarray(np.asarray(b_spatial_b[l], f32).reshape(1, 512))
        r = _run("J", [{"xin": xs[i], "modc": modc, "ident": ident, "w_fm": w_fm, "w_tm": w_tm, "rope": ropes[i],
                        "gnp": gnp, "wsT": wsT, "bsr": bsr} for i in range(NCORES)])
        fms = [q["fm"] for q in r]
        tms = [q["tm"] for q in r]
        ka = np.ascontiguousarray(np.concatenate([fms[0][FM_KA:FM_KA + 4][:, :, 2048:2304]] +
                                                 [fms[i][FM_KA:FM_KA + 4][:, :, :2048] for i in range(NCORES)], axis=2))
        va = np.ascontiguousarray(np.concatenate([tms[0][2048:2304, 0:512]] + [tms[i][:2048, 0:512] for i in range(NCORES)], axis=0))
        lam_init = 0.8 - 0.6 * np.exp(-0.3 * l)
        alam = np.concatenate([np.asarray(a_lambda[l], f32).reshape(256), np.array([lam_init, 1.0 - lam_init], f32)]).reshape(1, 258)
        alam = np.ascontiguousarray(alam.astype(f32))
        sink = np.ascontiguousarray(np.asarray(c_sink[l], f32).reshape(1, 8))
        m9 = mods[:, l].reshape(2, 9, D)
        gate = np.ascontiguousarray(m9[:, 5, :])
        lnp = np.ascontiguousarray(np.stack([ln_g[l, 1], ln_b[l, 1]]).astype(f32))
        wo = np.asarray(mix_w_out[l], f32)
        w_ab, w_cd = np.ascontiguousarray(wo[:1024]), np.ascontiguousarray(wo[1024:])
        rpb = np.asarray(d_rpb[l], f32)
        ins = []
        for i in range(NCORES):
            ins.append({"xin": xs[i], "fm": fms[i], "ka": ka, "va": va,
                        "kc": _ext_fm(fms, i, FM_KC, 2), "vc": _ext_tm(tms, i, 512, 128),
                        "kd": _ext_fm(fms, i, FM_KD, 4), "vd": _ext_tm(tms, i, 640, 512),
                        "cmask": cms[i], "dbias": d_bias_tables(rpb, i), "alam": alam, "sink": sink, "ident": ident,
                        "gate": gate, "lnp": lnp, "w_ab": w_ab, "w_cd": w_cd})
        r = _run("T", ins)
        xs = [q["xout"] for q in r]
        if _dump is not None:
            _dump["x2_%d" % l] = xs
            _dump["fm_%d" % l] = fms
        xl = [q[:2048] for q in xs]
        xc = np.ascontiguousarray(xs[0][2048:2304])
        spec = [(l, (6, 7, 8), 2, ffn2_w_in[l], ffn2_w_out[l])]
        if l + 1 < _nlayers:
            spec.append((l + 1, (0, 1, 2), 0, ffn1_w_in[l + 1], ffn1_w_out[l + 1]))
        ffn(spec)
    out = np.concatenate(xl, axis=0)
    return np.ascontiguousarray(out[None].astype(f32))
```
